# Optimizing a Trainium2 kernel written in Bass

```python
import jax, jax.numpy as jnp
from jax import lax
import numpy as np

D_MODEL = 1024
BATCH = 2
SEQ = 16384
DEPTH = 1
DEC_BATCH = 16
DEC_SEQ = 4096
PAST_LEN = 128

N_HEADS = 8
Q_RANK = 256
KV_RANK = 128
QK_NOPE = 64
QK_ROPE = 32
V_DIM = 64
ATTN_WIDTH = N_HEADS * V_DIM
ROPE_THETA = 10000.0
Q_BLOCK = 128
F_GROUPS = 4
F_GROUP_DIM = 128
F_WIDTH = F_GROUPS * F_GROUP_DIM
IN_COLS = Q_RANK + KV_RANK + QK_ROPE + F_WIDTH + 2 * D_MODEL
D_FF = 2816
CONV_W = 3
EPS = 1e-6

kernel_name = "hybrid_mla_fnet_convffn_encoder"


def _rms(x, g):
    xf = x.astype(jnp.float32)
    r = lax.rsqrt(jnp.mean(xf * xf, axis=-1, keepdims=True) + EPS)
    return (xf * r).astype(x.dtype) * g


def _rope(x, cos, sin):
    half = x.shape[-1] // 2
    xf = x.astype(jnp.float32)
    x1, x2 = xf[..., :half], xf[..., half:]
    out = jnp.concatenate([x1 * cos - x2 * sin, x2 * cos + x1 * sin], axis=-1)
    return out.astype(x.dtype)


def _mla_attention(q_nope, q_rope, k_nope, k_rope, v):
    B, S, H, _ = q_nope.shape
    nb = S // Q_BLOCK
    scale = (QK_NOPE + QK_ROPE) ** -0.5
    qn = q_nope.reshape(B, nb, Q_BLOCK, H, QK_NOPE).transpose(1, 0, 2, 3, 4)
    qr = q_rope.reshape(B, nb, Q_BLOCK, H, QK_ROPE).transpose(1, 0, 2, 3, 4)

    def one_block(args):
        qn_b, qr_b = args
        s = (jnp.einsum('bqhd,bkhd->bhqk', qn_b, k_nope)
             + jnp.einsum('bqhr,bkr->bhqk', qr_b, k_rope))
        p = jax.nn.softmax(s.astype(jnp.float32) * scale, axis=-1)
        return jnp.einsum('bhqk,bkhd->bqhd', p.astype(v.dtype), v)

    o = lax.map(one_block, (qn, qr))
    return o.transpose(1, 0, 2, 3, 4).reshape(B, S, H * V_DIM)


def _fourier(xf):
    B, S, _ = xf.shape
    g = xf.reshape(B, S, F_GROUPS, F_GROUP_DIM).astype(jnp.float32)
    f = jnp.fft.fft2(g, axes=(1, 3), norm='ortho').real
    return f.reshape(B, S, F_WIDTH).astype(xf.dtype)


def _dwconv3(u, w, b):
    up = jnp.pad(u, ((0, 0), (1, 1), (0, 0)))
    return up[:, :-2] * w[0] + up[:, 1:-1] * w[1] + up[:, 2:] * w[2] + b


def _layer(x, g_mix, w_in, g_q, w_uq, g_kv, w_ukv, w_attn_out, w_fourier_out,
           w_out, g_ffn, w_up, conv_w, conv_b, w_down):
    B, S, _ = x.shape
    h = _rms(x, g_mix)
    proj = h @ w_in
    c_q, c_kv, k_rope, xf, gates = jnp.split(
        proj, [Q_RANK, Q_RANK + KV_RANK, Q_RANK + KV_RANK + QK_ROPE,
               Q_RANK + KV_RANK + QK_ROPE + F_WIDTH], axis=-1)

    q = (_rms(c_q, g_q) @ w_uq).reshape(B, S, N_HEADS, QK_NOPE + QK_ROPE)
    q_nope, q_rope = q[..., :QK_NOPE], q[..., QK_NOPE:]
    kv = (_rms(c_kv, g_kv) @ w_ukv).reshape(B, S, N_HEADS, QK_NOPE + V_DIM)
    k_nope, v = kv[..., :QK_NOPE], kv[..., QK_NOPE:]
    pos = jnp.arange(S, dtype=jnp.float32)
    inv_freq = ROPE_THETA ** (-jnp.arange(0, QK_ROPE, 2, dtype=jnp.float32) / QK_ROPE)
    ang = pos[:, None] * inv_freq[None, :]
    cos, sin = jnp.cos(ang), jnp.sin(ang)
    q_rope = _rope(q_rope, cos[:, None, :], sin[:, None, :])
    k_rope = _rope(k_rope, cos, sin)
    a_branch = _mla_attention(q_nope, q_rope, k_nope, k_rope, v) @ w_attn_out

    f_branch = _fourier(xf) @ w_fourier_out

    g = jax.nn.sigmoid(gates)
    g_a, g_f = g[..., :D_MODEL], g[..., D_MODEL:]
    x = x + (g_a * a_branch + g_f * f_branch) @ w_out

    h2 = _rms(x, g_ffn)
    u = _dwconv3(h2 @ w_up, conv_w, conv_b)
    x = x + (jax.nn.silu(u[..., :D_FF]) * u[..., D_FF:]) @ w_down
    return x


def _trunk(x, g_mix, w_in, g_q, w_uq, g_kv, w_ukv, w_attn_out, w_fourier_out,
           w_out, g_ffn, w_up, conv_w, conv_b, w_down, g_final):
    for l in range(DEPTH):
        x = _layer(x, g_mix[l], w_in[l], g_q[l], w_uq[l], g_kv[l], w_ukv[l],
                   w_attn_out[l], w_fourier_out[l], w_out[l], g_ffn[l], w_up[l],
                   conv_w[l], conv_b[l], w_down[l])
    return _rms(x, g_final)


def setup_inputs(seed: int = 0) -> dict:
    key = jax.random.key(seed)
    ks = jax.random.split(key, 20)
    f32 = jnp.float32

    def w(k, shape, fan_in):
        return jax.random.normal(k, shape, f32) * (fan_in ** -0.5)

    def gain(k, shape):
        return 1.0 + 0.02 * jax.random.normal(k, shape, f32)

    L = DEPTH
    return {
        "x_prompt": jax.random.normal(ks[0], (BATCH, SEQ, D_MODEL), f32),
        "x_sample": jax.random.normal(ks[1], (DEC_BATCH, DEC_SEQ, D_MODEL), f32),
        "g_mix": gain(ks[2], (L, D_MODEL)),
        "w_in": w(ks[3], (L, D_MODEL, IN_COLS), D_MODEL),
        "g_q": gain(ks[4], (L, Q_RANK)),
        "w_uq": w(ks[5], (L, Q_RANK, N_HEADS * (QK_NOPE + QK_ROPE)), Q_RANK),
        "g_kv": gain(ks[6], (L, KV_RANK)),
        "w_ukv": w(ks[7], (L, KV_RANK, N_HEADS * (QK_NOPE + V_DIM)), KV_RANK),
        "w_attn_out": w(ks[8], (L, ATTN_WIDTH, D_MODEL), ATTN_WIDTH),
        "w_fourier_out": w(ks[9], (L, F_WIDTH, D_MODEL), F_WIDTH),
        "w_out": w(ks[10], (L, D_MODEL, D_MODEL), D_MODEL),
        "g_ffn": gain(ks[11], (L, D_MODEL)),
        "w_up": w(ks[12], (L, D_MODEL, 2 * D_FF), D_MODEL),
        "conv_w": w(ks[13], (L, CONV_W, 2 * D_FF), CONV_W),
        "conv_b": 0.02 * jax.random.normal(ks[14], (L, 2 * D_FF), f32),
        "w_down": w(ks[15], (L, D_FF, D_MODEL), D_FF),
        "g_final": gain(ks[16], (D_MODEL,)),
    }


def reference(x_prompt, x_sample, g_mix, w_in, g_q, w_uq, g_kv, w_ukv, w_attn_out,
              w_fourier_out, w_out, g_ffn, w_up, conv_w, conv_b, w_down, g_final):
    y_prompt = _trunk(x_prompt, g_mix, w_in, g_q, w_uq, g_kv, w_ukv, w_attn_out,
                      w_fourier_out, w_out, g_ffn, w_up, conv_w, conv_b, w_down, g_final)
    y_sample = _trunk(x_sample, g_mix, w_in, g_q, w_uq, g_kv, w_ukv, w_attn_out,
                      w_fourier_out, w_out, g_ffn, w_up, conv_w, conv_b, w_down, g_final)
    return (y_prompt, y_sample)
```

```python
import numpy as np
import ml_dtypes
from contextlib import ExitStack
import concourse.bass as bass
import concourse.mybir as mybir
from concourse.bass_utils import run_bass_kernel_spmd

F32 = mybir.dt.float32
BF16 = mybir.dt.bfloat16
AF = mybir.ActivationFunctionType
ALU = mybir.AluOpType

D = 1024
H = 8
QR = 256
KVR = 128
NOPE = 64
ROPE = 32
VD = 64
QK = NOPE + ROPE
FG = 4
DFF = 2816
NFC = 2 * DFF // 128
NHC = DFF // 128
EPS = 1e-6
IN_COLS = 2976
C_CQ, C_CKV, C_KR, C_KRR, C_XF, C_G, WI_COLS = 0, 256, 384, 416, 448, 960, 3008
FB = 510


from functools import partial

TRAMP = [
    lambda f: f(),
    lambda f: f(),
    lambda f: f(),
    lambda f: f(),
    lambda f: f(),
    lambda f: f(),
    lambda f: f(),
    lambda f: f(),
    lambda f: f(),
    lambda f: f(),
    lambda f: f(),
    lambda f: f(),
    lambda f: f(),
    lambda f: f(),
    lambda f: f(),
    lambda f: f(),
    lambda f: f(),
    lambda f: f(),
    lambda f: f(),
    lambda f: f(),
]


class Res:
    __slots__ = ("w", "r")

    def __init__(self):
        self.w = {}
        self.r = {}


class Eng:
    def __init__(self, nc, es, name, eng, sem=True):
        self.e = eng
        self.key = name
        self.sem = es.enter_context(nc.semaphore(name)) if sem else None
        self.cnt = 0
        self.seen = {}
        self.dsem = []
        self.di = 0


class DSem:
    def __init__(self, nc, es, name):
        self.key = name
        self.sem = es.enter_context(nc.semaphore(name))
        self.cnt = 0


class KB:
    def __init__(self, nc, es, ndma=20):
        self.nc = nc
        self.phase = 0
        self.PE = Eng(nc, es, "sPE", nc.tensor)
        self.ACT = Eng(nc, es, "sACT", nc.scalar)
        self.DVE = Eng(nc, es, "sDVE", nc.vector)
        self.POOL = Eng(nc, es, "sPOOL", nc.gpsimd)
        self.SP = Eng(nc, es, "sSP", nc.sync, sem=False)
        self.engs = [self.PE, self.ACT, self.DVE, self.POOL, self.SP]
        for q in (self.SP, self.POOL):
            q.dsem = [DSem(nc, es, f"d{q.key}{i}") for i in range(ndma)]

    def _wait(self, E, deps):
        for key, (sem, val) in deps.items():
            if key == E.key and val > E.cnt:
                continue
            if E.seen.get(key, 0) < val:
                E.e.wait_ge(sem, val)
                E.seen[key] = val

    @staticmethod
    def _gather(reads, writes, accum):
        deps = {}

        def add(d):
            for k, ev in d.items():
                if k not in deps or deps[k][1] < ev[1]:
                    deps[k] = ev
        for r in reads:
            add(r.w)
        for w in writes:
            if not accum:
                add(w.w)
            add(w.r)
        return deps

    def op(self, E, fn, reads=(), writes=(), signal=True):
        deps = self._gather(reads, writes, False)
        self._wait(E, deps)
        ins = TRAMP[self.phase](fn)
        val = E.cnt + 1
        if signal:
            ins.then_inc(E.sem, 1)
            E.cnt = val
        ev = (E.sem, val)
        for r in reads:
            if r.r.get(E.key, (None, 0))[1] < val:
                r.r[E.key] = ev
        for w in writes:
            w.w = {E.key: ev}
            w.r = {}
        return ins

    def dma(self, Q, out, in_, reads=(), writes=(), accum=False, slow=False):
        slot = Q.dsem[Q.di % len(Q.dsem)]
        Q.di += 1
        deps = self._gather(reads, writes, accum)
        if slot.cnt > 0:
            deps[slot.key] = (slot.sem, slot.cnt)
        self._wait(Q, deps)
        ins = TRAMP[self.phase](partial(Q.e.dma_start, out=out, in_=in_, allow_slow_non_contiguous=True) if slow
                                else partial(Q.e.dma_start, out=out, in_=in_))
        slot.cnt += 16
        ins.then_inc(slot.sem, 16)
        ev = (slot.sem, slot.cnt)
        for r in reads:
            r.r[slot.key] = ev
        for w in writes:
            if accum:
                w.w[slot.key] = ev
            else:
                w.w = {slot.key: ev}
                w.r = {}
        return ins

    def barrier(self):
        tg = {}
        for E in self.engs:
            if E.sem is not None and E.cnt > 0:
                tg[E.key] = (E.sem, E.cnt)
            for s in E.dsem:
                if s.cnt > 0:
                    tg[s.key] = (s.sem, s.cnt)
        for E in self.engs:
            self._wait(E, tg)

    def mm(self, out, lhsT, rhs, start, stop, reads=(), writes=(), signal=None):
        if signal is None:
            signal = stop
        return self.op(self.PE, partial(self.nc.tensor.matmul, out, lhsT=lhsT, rhs=rhs, start=start, stop=stop),
                       reads, writes, signal)

    def tr(self, out, in_, ident, reads=(), writes=(), signal=True):
        return self.op(self.PE, partial(self.nc.tensor.transpose, out, in_, ident), reads, writes, signal)

    def act(self, out, in_, func, reads=(), writes=(), **kw):
        return self.op(self.ACT, partial(self.nc.scalar.activation, out=out, in_=in_, func=func, **kw), reads, writes)

    def tt(self, E, out, in0, in1, op, reads=(), writes=()):
        return self.op(E, partial(E.e.tensor_tensor, out=out, in0=in0, in1=in1, op=op), reads, writes)

    def ts(self, E, out, in0, s1, s2, op0, op1=None, reads=(), writes=()):
        if op1 is None:
            return self.op(E, partial(E.e.tensor_scalar, out=out, in0=in0, scalar1=s1, scalar2=None, op0=op0),
                           reads, writes)
        return self.op(E, partial(E.e.tensor_scalar, out=out, in0=in0, scalar1=s1, scalar2=s2, op0=op0, op1=op1),
                       reads, writes)

    def stt(self, E, out, in0, scalar, in1, op0, op1, reads=(), writes=()):
        return self.op(E, partial(E.e.scalar_tensor_tensor, out=out, in0=in0, scalar=scalar, in1=in1, op0=op0, op1=op1),
                       reads, writes)

    def cp(self, E, out, in_, reads=(), writes=()):
        if E is self.ACT:
            return self.op(E, partial(E.e.activation, out=out, in_=in_, func=AF.Copy), reads, writes)
        return self.op(E, partial(E.e.tensor_copy, out=out, in_=in_), reads, writes)

    def memset(self, E, ap, v, writes=()):
        return self.op(E, partial(E.e.memset, ap, v), (), writes)

    def recip(self, out, in_, reads=(), writes=()):
        return self.op(self.DVE, partial(self.nc.vector.reciprocal, out=out, in_=in_), reads, writes)


class T:
    def __init__(self, t):
        self.t = t
        self.r = Res()


def build(cfg):
    SP_, NP_, SS_, NS_, NSEQ = cfg["SP"], cfg["NP"], cfg["SS"], cfg["NS"], cfg["NSEQ"]
    OWNP = SP_ // 4
    debug = cfg.get("debug", False)
    phases = cfg.get("phases", "AFQTMN")
    nc = bass.Bass("TRN2", target_bir_lowering=False)

    def din(name, shape, dt=F32):
        return nc.dram_tensor(name, list(shape), dt, kind="ExternalInput").ap()

    def dscr(name, shape, dt=BF16):
        return nc.dram_tensor(name, list(shape), dt, kind=("ExternalOutput" if debug else "Internal")).ap()

    xp_full = din("xp_full", [SP_, D])
    xp_own = din("xp_own", [OWNP + 2, D])
    xs = din("xs", [NSEQ, SS_, D])
    tabs = {}
    for tag, S, N, nq in (("p", SP_, NP_, OWNP + 2), ("s", SS_, NS_, SS_)):
        tabs[tag] = dict(
            kcs=din(f"kcs_{tag}", [64, S]), qcos=din(f"qcos_{tag}", [QK, nq]), qsin=din(f"qsin_{tag}", [QK, nq]),
            ecs=din(f"ecs_{tag}", [N, 2, nq], BF16),
            ca=din(f"ca_{tag}", [N, 2 * N], BF16), cb=din(f"cb_{tag}", [N, 2 * N], BF16))
    cs128_d = din("cs128", [128, 256], BF16)
    ident_d = din("ident", [128, 128], BF16)
    eplace_d = din("eplace", [64, QK], BF16)
    mask_d = din("mask", [128, 2])
    w_in_d = din("w_in", [D, IN_COLS])
    w_uq_d = din("w_uq", [QR, H * QK])
    w_ukv_d = din("w_ukv", [KVR, H * 128])
    w_ao_d = din("w_ao", [512, D])
    w_fo_d = din("w_fo", [512, D])
    w_out_d = din("w_out", [D, D])
    w_up_d = din("w_up", [D, 2 * DFF])
    w_dn_d = din("w_dn", [DFF, D])
    g_mix_d = din("g_mix8", [128, 8])
    g_q_d = din("g_q2", [128, 2])
    g_kv_d = din("g_kv1", [128, 1])
    g_ffn_d = din("g_ffn8", [128, 8])
    convw_d = din("convw", [128, NFC, 3])
    convb_d = din("convb", [128, NFC])
    gfin_d = din("gfin", [128, D])

    yp = nc.dram_tensor("yp", [OWNP, D], F32, kind="ExternalOutput").ap()
    ys = nc.dram_tensor("ys", [NSEQ, SS_, D], F32, kind="ExternalOutput").ap()

    Wi_s = dscr("Wi_s", [8, 128, WI_COLS])
    Wuq_s = dscr("Wuq_s", [2, 128, 2 * H * QK])
    Wkv_s = dscr("Wkv_s", [128, H * QK + H * VD])
    Wao_s = dscr("Wao_s", [4, 128, D])
    Wfo_s = dscr("Wfo_s", [4, 128, D])
    Wo_s = dscr("Wo_s", [8, 128, D])
    Wup_s = dscr("Wup_s", [8, 128, 2 * DFF])
    Wdn_s = dscr("Wdn_s", [NHC, 128, D])
    SMAX = max(SP_, SS_)
    NQMAX = max(OWNP + 2, SS_ + 2)
    KT_s = dscr("KT_s", [H, QK, SMAX])
    V_s = dscr("V_s", [H, 128, SMAX // 128, VD + 1])
    Y_s = dscr("Y_s", [FG, 2, 128 * SMAX])
    FT_s = dscr("FT_s", [FG, 128, NQMAX])
    QT_s = dscr("QT_s", [H, QK, NQMAX])
    AT_s = dscr("AT_s", [H * VD, NQMAX])
    X1_s = dscr("X1_s", [NQMAX, D], F32)
    H2T_s = dscr("H2T_s", [8, 128, NQMAX])

    es_top = ExitStack()
    with es_top:
        kb = KB(nc, es_top)
        PE, ACT, DVE, POOL, SP = kb.PE, kb.ACT, kb.DVE, kb.POOL, kb.SP

        uid = [0]

        def sb(es, name, shape, dt):
            uid[0] += 1
            return T(es.enter_context(nc.sbuf_tensor(f"sb{uid[0]}_{name}", list(shape), dt)))

        def ps(es, name, shape, dt):
            uid[0] += 1
            return T(es.enter_context(nc.psum_tensor(f"ps{uid[0]}_{name}", list(shape), dt)))

        ident = sb(es_top, "ident", [128, 128], BF16)
        eplace = sb(es_top, "eplace", [64, QK], BF16)
        cs128 = sb(es_top, "cs128", [128, 256], BF16)
        onesf = sb(es_top, "onesf", [128, 128], F32)
        epsq = sb(es_top, "epsq", [128, 1], F32)
        mask = sb(es_top, "mask", [128, 2], F32)
        zeros = sb(es_top, "zeros", [128, 16], BF16)
        kb.dma(SP, ident.t[:], ident_d[:, :], writes=[ident.r])
        kb.dma(SP, eplace.t[:], eplace_d[:, :], writes=[eplace.r])
        kb.dma(SP, cs128.t[:], cs128_d[:, :], writes=[cs128.r])
        kb.dma(SP, mask.t[:], mask_d[:, :], writes=[mask.r])
        kb.memset(DVE, onesf.t[:], 1.0, writes=[onesf.r])
        kb.memset(DVE, epsq.t[:], EPS, writes=[epsq.r])
        kb.memset(DVE, zeros.t[:], 0.0, writes=[zeros.r])
        PSALL = es_top.enter_context(nc.psum_tensor("psall", [128, 8, 512], F32))
        pb = [T(PSALL[:, i, :]) for i in range(8)]
        pT = T(PSALL[:, 7, :].bitcast(BF16))
        pT.r = pb[7].r
        kb.barrier()

        if "W" in phases or True:
            with ExitStack() as es:
                stg = [sb(es, f"stg{i}", [128, 2 * DFF], F32) for i in range(2)]
                wo = [sb(es, f"wo{i}", [128, 2 * DFF], BF16) for i in range(2)]
                gm = sb(es, "gm", [128, 8], F32)
                gq = sb(es, "gq", [128, 2], F32)
                gkv = sb(es, "gkv", [128, 1], F32)
                gf = sb(es, "gf", [128, 8], F32)
                for t_, d_ in ((gm, g_mix_d), (gq, g_q_d), (gkv, g_kv_d), (gf, g_ffn_d)):
                    kb.dma(SP, t_.t[:], d_[:, :], writes=[t_.r])
                cnt = [0]

                def conv(src, ncols, dst, fn):
                    i = cnt[0] % 2
                    cnt[0] += 1
                    kb.dma(SP, stg[i].t[:, 0:ncols], src, writes=[stg[i].r])
                    fn(stg[i], wo[i])
                    kb.dma(POOL, dst, wo[i].t[:, 0:dst.shape[-1]], reads=[wo[i].r])

                def scaled(E, o, i_, sc, rd, wr, neg=False):
                    if sc is None:
                        kb.cp(E, o, i_, reads=rd, writes=wr)
                    elif neg:
                        kb.ts(E, o, i_, sc, -1.0, ALU.mult, ALU.mult, reads=rd, writes=wr)
                    else:
                        kb.ts(E, o, i_, sc, None, ALU.mult, reads=rd, writes=wr)

                for dc in range(8):
                    def f_in(s_, o_, dc=dc):
                        sc = gm.t[:, dc:dc + 1]
                        rd, wr = [s_.r, gm.r], [o_.r]
                        scaled(DVE, o_.t[:, 0:416], s_.t[:, 0:416], sc, rd, wr)
                        scaled(DVE, o_.t[:, C_KRR:C_KRR + 16], s_.t[:, 400:416], sc, rd, wr, neg=True)
                        scaled(DVE, o_.t[:, C_KRR + 16:C_KRR + 32], s_.t[:, 384:400], sc, rd, wr)
                        scaled(DVE, o_.t[:, C_XF:WI_COLS], s_.t[:, 416:IN_COLS], sc, rd, wr)
                    conv(w_in_d[dc * 128:(dc + 1) * 128, :], IN_COLS, Wi_s[dc], f_in)
                for kc in range(2):
                    def f_uq(s_, o_, kc=kc):
                        sc = gq.t[:, kc:kc + 1]
                        rd, wr = [s_.r, gq.r], [o_.r]
                        n = H * QK
                        scaled(DVE, o_.t[:, 0:n], s_.t[:, 0:n], sc, rd, wr)
                        kb.memset(DVE, o_.t[:, n:2 * n], 0.0, writes=wr)
                        s3 = s_.t[:, 0:n].rearrange("p (h c) -> p h c", c=QK)
                        o3 = o_.t[:, n:2 * n].rearrange("p (h c) -> p h c", c=QK)
                        scaled(DVE, o3[:, :, 64:80], s3[:, :, 80:96], sc, rd, wr, neg=True)
                        scaled(DVE, o3[:, :, 80:96], s3[:, :, 64:80], sc, rd, wr)
                    conv(w_uq_d[kc * 128:(kc + 1) * 128, :], H * QK, Wuq_s[kc], f_uq)

                def f_kv(s_, o_):
                    sc = gkv.t[:, 0:1]
                    rd, wr = [s_.r, gkv.r], [o_.r]
                    kb.memset(DVE, o_.t[:, 0:H * QK], 0.0, writes=wr)
                    s3 = s_.t[:, 0:H * 128].rearrange("p (h c) -> p h c", c=128)
                    ok = o_.t[:, 0:H * QK].rearrange("p (h c) -> p h c", c=QK)
                    ov = o_.t[:, H * QK:H * QK + H * VD].rearrange("p (h c) -> p h c", c=VD)
                    scaled(DVE, ok[:, :, 0:64], s3[:, :, 0:64], sc, rd, wr)
                    scaled(DVE, ov, s3[:, :, 64:128], sc, rd, wr)
                conv(w_ukv_d[:, :], H * 128, Wkv_s[:, :], f_kv)

                def plain(sc_t=None, k=None):
                    def f(s_, o_):
                        n = o_.cur
                        if sc_t is None:
                            kb.cp(DVE, o_.t[:, 0:n], s_.t[:, 0:n], reads=[s_.r], writes=[o_.r])
                        else:
                            scaled(DVE, o_.t[:, 0:n], s_.t[:, 0:n], sc_t.t[:, k:k + 1], [s_.r, sc_t.r], [o_.r])
                    return f
                for src, dst, nch, ncols, sct in ((w_ao_d, Wao_s, 4, D, None), (w_fo_d, Wfo_s, 4, D, None),
                                                   (w_out_d, Wo_s, 8, D, None), (w_up_d, Wup_s, 8, 2 * DFF, gf),
                                                   (w_dn_d, Wdn_s, NHC, D, None)):
                    for c in range(nch):
                        for w_ in wo:
                            w_.cur = ncols
                        conv(src[c * 128:(c + 1) * 128, :], ncols, dst[c], plain(sct, c))
                kb.barrier()

        def norm_transpose(es_tiles, xt, tw, hT, col0, part="both"):
            junk, ss, rstd, hb = es_tiles
            if part in ("both", "norm"):
                kb.act(junk.t[0:tw, :], xt.t[0:tw, :], AF.Square, reads=[xt.r], writes=[junk.r, ss.r],
                       accum_out=ss.t[0:tw, :])
                kb.act(ss.t[0:tw, :], ss.t[0:tw, :], AF.Sqrt, reads=[ss.r, epsq.r], writes=[ss.r],
                       bias=epsq.t[0:tw, 0:1], scale=1.0 / D)
                kb.recip(rstd.t[0:tw, :], ss.t[0:tw, :], reads=[ss.r], writes=[rstd.r])
                kb.ts(DVE, hb.t[0:tw, :], xt.t[0:tw, :], rstd.t[0:tw, 0:1], None, ALU.mult,
                      reads=[xt.r, rstd.r], writes=[hb.r])
            if part == "norm":
                return rstd
            for dc in range(8):
                kb.tr(pT.t[:, dc * 128:dc * 128 + tw], hb.t[0:tw, dc * 128:(dc + 1) * 128], ident.t[0:tw, 0:tw],
                      reads=[hb.r, ident.r], writes=[pT.r], signal=(dc == 7))
            kb.cp(DVE, hT.t[:, :, col0:col0 + tw],
                  pT.t[:, :].rearrange("p (c t) -> p c t", t=128)[:, :, 0:tw],
                  reads=[pT.r], writes=[hT.r])
            return rstd

        def mk_norm_tiles(es, tag, junk=None):
            return (junk if junk is not None else sb(es, f"junk{tag}", [128, D], BF16), sb(es, f"ss{tag}", [128, 1], F32),
                    sb(es, f"rstd{tag}", [128, 1], F32), sb(es, f"hb{tag}", [128, D], BF16))

        def fm_rstd(es_t, src_list, W, nfeat):
            sq, pbc, rs, rbc = es_t
            for i, (p_, npart) in enumerate(src_list):
                kb.act(sq.t[0:npart, i, 0:W], p_.t[0:npart, 0:W], AF.Square, reads=[p_.r], writes=[sq.r])
            for i, (p_, npart) in enumerate(src_list):
                kb.mm(pbc.t[:, 0:W], onesf.t[0:npart, :], sq.t[0:npart, i, 0:W], i == 0, i == len(src_list) - 1,
                      reads=[sq.r, onesf.r], writes=[pbc.r])
            kb.act(rs.t[:, 0:W], pbc.t[:, 0:W], AF.Sqrt, reads=[pbc.r, epsq.r], writes=[rs.r],
                   bias=epsq.t[:, 0:1], scale=1.0 / nfeat)
            kb.recip(rbc.t[:, 0:W], rs.t[:, 0:W], reads=[rs.r], writes=[rbc.r])
            return rbc

        def run_sequence(seq_idx, tag, S, N, x_full, x_own, n_own, halo, y_out):
            tb = tabs[tag]
            nq = n_own + (2 if halo else 0)
            nblk_a = S // 512
            R = 128 // N
            qblocks = [(b * 512, 512) for b in range(n_own // 512)]
            if halo:
                qblocks.append((n_own, 2))
            xv = x_full.rearrange("(n1 n2) d -> n2 n1 d", n2=N)
            Yv = [[Y_s[g, ri, 0:N * N * 128].rearrange("(n1 n2 c) -> n1 n2 c", n2=N, c=128) for ri in range(2)]
                  for g in range(FG)]

            if "A" in phases:
                kb.phase = 1 + 6 * seq_idx + 0
                with ExitStack() as es:
                    Wi = sb(es, "Wi_a", [128, 8, 704], BF16)
                    for dc in range(8):
                        kb.dma(SP, Wi.t[:, dc, :], Wi_s[dc, :, C_CKV:C_G], writes=[Wi.r], accum=True)
                    Wkv = sb(es, "Wkv_a", [128, H * QK + H * VD], BF16)
                    kb.dma(SP, Wkv.t[:], Wkv_s[:, :], writes=[Wkv.r])
                    xt = [sb(es, f"xa{i}", [128, D], F32) for i in range(5)]
                    nt = [mk_norm_tiles(es, "a")]
                    for k_ in range(1, 4):
                        nt.append((nt[0][0], sb(es, f"ssa{k_}", [128, 1], F32), sb(es, f"rstda{k_}", [128, 1], F32),
                                   sb(es, f"hba{k_}", [128, D], BF16)))
                    hT = [sb(es, f"hTa{i}", [128, 8, 512], BF16) for i in range(2)]
                    fr = (sb(es, "sq_a", [128, 1, 512], F32), pb[1], sb(es, "rs_a", [128, 512], F32),
                          sb(es, "rbc_a", [128, 512], F32))
                    ckvn = sb(es, "ckvn", [128, 512], BF16)
                    kcs = [sb(es, f"kcs{i}", [64, 512], F32) for i in range(2)]
                    kr2 = sb(es, "kr2", [64, 512], BF16)
                    KTb = [sb(es, f"KTb{i}", [QK, H, 512], BF16) for i in range(2)]
                    Vb = [sb(es, f"Vb{i}", [128, H, 4, VD + 1], BF16) for i in range(2)]
                    for v_ in Vb:
                        kb.memset(DVE, v_.t[:], 1.0, writes=[v_.r])
                    xfT = [sb(es, f"xfT{i}", [128, 512], BF16) for i in range(2)]
                    Yb = [sb(es, f"Yb{i}", [128, FG, 2, 4, 128], BF16) for i in range(2)]
                    tia = [0]

                    xa_of = {}

                    def prep_a(blk, part):
                        if blk >= nblk_a:
                            return
                        hTb = hT[blk % 2]
                        if part == "norm":
                            xa_of[blk] = []
                        for i in range(4):
                            if part == "norm":
                                x_ = xt[tia[0] % 5]
                                tia[0] += 1
                                xa_of[blk].append(x_)
                                tile_idx = blk * 4 + i
                                for q in range(R):
                                    kb.dma(SP, x_.t[q * N:(q + 1) * N, :], xv[tile_idx * R + q], writes=[x_.r], accum=(q > 0))
                            norm_transpose(nt[i], xa_of[blk][i], 128, hTb, i * 128, part=part)
                    prep_a(0, "norm")
                    prep_a(0, "tr")
                    for blk in range(nblk_a):
                        hTb = hT[blk % 2]
                        prep_a(blk + 1, "norm")
                        kc_ = kcs[blk % 2]
                        kb.dma(SP, kc_.t[:], tb["kcs"][:, blk * 512:(blk + 1) * 512], writes=[kc_.r])
                        for dc in range(8):
                            kb.mm(pb[0].t[:, :], Wi.t[:, dc, 0:128], hTb.t[:, dc, :], dc == 0, dc == 7,
                                  reads=[Wi.r, hTb.r], writes=[pb[0].r])
                        rbc = fm_rstd(fr, [(pb[0], 128)], 512, KVR)
                        kb.tt(DVE, ckvn.t[:], pb[0].t[:, :], rbc.t[:], ALU.mult, reads=[pb[0].r, rbc.r], writes=[ckvn.r])
                        for dc in range(8):
                            kb.mm(pb[2].t[0:64, :], Wi.t[:, dc, 128:192], hTb.t[:, dc, :], dc == 0, dc == 7,
                                  reads=[Wi.r, hTb.r], writes=[pb[2].r])
                        kb.tt(DVE, kr2.t[:], pb[2].t[0:64, :], kc_.t[:], ALU.mult, reads=[pb[2].r, kc_.r], writes=[kr2.r])
                        Y_ = Yb[blk % 2]
                        for g in range(FG):
                            p_ = pb[3 + g % 2]
                            for dc in range(8):
                                kb.mm(p_.t[:, :], Wi.t[:, dc, 192 + g * 128:192 + (g + 1) * 128], hTb.t[:, dc, :],
                                      dc == 0, dc == 7, reads=[Wi.r, hTb.r], writes=[p_.r])
                            xf_ = xfT[g % 2]
                            kb.cp(ACT, xf_.t[:], p_.t[:, :], reads=[p_.r], writes=[xf_.r])
                            for i2 in range(2):
                                p2 = pb[5 + i2]
                                for i3 in range(2):
                                    i = i2 * 2 + i3
                                    kb.mm(p2.t[:, i3 * 256:(i3 + 1) * 256], xf_.t[:, i * 128:(i + 1) * 128], cs128.t[:],
                                          True, True, reads=[xf_.r, cs128.r], writes=[p2.r], signal=(i3 == 1))
                                kb.cp(DVE, Y_.t[:, g, :, i2 * 2:i2 * 2 + 2, :],
                                      p2.t[:, :].rearrange("p (i r c) -> p r i c", i=2, r=2),
                                      reads=[p2.r], writes=[Y_.r])
                        for g in range(FG):
                            for ri in range(2):
                                for q in range(R):
                                    n2s = slice(blk * 4 * R + q, blk * 4 * R + q + 3 * R + 1, R) if R > 1 else \
                                        slice(blk * 4, blk * 4 + 4)
                                    kb.dma(POOL, Yv[g][ri][:, n2s, :], Y_.t[q * N:(q + 1) * N, g, ri, :, :],
                                           reads=[Y_.r])
                        KT_ = KTb[blk % 2]
                        for h in range(H):
                            p_ = pb[3 + h % 2]
                            kb.mm(p_.t[0:QK, :], Wkv.t[:, h * QK:(h + 1) * QK], ckvn.t[:], True, False,
                                  reads=[Wkv.r, ckvn.r], writes=[p_.r])
                            kb.mm(p_.t[0:QK, :], eplace.t[:, :], kr2.t[:], False, True,
                                  reads=[eplace.r, kr2.r], writes=[p_.r])
                            kb.cp(ACT if h % 2 else DVE, KT_.t[:, h, :], p_.t[0:QK, :], reads=[p_.r], writes=[KT_.r])
                        kb.dma(POOL, KT_s[:, :, blk * 512:(blk + 1) * 512].rearrange("h c s -> c h s"), KT_.t[:],
                               reads=[KT_.r])
                        V_ = Vb[blk % 2]
                        for i in range(4):
                            p_ = pb[5 + i % 2]
                            kb.mm(p_.t[:, :], ckvn.t[:, i * 128:(i + 1) * 128], Wkv.t[:, H * QK:H * QK + H * VD],
                                  True, True, reads=[ckvn.r, Wkv.r], writes=[p_.r])
                            kb.cp(ACT if i % 2 else DVE, V_.t[:, :, i, 0:VD],
                                  p_.t[:, :].rearrange("p (h c) -> p h c", c=VD), reads=[p_.r], writes=[V_.r])
                        kb.dma(POOL, V_s[:, :, blk * 4:(blk + 1) * 4, :].rearrange("h p c e -> p h c e"), V_.t[:],
                               reads=[V_.r])
                        prep_a(blk + 1, "tr")
                    kb.barrier()

            if "F" in phases:
                kb.phase = 1 + 6 * seq_idx + 1
                with ExitStack() as es:
                    Yt = sb(es, "Yt", [128, 2, N * 128], BF16)
                    At = sb(es, "At", [128, 2, 128 * N], BF16)
                    ca = sb(es, "ca", [128, 2 * N], BF16)
                    cbm = sb(es, "cbm", [128, 2 * N], BF16)
                    ecs = sb(es, "ecs", [128, 2, nq], BF16)
                    FTb = [sb(es, f"FTb{i}", [128, nq], BF16) for i in range(2)]
                    kb.dma(SP, ca.t[0:N, :], tb["ca"][:, :], writes=[ca.r])
                    kb.dma(SP, cbm.t[0:N, :], tb["cb"][:, :], writes=[cbm.r])
                    kb.dma(SP, ecs.t[0:N, :, :], tb["ecs"][:, :, :], writes=[ecs.r])
                    K2L = n_own // N
                    cpb = 512 // (2 * N)
                    kpb = 512 // K2L
                    fscale = 1.0 / float(np.sqrt(S * 128.0))
                    A4 = At.t[:, :, :].rearrange("p r (c k) -> p r c k", k=N)
                    Y4 = Yt.t[:, :, :].rearrange("p r (n c) -> p r n c", c=128)
                    for g in range(FG):
                        for ri in range(2):
                            kb.dma(SP, Yt.t[0:N, ri, :], Y_s[g, ri, 0:N * N * 128].rearrange("(n r) -> n r", n=N),
                                   writes=[Yt.r], accum=(ri > 0))
                        for cg in range(128 // cpb):
                            p_ = pb[cg % 3]
                            for ci in range(cpb):
                                c_ = cg * cpb + ci
                                o_ = p_.t[0:N, ci * 2 * N:(ci + 1) * 2 * N]
                                kb.mm(o_, Y4[0:N, 0, :, c_], ca.t[0:N, :], True, False, reads=[Yt.r, ca.r], writes=[p_.r])
                                kb.mm(o_, Y4[0:N, 1, :, c_], cbm.t[0:N, :], False, True, reads=[Yt.r, cbm.r], writes=[p_.r],
                                      signal=(ci == cpb - 1))
                            kb.cp(ACT if cg % 2 else DVE, A4[0:N, :, cg * cpb:(cg + 1) * cpb, :],
                                  p_.t[0:N, :].rearrange("p (c r k) -> p r c k", c=cpb, r=2),
                                  reads=[p_.r], writes=[At.r])
                        F_ = FTb[g % 2]
                        Fv = F_.t[:, 0:n_own].rearrange("p (k2 k1) -> p k1 k2", k1=N)
                        for kg in range(N // kpb):
                            p_ = pb[3 + kg % 3]
                            for ki in range(kpb):
                                k1 = kg * kpb + ki
                                o_ = p_.t[:, ki * K2L:(ki + 1) * K2L]
                                kb.mm(o_, A4[0:N, 0, :, k1], ecs.t[0:N, 0, k1:n_own:N], True, False,
                                      reads=[At.r, ecs.r], writes=[p_.r])
                                kb.mm(o_, A4[0:N, 1, :, k1], ecs.t[0:N, 1, k1:n_own:N], False, True,
                                      reads=[At.r, ecs.r], writes=[p_.r], signal=(ki == kpb - 1))
                            kb.act(Fv[:, kg * kpb:(kg + 1) * kpb, :], p_.t[:, :].rearrange("p (k c) -> p k c", c=K2L),
                                   AF.Copy, reads=[p_.r], writes=[F_.r], scale=fscale)
                        if halo:
                            p_ = pb[6]
                            for hi, k1 in enumerate((N - 1, 0)):
                                o_ = p_.t[:, hi:hi + 1]
                                kb.mm(o_, A4[0:N, 0, :, k1], ecs.t[0:N, 0, n_own + hi:n_own + hi + 1], True, False,
                                      reads=[At.r, ecs.r], writes=[p_.r])
                                kb.mm(o_, A4[0:N, 1, :, k1], ecs.t[0:N, 1, n_own + hi:n_own + hi + 1], False, True,
                                      reads=[At.r, ecs.r], writes=[p_.r], signal=(hi == 1))
                            kb.act(F_.t[:, n_own:n_own + 2], p_.t[:, 0:2], AF.Copy, reads=[p_.r], writes=[F_.r],
                                   scale=fscale)
                        kb.dma(POOL, FT_s[g, :, 0:nq], F_.t[:, 0:nq], reads=[F_.r])
                    kb.barrier()

            if "Q" in phases:
                kb.phase = 1 + 6 * seq_idx + 2
                with ExitStack() as es:
                    Wi = sb(es, "Wi_q", [128, 8, 256], BF16)
                    for dc in range(8):
                        kb.dma(SP, Wi.t[:, dc, :], Wi_s[dc, :, 0:256], writes=[Wi.r], accum=True)
                    Wuq = sb(es, "Wuq", [128, 2, 2 * H * QK], BF16)
                    for kc in range(2):
                        kb.dma(SP, Wuq.t[:, kc, :], Wuq_s[kc], writes=[Wuq.r], accum=True)
                    xt = [sb(es, f"xq{i}", [128, D], F32) for i in range(5)]
                    nt = [mk_norm_tiles(es, "q")]
                    for k_ in range(1, 4):
                        nt.append((nt[0][0], sb(es, f"ssq{k_}", [128, 1], F32), sb(es, f"rstdq{k_}", [128, 1], F32),
                                   sb(es, f"hbq{k_}", [128, D], BF16)))
                    hT = [sb(es, f"hTq{i}", [128, 8, 512], BF16) for i in range(2)]
                    fr = (sb(es, "sq_q", [128, 2, 512], F32), pb[2], sb(es, "rs_q", [128, 512], F32),
                          sb(es, "rbc_q", [128, 512], F32))
                    cqn = sb(es, "cqn", [128, 2, 512], BF16)
                    qc = [sb(es, f"qc{i}", [QK, 512], F32) for i in range(2)]
                    qs = [sb(es, f"qs{i}", [QK, 512], F32) for i in range(2)]
                    t1 = [sb(es, f"t1{i}", [QK, 512], F32) for i in range(3)]
                    t2 = [sb(es, f"t2{i}", [QK, 512], F32) for i in range(3)]
                    Qb = [sb(es, f"Qb{i}", [QK, H, 512], BF16) for i in range(2)]
                    tiq = [0]

                    xq_of = {}

                    def prep_q(bi, part):
                        if bi >= len(qblocks):
                            return
                        c0, W = qblocks[bi]
                        hTb = hT[bi % 2]
                        ntile = (W + 127) // 128
                        if part == "norm":
                            xq_of[bi] = []
                        for i in range(ntile):
                            tw = min(128, W - i * 128)
                            if part == "norm":
                                x_ = xt[tiq[0] % 5]
                                tiq[0] += 1
                                xq_of[bi].append(x_)
                                kb.dma(SP, x_.t[0:tw, :], x_own[c0 + i * 128:c0 + i * 128 + tw, :], writes=[x_.r])
                            norm_transpose(nt[i], xq_of[bi][i], tw, hTb, i * 128, part=part)
                    prep_q(0, "norm")
                    prep_q(0, "tr")
                    for bi, (c0, W) in enumerate(qblocks):
                        hTb = hT[bi % 2]
                        prep_q(bi + 1, "norm")
                        qc_, qs_ = qc[bi % 2], qs[bi % 2]
                        kb.dma(SP, qc_.t[:, 0:W], tb["qcos"][:, c0:c0 + W], writes=[qc_.r])
                        kb.dma(SP, qs_.t[:, 0:W], tb["qsin"][:, c0:c0 + W], writes=[qs_.r])
                        for kc in range(2):
                            for dc in range(8):
                                kb.mm(pb[kc].t[:, 0:W], Wi.t[:, dc, kc * 128:(kc + 1) * 128], hTb.t[:, dc, 0:W],
                                      dc == 0, dc == 7, reads=[Wi.r, hTb.r], writes=[pb[kc].r])
                        rbc = fm_rstd(fr, [(pb[0], 128), (pb[1], 128)], W, QR)
                        for kc in range(2):
                            kb.tt(DVE, cqn.t[:, kc, 0:W], pb[kc].t[:, 0:W], rbc.t[:, 0:W], ALU.mult,
                                  reads=[pb[kc].r, rbc.r], writes=[cqn.r])
                        Q_ = Qb[bi % 2]
                        for h in range(H):
                            pq, pr = ((pb[3], pb[4]), (pb[5], pb[6]), (pb[0], pb[1]))[h % 3]
                            for kc in range(2):
                                kb.mm(pq.t[0:QK, 0:W], Wuq.t[:, kc, h * QK:(h + 1) * QK], cqn.t[:, kc, 0:W],
                                      kc == 0, kc == 1, reads=[Wuq.r, cqn.r], writes=[pq.r])
                            for kc in range(2):
                                kb.mm(pr.t[0:QK, 0:W], Wuq.t[:, kc, H * QK + h * QK:H * QK + (h + 1) * QK],
                                      cqn.t[:, kc, 0:W], kc == 0, kc == 1, reads=[Wuq.r, cqn.r], writes=[pr.r])
                            a_, b_ = t1[h % 3], t2[h % 3]
                            kb.tt(DVE, a_.t[:, 0:W], pq.t[0:QK, 0:W], qc_.t[:, 0:W], ALU.mult,
                                  reads=[pq.r, qc_.r], writes=[a_.r])
                            kb.tt(DVE, b_.t[:, 0:W], pr.t[0:QK, 0:W], qs_.t[:, 0:W], ALU.mult,
                                  reads=[pr.r, qs_.r], writes=[b_.r])
                            kb.tt(POOL, Q_.t[:, h, 0:W], a_.t[:, 0:W], b_.t[:, 0:W], ALU.add,
                                  reads=[a_.r, b_.r], writes=[Q_.r])
                        kb.dma(POOL, QT_s[:, :, c0:c0 + W].rearrange("h c s -> c h s"), Q_.t[:, :, 0:W], reads=[Q_.r])
                        prep_q(bi + 1, "tr")
                    kb.barrier()

            if "T" in phases:
                kb.phase = 1 + 6 * seq_idx + 3
                with ExitStack() as es:
                    nch = S // 128
                    KT = [sb(es, f"KT{i}", [QK, S], BF16) for i in range(2)]
                    Vt = [sb(es, f"Vt{i}", [128, nch, VD + 1], BF16) for i in range(2)]
                    Qt = [sb(es, f"Qt{i}", [QK, 512], BF16) for i in range(2)]
                    Osb = [sb(es, f"Osb{i}", [VD + 1, 512], F32) for i in range(2)]
                    rinv = [sb(es, f"rinv{i}", [VD + 1, 512], F32) for i in range(2)]
                    Ab = [sb(es, f"Ab{i}", [VD, 512], BF16) for i in range(2)]
                    Pt2 = [sb(es, f"Pp{i}", [128, 2, 512], BF16) for i in range(3)]
                    SB = []
                    for i in range(3):
                        t_ = T(PSALL[:, 2 * i:2 * i + 2, :])
                        SB.append(t_)
                    psO = [pb[6], pb[7]]
                    npair = nch // 2
                    blocks = [(h, qi) for h in range(H) for qi in range(len(qblocks))]
                    ptasks = [(bi, j) for bi in range(len(blocks)) for j in range(npair)]
                    bst = {}
                    cur_head = [-1]
                    loaded = set()

                    def load_head(hh):
                        if hh < H and hh not in loaded:
                            loaded.add(hh)
                            kb.dma(SP, KT[hh % 2].t[:, :], KT_s[hh, :, 0:S], writes=[KT[hh % 2].r])
                            kb.dma(SP, Vt[hh % 2].t[:, :, :], V_s[hh, :, 0:nch, :], writes=[Vt[hh % 2].r])

                    qloaded = set()

                    def load_q(bi):
                        if bi < len(blocks) and bi not in qloaded:
                            qloaded.add(bi)
                            h_, qi_ = blocks[bi]
                            c0_, W_ = qblocks[qi_]
                            kb.dma(SP, Qt[bi % 2].t[:, 0:W_], QT_s[h_, :, c0_:c0_ + W_], writes=[Qt[bi % 2].r])

                    def ensure_block(bi):
                        if bi in bst:
                            return bst[bi]
                        h, qi = blocks[bi]
                        c0, W = qblocks[qi]
                        KT_, V_ = KT[h % 2], Vt[h % 2]
                        Q_ = Qt[bi % 2]
                        load_q(bi)
                        load_head(h)
                        bst[bi] = dict(h=h, c0=c0, W=W, KT=KT_, V=V_, Q=Q_, pO=psO[bi % 2])
                        return bst[bi]

                    def score_pair(gi):
                        bi, j = ptasks[gi]
                        st = ensure_block(bi)
                        p_ = SB[gi % 3]
                        W = st["W"]
                        for c in range(2):
                            kc = 2 * j + c
                            kb.mm(p_.t[:, c, 0:W], st["KT"].t[:, kc * 128:(kc + 1) * 128], st["Q"].t[:, 0:W], True, True,
                                  reads=[st["KT"].r, st["Q"].r], writes=[p_.r], signal=(c == 1))

                    ntask = len(ptasks)
                    for gi in range(min(2, ntask)):
                        score_pair(gi)
                    pend = []
                    for gi in range(ntask):
                        if gi + 2 < ntask:
                            score_pair(gi + 2)
                        while pend and pend[0][0] <= gi:
                            pend.pop(0)[1]()
                        bi, j = ptasks[gi]
                        st = bst[bi]
                        if j == 0:
                            load_q(bi + 1)
                        if blocks[bi][1] == 0 and j == min(2, npair - 1):
                            load_head(blocks[bi][0] + 1)
                        W, pO, V_ = st["W"], st["pO"], st["V"]
                        p_ = SB[gi % 3]
                        P_ = Pt2[gi % 3]
                        kb.act(P_.t[:, :, 0:W], p_.t[:, :, 0:W], AF.Exp, reads=[p_.r], writes=[P_.r])
                        for c in range(2):
                            kc = 2 * j + c
                            kb.mm(pO.t[0:VD + 1, 0:W], V_.t[:, kc, :], P_.t[:, c, 0:W], kc == 0, kc == nch - 1,
                                  reads=[V_.r, P_.r], writes=[pO.r], signal=(c == 1))
                        if j == npair - 1:
                            O_ = Osb[bi % 2]
                            r_ = rinv[bi % 2]
                            kb.cp(DVE, O_.t[:, 0:W], pO.t[0:VD + 1, 0:W], reads=[pO.r], writes=[O_.r])
                            kb.recip(r_.t[VD:VD + 1, 0:W], O_.t[VD:VD + 1, 0:W], reads=[O_.r], writes=[r_.r])

                            def part2(bi=bi, st=st, O_=O_, r_=r_, W=W, pO=pO):
                                h, c0 = st["h"], st["c0"]
                                kb.mm(pO.t[0:VD, 0:W], onesf.t[VD:VD + 1, 0:VD], r_.t[VD:VD + 1, 0:W], True, True,
                                      reads=[onesf.r, r_.r], writes=[pO.r])
                                A_ = Ab[bi % 2]
                                kb.tt(DVE, A_.t[:, 0:W], pO.t[0:VD, 0:W], O_.t[0:VD, 0:W], ALU.mult,
                                      reads=[pO.r, O_.r], writes=[A_.r])
                                kb.dma(POOL, AT_s[h * VD:(h + 1) * VD, c0:c0 + W], A_.t[:, 0:W], reads=[A_.r])
                            pend.append((gi + 2, part2))
                            del bst[bi]
                    for _, f_ in pend:
                        f_()
                    kb.barrier()

            if "M" in phases:
                kb.phase = 1 + 6 * seq_idx + 4
                with ExitStack() as es:
                    Wg = sb(es, "Wg", [128, 8, 2048], BF16)
                    for dc in range(8):
                        kb.dma(SP, Wg.t[:, dc, :], Wi_s[dc, :, C_G:WI_COLS], writes=[Wg.r], accum=True)
                    Wao = sb(es, "Wao", [128, 4, D], BF16)
                    Wfo = sb(es, "Wfo", [128, 4, D], BF16)
                    Wo = sb(es, "Wo", [128, 8, D], BF16)
                    for c in range(4):
                        kb.dma(SP, Wao.t[:, c, :], Wao_s[c], writes=[Wao.r], accum=True)
                        kb.dma(SP, Wfo.t[:, c, :], Wfo_s[c], writes=[Wfo.r], accum=True)
                    for c in range(8):
                        kb.dma(SP, Wo.t[:, c, :], Wo_s[c], writes=[Wo.r], accum=True)
                    xt = [sb(es, f"xm{i}", [128, D], F32) for i in range(8)]
                    nt = [mk_norm_tiles(es, "m")]
                    for k_ in range(1, 4):
                        nt.append((nt[0][0], sb(es, f"ssm{k_}", [128, 1], F32), sb(es, f"rstdm{k_}", [128, 1], F32),
                                   sb(es, f"hbm{k_}", [128, D], BF16)))
                    nt2 = [mk_norm_tiles(es, "m2", junk=nt[0][0])]
                    for k_ in range(1, 4):
                        nt2.append((nt2[0][0], sb(es, f"ss2{k_}", [128, 1], F32), sb(es, f"rstd2{k_}", [128, 1], F32),
                                    sb(es, f"hb2{k_}", [128, D], BF16)))
                    hT = [sb(es, f"hTm{i}", [128, 8, 512], BF16) for i in range(2)]
                    Gt = sb(es, "Gt", [128, 16, 512], BF16)
                    ATt = [sb(es, f"ATt{i}", [128, 4, 512], BF16) for i in range(1)]
                    FTt = [sb(es, f"FTt{i}", [128, 4, 512], BF16) for i in range(1)]
                    ma = [sb(es, f"ma{i}", [128, 512], F32) for i in range(2)]
                    mf = [sb(es, f"mf{i}", [128, 512], F32) for i in range(2)]
                    mg = sb(es, "mg", [128, 8, 512], BF16)
                    x1 = [sb(es, f"x1{i}", [128, D], F32) for i in range(4)]
                    h2T = [sb(es, f"h2T{i}", [128, 8, 512], BF16) for i in range(2)]
                    if not halo:
                        for dc in range(8):
                            kb.dma(POOL, H2T_s[dc, :, 0:1], zeros.t[:, 0:1], reads=[zeros.r], slow=True)
                            kb.dma(POOL, H2T_s[dc, :, n_own + 1:n_own + 2], zeros.t[:, 0:1], reads=[zeros.r], slow=True)
                    tim = [0]
                    xts_of = {}

                    def prep_m(bi, part):
                        if bi >= len(qblocks):
                            return
                        c0, W = qblocks[bi]
                        hTb = hT[bi % 2]
                        ntile = (W + 127) // 128
                        if part == "norm":
                            xts_of[bi] = []
                        for i in range(ntile):
                            tw = min(128, W - i * 128)
                            if part == "norm":
                                x_ = xt[tim[0] % 8]
                                tim[0] += 1
                                xts_of[bi].append((x_, tw))
                                kb.dma(SP, x_.t[0:tw, :], x_own[c0 + i * 128:c0 + i * 128 + tw, :], writes=[x_.r])
                            norm_transpose(nt[i], xts_of[bi][i][0], tw, hTb, i * 128, part=part)
                    prep_m(0, "norm")
                    prep_m(0, "tr")
                    for bi, (c0, W) in enumerate(qblocks):
                        is_halo = halo and bi == len(qblocks) - 1
                        hTb = hT[bi % 2]
                        prep_m(bi + 1, "norm")
                        xts = xts_of[bi]
                        AT_, FT_ = ATt[0], FTt[0]
                        for c in range(4):
                            kb.dma(SP, AT_.t[:, c, 0:W], AT_s[c * 128:(c + 1) * 128, c0:c0 + W], writes=[AT_.r], accum=(c > 0))
                            kb.dma(SP, FT_.t[:, c, 0:W], FT_s[c, :, c0:c0 + W], writes=[FT_.r], accum=(c > 0))
                        for gc in range(16):
                            p_ = pb[gc % 3]
                            for dc in range(8):
                                kb.mm(p_.t[:, 0:W], Wg.t[:, dc, gc * 128:(gc + 1) * 128], hTb.t[:, dc, 0:W],
                                      dc == 0, dc == 7, reads=[Wg.r, hTb.r], writes=[p_.r])
                            kb.act(Gt.t[:, gc, 0:W], p_.t[:, 0:W], AF.Sigmoid, reads=[p_.r], writes=[Gt.r])
                        for oc in range(8):
                            pa, pf = pb[3 + 2 * (oc % 2)], pb[4 + 2 * (oc % 2)]
                            for c in range(4):
                                kb.mm(pa.t[:, 0:W], Wao.t[:, c, oc * 128:(oc + 1) * 128], AT_.t[:, c, 0:W],
                                      c == 0, c == 3, reads=[Wao.r, AT_.r], writes=[pa.r])
                            for c in range(4):
                                kb.mm(pf.t[:, 0:W], Wfo.t[:, c, oc * 128:(oc + 1) * 128], FT_.t[:, c, 0:W],
                                      c == 0, c == 3, reads=[Wfo.r, FT_.r], writes=[pf.r])
                            a_, f_ = ma[oc % 2], mf[oc % 2]
                            kb.tt(DVE, a_.t[:, 0:W], pa.t[:, 0:W], Gt.t[:, oc, 0:W], ALU.mult,
                                  reads=[pa.r, Gt.r], writes=[a_.r])
                            kb.tt(DVE, f_.t[:, 0:W], pf.t[:, 0:W], Gt.t[:, 8 + oc, 0:W], ALU.mult,
                                  reads=[pf.r, Gt.r], writes=[f_.r])
                            kb.tt(POOL, mg.t[:, oc, 0:W], a_.t[:, 0:W], f_.t[:, 0:W], ALU.add,
                                  reads=[a_.r, f_.r], writes=[mg.r])
                        h2b = h2T[bi % 2]
                        for i, (x_, tw) in enumerate(xts):
                            x1_ = x1[i % 4]
                            for hf in range(2):
                                p_ = pb[hf + 2 * (i % 2)]
                                for kc in range(8):
                                    kb.mm(p_.t[0:tw, :], mg.t[:, kc, i * 128:i * 128 + tw], Wo.t[:, kc, hf * 512:(hf + 1) * 512],
                                          kc == 0, kc == 7, reads=[mg.r, Wo.r], writes=[p_.r])
                                kb.tt(DVE, x1_.t[0:tw, hf * 512:(hf + 1) * 512], p_.t[0:tw, :],
                                      x_.t[0:tw, hf * 512:(hf + 1) * 512], ALU.add, reads=[p_.r, x_.r], writes=[x1_.r])
                            if not is_halo:
                                kb.dma(POOL, X1_s[c0 + i * 128:c0 + i * 128 + tw, :], x1_.t[0:tw, :], reads=[x1_.r])
                            norm_transpose(nt2[i % 4], x1_, tw, h2b, i * 128, part="norm")
                        prep_m(bi + 1, "tr")
                        for i, (x_, tw) in enumerate(xts):
                            norm_transpose(nt2[i % 4], x1[i % 4], tw, h2b, i * 128, part="tr")
                        if is_halo:
                            kb.tt(DVE, h2b.t[:, :, 0:2], h2b.t[:, :, 0:2],
                                  mask.t[:, :].unsqueeze(1).to_broadcast([128, 8, 2]), ALU.mult,
                                  reads=[h2b.r, mask.r], writes=[h2b.r])
                            kb.dma(POOL, H2T_s[:, :, 0:1].rearrange("c p s -> p c s"), h2b.t[:, :, 0:1], reads=[h2b.r], slow=True)
                            kb.dma(POOL, H2T_s[:, :, n_own + 1:n_own + 2].rearrange("c p s -> p c s"), h2b.t[:, :, 1:2],
                                   reads=[h2b.r], slow=True)
                        else:
                            kb.dma(POOL, H2T_s[:, :, 1 + c0:1 + c0 + W].rearrange("c p s -> p c s"), h2b.t[:, :, 0:W],
                                   reads=[h2b.r])
                    kb.barrier()

            if "N" in phases:
                kb.phase = 1 + 6 * seq_idx + 5
                with ExitStack() as es:
                    Wup = sb(es, "Wup", [128, 8, 2 * DFF], BF16)
                    Wdn = sb(es, "Wdn", [128, NHC, D], BF16)
                    NWB = 11
                    wup_r = [Res() for _ in range(NWB)]
                    order = []
                    for cp_ in range(NHC):
                        for blk_ in (cp_ // 4, (cp_ + NHC) // 4):
                            if blk_ not in order:
                                order.append(blk_)
                    for blk_ in order:
                        for c in range(8):
                            kb.dma(SP, Wup.t[:, c, blk_ * 512:(blk_ + 1) * 512], Wup_s[c, :, blk_ * 512:(blk_ + 1) * 512],
                                   writes=[wup_r[blk_]], accum=True)
                    for c in range(NHC):
                        kb.dma(SP, Wdn.t[:, c, :], Wdn_s[c], writes=[Wdn.r], accum=True)
                    cw = sb(es, "cw", [128, NFC, 3], F32)
                    cbias = sb(es, "cbias", [128, NFC], F32)
                    gfin = sb(es, "gfin", [128, D], F32)
                    kb.dma(SP, cw.t[:], convw_d[:, :, :], writes=[cw.r])
                    kb.dma(SP, cbias.t[:], convb_d[:, :], writes=[cbias.r])
                    kb.dma(SP, gfin.t[:], gfin_d[:, :], writes=[gfin.r])
                    h2 = [sb(es, f"h2n{i}", [128, 8, 512], BF16) for i in range(2)]
                    acc = [sb(es, f"acc{i}", [128, 512], F32) for i in range(4)]
                    sg = [sb(es, f"sg{i}", [128, 512], F32) for i in range(2)]
                    actT = sb(es, "actT", [128, NHC, 512], BF16)
                    x1t = [sb(es, f"x1n{i}", [128, D], F32) for i in range(2)]
                    yo = [sb(es, f"yo{i}", [128, D], F32) for i in range(2)]
                    ss = [sb(es, f"ssn{i}", [128, 1], F32) for i in range(2)]
                    rstd = [sb(es, f"rstdn{i}", [128, 1], F32) for i in range(2)]
                    nb = (n_own + FB - 1) // FB
                    ti = 0
                    ai = 0
                    for b in range(nb):
                        t0 = b * FB
                        Wb = min(FB, n_own - t0)
                        h2_ = h2[b % 2]
                        kb.dma(SP, h2_.t[:, :, 0:Wb + 2], H2T_s[:, :, t0:t0 + Wb + 2].rearrange("c p s -> p c s"),
                               writes=[h2_.r])
                        for cp_ in range(NHC):
                            accs = []
                            for half, ch in enumerate((cp_, cp_ + NHC)):
                                p_ = pb[(2 * cp_ + half) % 4]
                                for dc in range(8):
                                    kb.mm(p_.t[:, 0:Wb + 2], Wup.t[:, dc, ch * 128:(ch + 1) * 128], h2_.t[:, dc, 0:Wb + 2],
                                          dc == 0, dc == 7, reads=[wup_r[ch // 4], h2_.r], writes=[p_.r])
                                a_ = acc[ai % 4]
                                ai += 1
                                kb.act(a_.t[:, 0:Wb], p_.t[:, 1:Wb + 1], AF.Identity, reads=[p_.r, cw.r, cbias.r],
                                       writes=[a_.r], scale=cw.t[:, ch, 1:2], bias=cbias.t[:, ch:ch + 1])
                                kb.stt(DVE, a_.t[:, 0:Wb], p_.t[:, 0:Wb], cw.t[:, ch, 0:1], a_.t[:, 0:Wb], ALU.mult, ALU.add,
                                       reads=[p_.r, a_.r, cw.r], writes=[a_.r])
                                kb.stt(DVE, a_.t[:, 0:Wb], p_.t[:, 2:Wb + 2], cw.t[:, ch, 2:3], a_.t[:, 0:Wb], ALU.mult, ALU.add,
                                       reads=[p_.r, a_.r, cw.r], writes=[a_.r])
                                accs.append(a_)
                            s_ = sg[cp_ % 2]
                            kb.act(s_.t[:, 0:Wb], accs[0].t[:, 0:Wb], AF.Silu, reads=[accs[0].r], writes=[s_.r])
                            kb.tt(POOL, actT.t[:, cp_, 0:Wb], s_.t[:, 0:Wb], accs[1].t[:, 0:Wb], ALU.mult,
                                  reads=[s_.r, accs[1].r], writes=[actT.r])
                        ntile = (Wb + 127) // 128
                        for i in range(ntile):
                            tw = min(128, Wb - i * 128)
                            r0 = t0 + i * 128
                            x1_ = x1t[ti % 2]
                            x2_ = x1_
                            y_ = yo[ti % 2]
                            ss_ = ss[ti % 2]
                            rs_ = rstd[ti % 2]
                            kb.dma(SP, x1_.t[0:tw, :], X1_s[r0:r0 + tw, :], writes=[x1_.r])
                            for hf in range(2):
                                p_ = pb[4 + hf]
                                for c in range(NHC):
                                    kb.mm(p_.t[0:tw, :], actT.t[:, c, i * 128:i * 128 + tw], Wdn.t[:, c, hf * 512:(hf + 1) * 512],
                                          c == 0, c == NHC - 1, reads=[actT.r, Wdn.r], writes=[p_.r])
                                kb.tt(DVE, x2_.t[0:tw, hf * 512:(hf + 1) * 512], p_.t[0:tw, :],
                                      x1_.t[0:tw, hf * 512:(hf + 1) * 512], ALU.add, reads=[p_.r, x1_.r], writes=[x2_.r])
                            kb.act(y_.t[0:tw, :], x2_.t[0:tw, :], AF.Square, reads=[x2_.r], writes=[y_.r, ss_.r],
                                   accum_out=ss_.t[0:tw, :])
                            kb.act(ss_.t[0:tw, :], ss_.t[0:tw, :], AF.Sqrt, reads=[ss_.r, epsq.r], writes=[ss_.r],
                                   bias=epsq.t[0:tw, 0:1], scale=1.0 / D)
                            kb.recip(rs_.t[0:tw, :], ss_.t[0:tw, :], reads=[ss_.r], writes=[rs_.r])
                            kb.stt(DVE, y_.t[0:tw, :], x2_.t[0:tw, :], rs_.t[0:tw, 0:1], gfin.t[0:tw, :], ALU.mult, ALU.mult,
                                   reads=[x2_.r, rs_.r, gfin.r], writes=[y_.r])
                            kb.dma(POOL, y_out[r0:r0 + tw, :], y_.t[0:tw, :], reads=[y_.r])
                            ti += 1
                    kb.barrier()

        if cfg.get("do_prompt", True):
            run_sequence(0, "p", SP_, NP_, xp_full, xp_own, OWNP, True, yp)
        for si in range(cfg.get("n_run_seq", NSEQ)):
            run_sequence(1 + si, "s", SS_, NS_, xs[si], xs[si], SS_, False, ys[si])
        kb.barrier()
    return nc


def _bf(a):
    return np.ascontiguousarray(a.astype(ml_dtypes.bfloat16))


def host_tables(S, N, own_pos, scale):
    inv = (10000.0 ** (-np.arange(0, ROPE, 2, dtype=np.float32) / ROPE)).astype(np.float32)
    tq = np.arange(S)
    posA = (tq % N) * N + tq // N
    angK = posA[None, :].astype(np.float32) * np.concatenate([inv, inv])[:, None]
    kcs = np.concatenate([np.cos(angK), np.sin(angK)], 0).astype(np.float32)
    angQ = np.asarray(own_pos, np.float32)[None, :] * np.concatenate([inv, inv])[:, None]
    nq = len(own_pos)
    qcos = np.concatenate([np.full((NOPE, nq), scale, np.float32), scale * np.cos(angQ)], 0).astype(np.float32)
    qsin = np.concatenate([np.zeros((NOPE, nq), np.float32), scale * np.sin(angQ)], 0).astype(np.float32)
    n2 = np.arange(N, dtype=np.float64)[:, None]
    ph = 2 * np.pi * n2 * np.asarray(own_pos, np.float64)[None, :] / S
    ecs = np.stack([np.cos(ph), np.sin(ph)], 1)
    th = 2 * np.pi * np.outer(np.arange(N), np.arange(N)) / N
    ca = np.concatenate([np.cos(th), -np.sin(th)], 1)
    cb = np.concatenate([np.sin(th), np.cos(th)], 1)
    return dict(kcs=kcs, qcos=qcos, qsin=qsin, ecs=_bf(ecs), ca=_bf(ca), cb=_bf(cb))


def host_consts():
    th = 2 * np.pi * np.outer(np.arange(128), np.arange(128)) / 128
    cs128 = np.concatenate([np.cos(th), -np.sin(th)], 1)
    eplace = np.zeros((64, QK), np.float32)
    for r in range(32):
        eplace[r, 64 + r] = 1.0
        eplace[32 + r, 64 + r] = 1.0
    return dict(cs128=_bf(cs128), ident=_bf(np.eye(128, dtype=np.float32)), eplace=_bf(eplace))


def weight_maps(g_mix, w_in, g_q, w_uq, g_kv, w_ukv, w_attn_out, w_fourier_out, w_out, g_ffn, w_up,
                conv_w, conv_b, w_down, g_final):
    c = np.ascontiguousarray
    f = np.float32
    return dict(
        w_in=c(w_in[0], f), w_uq=c(w_uq[0], f), w_ukv=c(w_ukv[0], f), w_ao=c(w_attn_out[0], f),
        w_fo=c(w_fourier_out[0], f), w_out=c(w_out[0], f), w_up=c(w_up[0], f), w_dn=c(w_down[0], f),
        g_mix8=c(g_mix[0].reshape(8, 128).T, f), g_q2=c(g_q[0].reshape(2, 128).T, f),
        g_kv1=c(g_kv[0].reshape(1, 128).T, f), g_ffn8=c(g_ffn[0].reshape(8, 128).T, f),
        convw=c(conv_w[0].reshape(3, NFC, 128).transpose(2, 1, 0), f),
        convb=c(conv_b[0].reshape(NFC, 128).T, f),
        gfin=c(np.broadcast_to(g_final.reshape(1, D), (128, D)), f))


_NC_CACHE = {}


def run(cfg, x_prompt, x_sample, wm, n_cores=8):
    SP_, SS_, NSEQ = cfg["SP"], cfg["SS"], cfg["NSEQ"]
    OWNP = SP_ // 4
    key = tuple(sorted((k, str(v)) for k, v in cfg.items()))
    if key not in _NC_CACHE:
        _NC_CACHE[key] = build(cfg)
    nc = _NC_CACHE[key]
    scale = float(QK) ** -0.5
    consts = host_consts()
    ts = host_tables(SS_, cfg["NS"], np.arange(SS_), scale)
    in_maps = []
    for c in range(n_cores):
        b, j = c // 4, c % 4
        own_pos = list(range(OWNP * j, OWNP * (j + 1))) + [OWNP * j - 1, OWNP * (j + 1)]
        tp = host_tables(SP_, cfg["NP"], own_pos, scale)
        xo = np.zeros((OWNP + 2, D), np.float32)
        xo[:OWNP] = x_prompt[b, OWNP * j:OWNP * (j + 1)]
        mask = np.zeros((128, 2), np.float32)
        if j > 0:
            xo[OWNP] = x_prompt[b, OWNP * j - 1]
            mask[:, 0] = 1.0
        if j < 3:
            xo[OWNP + 1] = x_prompt[b, OWNP * (j + 1)]
            mask[:, 1] = 1.0
        m = dict(xp_full=np.ascontiguousarray(x_prompt[b]), xp_own=xo,
                 xs=np.ascontiguousarray(x_sample[NSEQ * c:NSEQ * (c + 1)]), mask=mask)
        for k, v in tp.items():
            m[f"{k}_p"] = v
        for k, v in ts.items():
            m[f"{k}_s"] = v
        m.update(consts)
        m.update(wm)
        in_maps.append(m)
    res = run_bass_kernel_spmd(nc, in_maps, core_ids=list(range(n_cores)))
    return res.results


FULL_CFG = dict(SP=16384, NP=128, SS=4096, NS=64, NSEQ=2)


def kernel(x_prompt, x_sample, g_mix, w_in, g_q, w_uq, g_kv, w_ukv, w_attn_out, w_fourier_out, w_out,
           g_ffn, w_up, conv_w, conv_b, w_down, g_final):
    cfg = FULL_CFG
    x_prompt = np.asarray(x_prompt, np.float32)
    x_sample = np.asarray(x_sample, np.float32)
    wm = weight_maps(*[np.asarray(a, np.float32) for a in (g_mix, w_in, g_q, w_uq, g_kv, w_ukv, w_attn_out,
                                                            w_fourier_out, w_out, g_ffn, w_up, conv_w, conv_b,
                                                            w_down, g_final)])
    results = run(cfg, x_prompt, x_sample, wm)
    OWNP = cfg["SP"] // 4
    yp = np.zeros_like(x_prompt)
    ysm = np.zeros_like(x_sample)
    for c in range(8):
        b, j = c // 4, c % 4
        yp[b, OWNP * j:OWNP * (j + 1)] = results[c]["yp"]
        ysm[cfg["NSEQ"] * c:cfg["NSEQ"] * (c + 1)] = results[c]["ys"]
    return (yp, ysm)
```

```python
import numpy as np
import ml_dtypes
from contextlib import ExitStack
import concourse.bass as bass
import concourse.mybir as mybir
from concourse.bass_utils import run_bass_kernel_spmd

F32 = mybir.dt.float32
BF16 = mybir.dt.bfloat16
AF = mybir.ActivationFunctionType
ALU = mybir.AluOpType

D = 1024
H = 8
QR = 256
KVR = 128
NOPE = 64
ROPE = 32
VD = 64
QK = NOPE + ROPE
FG = 4
DFF = 2816
NFC = 2 * DFF // 128
NHC = DFF // 128
EPS = 1e-6
IN_COLS = 2976
C_CQ, C_CKV, C_KR, C_KRR, C_XF, C_G, WI_COLS = 0, 256, 384, 416, 448, 960, 3008
FB = 510


from functools import partial

TRAMP = [
    lambda f: f(),
    lambda f: f(),
    lambda f: f(),
    lambda f: f(),
    lambda f: f(),
    lambda f: f(),
    lambda f: f(),
    lambda f: f(),
    lambda f: f(),
    lambda f: f(),
    lambda f: f(),
    lambda f: f(),
    lambda f: f(),
    lambda f: f(),
    lambda f: f(),
    lambda f: f(),
    lambda f: f(),
    lambda f: f(),
    lambda f: f(),
    lambda f: f(),
]


class Res:
    __slots__ = ("w", "r")

    def __init__(self):
        self.w = {}
        self.r = {}


class Eng:
    def __init__(self, nc, es, name, eng, sem=True):
        self.e = eng
        self.key = name
        self.sem = es.enter_context(nc.semaphore(name)) if sem else None
        self.cnt = 0
        self.seen = {}
        self.dsem = []
        self.di = 0


class DSem:
    def __init__(self, nc, es, name):
        self.key = name
        self.sem = es.enter_context(nc.semaphore(name))
        self.cnt = 0


class KB:
    def __init__(self, nc, es, ndma=20):
        self.nc = nc
        self.phase = 0
        self.PE = Eng(nc, es, "sPE", nc.tensor)
        self.ACT = Eng(nc, es, "sACT", nc.scalar)
        self.DVE = Eng(nc, es, "sDVE", nc.vector)
        self.POOL = Eng(nc, es, "sPOOL", nc.gpsimd)
        self.SP = Eng(nc, es, "sSP", nc.sync, sem=False)
        self.engs = [self.PE, self.ACT, self.DVE, self.POOL, self.SP]
        for q in (self.SP, self.POOL):
            q.dsem = [DSem(nc, es, f"d{q.key}{i}") for i in range(ndma)]

    def _wait(self, E, deps):
        for key, (sem, val) in deps.items():
            if key == E.key and val > E.cnt:
                continue
            if E.seen.get(key, 0) < val:
                E.e.wait_ge(sem, val)
                E.seen[key] = val

    @staticmethod
    def _gather(reads, writes, accum):
        deps = {}

        def add(d):
            for k, ev in d.items():
                if k not in deps or deps[k][1] < ev[1]:
                    deps[k] = ev
        for r in reads:
            add(r.w)
        for w in writes:
            if not accum:
                add(w.w)
            add(w.r)
        return deps

    def op(self, E, fn, reads=(), writes=(), signal=True):
        deps = self._gather(reads, writes, False)
        self._wait(E, deps)
        ins = TRAMP[self.phase](fn)
        val = E.cnt + 1
        if signal:
            ins.then_inc(E.sem, 1)
            E.cnt = val
        ev = (E.sem, val)
        for r in reads:
            if r.r.get(E.key, (None, 0))[1] < val:
                r.r[E.key] = ev
        for w in writes:
            w.w = {E.key: ev}
            w.r = {}
        return ins

    def dma(self, Q, out, in_, reads=(), writes=(), accum=False, slow=False):
        slot = Q.dsem[Q.di % len(Q.dsem)]
        Q.di += 1
        deps = self._gather(reads, writes, accum)
        if slot.cnt > 0:
            deps[slot.key] = (slot.sem, slot.cnt)
        self._wait(Q, deps)
        ins = TRAMP[self.phase](partial(Q.e.dma_start, out=out, in_=in_, allow_slow_non_contiguous=True) if slow
                                else partial(Q.e.dma_start, out=out, in_=in_))
        slot.cnt += 16
        ins.then_inc(slot.sem, 16)
        ev = (slot.sem, slot.cnt)
        for r in reads:
            r.r[slot.key] = ev
        for w in writes:
            if accum:
                w.w[slot.key] = ev
            else:
                w.w = {slot.key: ev}
                w.r = {}
        return ins

    def barrier(self):
        tg = {}
        for E in self.engs:
            if E.sem is not None and E.cnt > 0:
                tg[E.key] = (E.sem, E.cnt)
            for s in E.dsem:
                if s.cnt > 0:
                    tg[s.key] = (s.sem, s.cnt)
        for E in self.engs:
            self._wait(E, tg)

    def mm(self, out, lhsT, rhs, start, stop, reads=(), writes=(), signal=None):
        if signal is None:
            signal = stop
        return self.op(self.PE, partial(self.nc.tensor.matmul, out, lhsT=lhsT, rhs=rhs, start=start, stop=stop),
                       reads, writes, signal)

    def tr(self, out, in_, ident, reads=(), writes=(), signal=True):
        return self.op(self.PE, partial(self.nc.tensor.transpose, out, in_, ident), reads, writes, signal)

    def act(self, out, in_, func, reads=(), writes=(), **kw):
        return self.op(self.ACT, partial(self.nc.scalar.activation, out=out, in_=in_, func=func, **kw), reads, writes)

    def tt(self, E, out, in0, in1, op, reads=(), writes=()):
        return self.op(E, partial(E.e.tensor_tensor, out=out, in0=in0, in1=in1, op=op), reads, writes)

    def ts(self, E, out, in0, s1, s2, op0, op1=None, reads=(), writes=()):
        if op1 is None:
            return self.op(E, partial(E.e.tensor_scalar, out=out, in0=in0, scalar1=s1, scalar2=None, op0=op0),
                           reads, writes)
        return self.op(E, partial(E.e.tensor_scalar, out=out, in0=in0, scalar1=s1, scalar2=s2, op0=op0, op1=op1),
                       reads, writes)

    def stt(self, E, out, in0, scalar, in1, op0, op1, reads=(), writes=()):
        return self.op(E, partial(E.e.scalar_tensor_tensor, out=out, in0=in0, scalar=scalar, in1=in1, op0=op0, op1=op1),
                       reads, writes)

    def cp(self, E, out, in_, reads=(), writes=()):
        if E is self.ACT:
            return self.op(E, partial(E.e.activation, out=out, in_=in_, func=AF.Copy), reads, writes)
        return self.op(E, partial(E.e.tensor_copy, out=out, in_=in_), reads, writes)

    def memset(self, E, ap, v, writes=()):
        return self.op(E, partial(E.e.memset, ap, v), (), writes)

    def recip(self, out, in_, reads=(), writes=()):
        return self.op(self.DVE, partial(self.nc.vector.reciprocal, out=out, in_=in_), reads, writes)


class T:
    def __init__(self, t):
        self.t = t
        self.r = Res()


def build(cfg):
    SP_, NP_, SS_, NS_, NSEQ = cfg["SP"], cfg["NP"], cfg["SS"], cfg["NS"], cfg["NSEQ"]
    OWNP = SP_ // 4
    debug = cfg.get("debug", False)
    phases = cfg.get("phases", "AFQTMN")
    nc = bass.Bass("TRN2", target_bir_lowering=False)

    def din(name, shape, dt=F32):
        return nc.dram_tensor(name, list(shape), dt, kind="ExternalInput").ap()

    def dscr(name, shape, dt=BF16):
        return nc.dram_tensor(name, list(shape), dt, kind=("ExternalOutput" if debug else "Internal")).ap()

    xp_full = din("xp_full", [SP_, D])
    xp_own = din("xp_own", [OWNP + 2, D])
    xs = din("xs", [NSEQ, SS_, D])
    tabs = {}
    for tag, S, N, nq in (("p", SP_, NP_, OWNP + 2), ("s", SS_, NS_, SS_)):
        tabs[tag] = dict(
            kcs=din(f"kcs_{tag}", [64, S]), qcos=din(f"qcos_{tag}", [QK, nq]), qsin=din(f"qsin_{tag}", [QK, nq]),
            ecs=din(f"ecs_{tag}", [N, 2, nq], BF16),
            ca=din(f"ca_{tag}", [N, 2 * N], BF16), cb=din(f"cb_{tag}", [N, 2 * N], BF16))
    cs128_d = din("cs128", [128, 256], BF16)
    ident_d = din("ident", [128, 128], BF16)
    eplace_d = din("eplace", [64, QK], BF16)
    mask_d = din("mask", [128, 2])
    w_in_d = din("w_in", [D, IN_COLS])
    w_uq_d = din("w_uq", [QR, H * QK])
    w_ukv_d = din("w_ukv", [KVR, H * 128])
    w_ao_d = din("w_ao", [512, D])
    w_fo_d = din("w_fo", [512, D])
    w_out_d = din("w_out", [D, D])
    w_up_d = din("w_up", [D, 2 * DFF])
    w_dn_d = din("w_dn", [DFF, D])
    g_mix_d = din("g_mix8", [128, 8])
    g_q_d = din("g_q2", [128, 2])
    g_kv_d = din("g_kv1", [128, 1])
    g_ffn_d = din("g_ffn8", [128, 8])
    convw_d = din("convw", [128, NFC, 3])
    convb_d = din("convb", [128, NFC])
    gfin_d = din("gfin", [128, D])

    yp = nc.dram_tensor("yp", [OWNP, D], F32, kind="ExternalOutput").ap()
    ys = nc.dram_tensor("ys", [NSEQ, SS_, D], F32, kind="ExternalOutput").ap()

    Wi_s = dscr("Wi_s", [8, 128, WI_COLS])
    Wuq_s = dscr("Wuq_s", [2, 128, 2 * H * QK])
    Wkv_s = dscr("Wkv_s", [128, H * QK + H * VD])
    Wao_s = dscr("Wao_s", [4, 128, D])
    Wfo_s = dscr("Wfo_s", [4, 128, D])
    Wo_s = dscr("Wo_s", [8, 128, D])
    Wup_s = dscr("Wup_s", [8, 128, 2 * DFF])
    Wdn_s = dscr("Wdn_s", [NHC, 128, D])
    SMAX = max(SP_, SS_)
    NQMAX = max(OWNP + 2, SS_ + 2)
    KT_s = dscr("KT_s", [H, QK, SMAX])
    V_s = dscr("V_s", [H, 128, SMAX // 128, VD + 1])
    Y_s = dscr("Y_s", [FG, 2, 128 * SMAX])
    FT_s = dscr("FT_s", [FG, 128, NQMAX])
    QT_s = dscr("QT_s", [H, QK, NQMAX])
    AT_s = dscr("AT_s", [H * VD, NQMAX])
    X1_s = dscr("X1_s", [NQMAX, D], F32)
    H2T_s = dscr("H2T_s", [8, 128, NQMAX])

    es_top = ExitStack()
    with es_top:
        kb = KB(nc, es_top)
        PE, ACT, DVE, POOL, SP = kb.PE, kb.ACT, kb.DVE, kb.POOL, kb.SP

        uid = [0]

        def sb(es, name, shape, dt):
            uid[0] += 1
            return T(es.enter_context(nc.sbuf_tensor(f"sb{uid[0]}_{name}", list(shape), dt)))

        def ps(es, name, shape, dt):
            uid[0] += 1
            return T(es.enter_context(nc.psum_tensor(f"ps{uid[0]}_{name}", list(shape), dt)))

        ident = sb(es_top, "ident", [128, 128], BF16)
        eplace = sb(es_top, "eplace", [64, QK], BF16)
        cs128 = sb(es_top, "cs128", [128, 256], BF16)
        onesf = sb(es_top, "onesf", [128, 128], F32)
        epsq = sb(es_top, "epsq", [128, 1], F32)
        mask = sb(es_top, "mask", [128, 2], F32)
        zeros = sb(es_top, "zeros", [128, 16], BF16)
        kb.dma(SP, ident.t[:], ident_d[:, :], writes=[ident.r])
        kb.dma(SP, eplace.t[:], eplace_d[:, :], writes=[eplace.r])
        kb.dma(SP, cs128.t[:], cs128_d[:, :], writes=[cs128.r])
        kb.dma(SP, mask.t[:], mask_d[:, :], writes=[mask.r])
        kb.memset(DVE, onesf.t[:], 1.0, writes=[onesf.r])
        kb.memset(DVE, epsq.t[:], EPS, writes=[epsq.r])
        kb.memset(DVE, zeros.t[:], 0.0, writes=[zeros.r])
        PSALL = es_top.enter_context(nc.psum_tensor("psall", [128, 8, 512], F32))
        pb = [T(PSALL[:, i, :]) for i in range(8)]
        pT = T(PSALL[:, 7, :].bitcast(BF16))
        pT.r = pb[7].r
        kb.barrier()

        if "W" in phases or True:
            with ExitStack() as es:
                stg = [sb(es, f"stg{i}", [128, 2 * DFF], F32) for i in range(2)]
                wo = [sb(es, f"wo{i}", [128, 2 * DFF], BF16) for i in range(2)]
                gm = sb(es, "gm", [128, 8], F32)
                gq = sb(es, "gq", [128, 2], F32)
                gkv = sb(es, "gkv", [128, 1], F32)
                gf = sb(es, "gf", [128, 8], F32)
                for t_, d_ in ((gm, g_mix_d), (gq, g_q_d), (gkv, g_kv_d), (gf, g_ffn_d)):
                    kb.dma(SP, t_.t[:], d_[:, :], writes=[t_.r])
                cnt = [0]

                def conv(src, ncols, dst, fn):
                    i = cnt[0] % 2
                    cnt[0] += 1
                    kb.dma(SP, stg[i].t[:, 0:ncols], src, writes=[stg[i].r])
                    fn(stg[i], wo[i])
                    kb.dma(POOL, dst, wo[i].t[:, 0:dst.shape[-1]], reads=[wo[i].r])

                def scaled(E, o, i_, sc, rd, wr, neg=False):
                    if sc is None:
                        kb.cp(E, o, i_, reads=rd, writes=wr)
                    elif neg:
                        kb.ts(E, o, i_, sc, -1.0, ALU.mult, ALU.mult, reads=rd, writes=wr)
                    else:
                        kb.ts(E, o, i_, sc, None, ALU.mult, reads=rd, writes=wr)

                for dc in range(8):
                    def f_in(s_, o_, dc=dc):
                        sc = gm.t[:, dc:dc + 1]
                        rd, wr = [s_.r, gm.r], [o_.r]
                        scaled(DVE, o_.t[:, 0:416], s_.t[:, 0:416], sc, rd, wr)
                        scaled(DVE, o_.t[:, C_KRR:C_KRR + 16], s_.t[:, 400:416], sc, rd, wr, neg=True)
                        scaled(DVE, o_.t[:, C_KRR + 16:C_KRR + 32], s_.t[:, 384:400], sc, rd, wr)
                        scaled(DVE, o_.t[:, C_XF:WI_COLS], s_.t[:, 416:IN_COLS], sc, rd, wr)
                    conv(w_in_d[dc * 128:(dc + 1) * 128, :], IN_COLS, Wi_s[dc], f_in)
                for kc in range(2):
                    def f_uq(s_, o_, kc=kc):
                        sc = gq.t[:, kc:kc + 1]
                        rd, wr = [s_.r, gq.r], [o_.r]
                        n = H * QK
                        scaled(DVE, o_.t[:, 0:n], s_.t[:, 0:n], sc, rd, wr)
                        kb.memset(DVE, o_.t[:, n:2 * n], 0.0, writes=wr)
                        s3 = s_.t[:, 0:n].rearrange("p (h c) -> p h c", c=QK)
                        o3 = o_.t[:, n:2 * n].rearrange("p (h c) -> p h c", c=QK)
                        scaled(DVE, o3[:, :, 64:80], s3[:, :, 80:96], sc, rd, wr, neg=True)
                        scaled(DVE, o3[:, :, 80:96], s3[:, :, 64:80], sc, rd, wr)
                    conv(w_uq_d[kc * 128:(kc + 1) * 128, :], H * QK, Wuq_s[kc], f_uq)

                def f_kv(s_, o_):
                    sc = gkv.t[:, 0:1]
                    rd, wr = [s_.r, gkv.r], [o_.r]
                    kb.memset(DVE, o_.t[:, 0:H * QK], 0.0, writes=wr)
                    s3 = s_.t[:, 0:H * 128].rearrange("p (h c) -> p h c", c=128)
                    ok = o_.t[:, 0:H * QK].rearrange("p (h c) -> p h c", c=QK)
                    ov = o_.t[:, H * QK:H * QK + H * VD].rearrange("p (h c) -> p h c", c=VD)
                    scaled(DVE, ok[:, :, 0:64], s3[:, :, 0:64], sc, rd, wr)
                    scaled(DVE, ov, s3[:, :, 64:128], sc, rd, wr)
                conv(w_ukv_d[:, :], H * 128, Wkv_s[:, :], f_kv)

                def plain(sc_t=None, k=None):
                    def f(s_, o_):
                        n = o_.cur
                        if sc_t is None:
                            kb.cp(DVE, o_.t[:, 0:n], s_.t[:, 0:n], reads=[s_.r], writes=[o_.r])
                        else:
                            scaled(DVE, o_.t[:, 0:n], s_.t[:, 0:n], sc_t.t[:, k:k + 1], [s_.r, sc_t.r], [o_.r])
                    return f
                for src, dst, nch, ncols, sct in ((w_ao_d, Wao_s, 4, D, None), (w_fo_d, Wfo_s, 4, D, None),
                                                   (w_out_d, Wo_s, 8, D, None), (w_up_d, Wup_s, 8, 2 * DFF, gf),
                                                   (w_dn_d, Wdn_s, NHC, D, None)):
                    for c in range(nch):
                        for w_ in wo:
                            w_.cur = ncols
                        conv(src[c * 128:(c + 1) * 128, :], ncols, dst[c], plain(sct, c))
                kb.barrier()

        def norm_transpose(es_tiles, xt, tw, hT, col0, part="both"):
            junk, ss, rstd, hb = es_tiles
            if part in ("both", "norm"):
                kb.act(junk.t[0:tw, :], xt.t[0:tw, :], AF.Square, reads=[xt.r], writes=[junk.r, ss.r],
                       accum_out=ss.t[0:tw, :])
                kb.act(ss.t[0:tw, :], ss.t[0:tw, :], AF.Sqrt, reads=[ss.r, epsq.r], writes=[ss.r],
                       bias=epsq.t[0:tw, 0:1], scale=1.0 / D)
                kb.recip(rstd.t[0:tw, :], ss.t[0:tw, :], reads=[ss.r], writes=[rstd.r])
                kb.ts(DVE, hb.t[0:tw, :], xt.t[0:tw, :], rstd.t[0:tw, 0:1], None, ALU.mult,
                      reads=[xt.r, rstd.r], writes=[hb.r])
            if part == "norm":
                return rstd
            for dc in range(8):
                kb.tr(pT.t[:, dc * 128:dc * 128 + tw], hb.t[0:tw, dc * 128:(dc + 1) * 128], ident.t[0:tw, 0:tw],
                      reads=[hb.r, ident.r], writes=[pT.r], signal=(dc == 7))
            kb.cp(DVE, hT.t[:, :, col0:col0 + tw],
                  pT.t[:, :].rearrange("p (c t) -> p c t", t=128)[:, :, 0:tw],
                  reads=[pT.r], writes=[hT.r])
            return rstd

        def mk_norm_tiles(es, tag, junk=None):
            return (junk if junk is not None else sb(es, f"junk{tag}", [128, D], BF16), sb(es, f"ss{tag}", [128, 1], F32),
                    sb(es, f"rstd{tag}", [128, 1], F32), sb(es, f"hb{tag}", [128, D], BF16))

        def fm_rstd(es_t, src_list, W, nfeat):
            sq, pbc, rs, rbc = es_t
            for i, (p_, npart) in enumerate(src_list):
                kb.act(sq.t[0:npart, i, 0:W], p_.t[0:npart, 0:W], AF.Square, reads=[p_.r], writes=[sq.r])
            for i, (p_, npart) in enumerate(src_list):
                kb.mm(pbc.t[:, 0:W], onesf.t[0:npart, :], sq.t[0:npart, i, 0:W], i == 0, i == len(src_list) - 1,
                      reads=[sq.r, onesf.r], writes=[pbc.r])
            kb.act(rs.t[:, 0:W], pbc.t[:, 0:W], AF.Sqrt, reads=[pbc.r, epsq.r], writes=[rs.r],
                   bias=epsq.t[:, 0:1], scale=1.0 / nfeat)
            kb.recip(rbc.t[:, 0:W], rs.t[:, 0:W], reads=[rs.r], writes=[rbc.r])
            return rbc

        def run_sequence(seq_idx, tag, S, N, x_full, x_own, n_own, halo, y_out):
            tb = tabs[tag]
            nq = n_own + (2 if halo else 0)
            nblk_a = S // 512
            R = 128 // N
            qblocks = [(b * 512, 512) for b in range(n_own // 512)]
            if halo:
                qblocks.append((n_own, 2))
            xv = x_full.rearrange("(n1 n2) d -> n2 n1 d", n2=N)
            Yv = [[Y_s[g, ri, 0:N * N * 128].rearrange("(n1 n2 c) -> n1 n2 c", n2=N, c=128) for ri in range(2)]
                  for g in range(FG)]

            if "A" in phases:
                kb.phase = 1 + 6 * seq_idx + 0
                with ExitStack() as es:
                    Wi = sb(es, "Wi_a", [128, 8, 704], BF16)
                    for dc in range(8):
                        kb.dma(SP, Wi.t[:, dc, :], Wi_s[dc, :, C_CKV:C_G], writes=[Wi.r], accum=True)
                    Wkv = sb(es, "Wkv_a", [128, H * QK + H * VD], BF16)
                    kb.dma(SP, Wkv.t[:], Wkv_s[:, :], writes=[Wkv.r])
                    xt = [sb(es, f"xa{i}", [128, D], F32) for i in range(5)]
                    nt = [mk_norm_tiles(es, "a")]
                    for k_ in range(1, 4):
                        nt.append((nt[0][0], sb(es, f"ssa{k_}", [128, 1], F32), sb(es, f"rstda{k_}", [128, 1], F32),
                                   sb(es, f"hba{k_}", [128, D], BF16)))
                    hT = [sb(es, f"hTa{i}", [128, 8, 512], BF16) for i in range(2)]
                    fr = (sb(es, "sq_a", [128, 1, 512], F32), pb[1], sb(es, "rs_a", [128, 512], F32),
                          sb(es, "rbc_a", [128, 512], F32))
                    ckvn = sb(es, "ckvn", [128, 512], BF16)
                    kcs = [sb(es, f"kcs{i}", [64, 512], F32) for i in range(2)]
                    kr2 = sb(es, "kr2", [64, 512], BF16)
                    KTb = [sb(es, f"KTb{i}", [QK, H, 512], BF16) for i in range(2)]
                    Vb = [sb(es, f"Vb{i}", [128, H, 4, VD + 1], BF16) for i in range(2)]
                    for v_ in Vb:
                        kb.memset(DVE, v_.t[:], 1.0, writes=[v_.r])
                    xfT = [sb(es, f"xfT{i}", [128, 512], BF16) for i in range(2)]
                    Yb = [sb(es, f"Yb{i}", [128, FG, 2, 4, 128], BF16) for i in range(2)]
                    tia = [0]

                    xa_of = {}

                    def prep_a(blk, part):
                        if blk >= nblk_a:
                            return
                        hTb = hT[blk % 2]
                        if part == "norm":
                            xa_of[blk] = []
                        for i in range(4):
                            if part == "norm":
                                x_ = xt[tia[0] % 5]
                                tia[0] += 1
                                xa_of[blk].append(x_)
                                tile_idx = blk * 4 + i
                                for q in range(R):
                                    kb.dma(SP, x_.t[q * N:(q + 1) * N, :], xv[tile_idx * R + q], writes=[x_.r], accum=(q > 0))
                            norm_transpose(nt[i], xa_of[blk][i], 128, hTb, i * 128, part=part)
                    prep_a(0, "norm")
                    prep_a(0, "tr")
                    for blk in range(nblk_a):
                        hTb = hT[blk % 2]
                        prep_a(blk + 1, "norm")
                        kc_ = kcs[blk % 2]
                        kb.dma(SP, kc_.t[:], tb["kcs"][:, blk * 512:(blk + 1) * 512], writes=[kc_.r])
                        for dc in range(8):
                            kb.mm(pb[0].t[:, :], Wi.t[:, dc, 0:128], hTb.t[:, dc, :], dc == 0, dc == 7,
                                  reads=[Wi.r, hTb.r], writes=[pb[0].r])
                        rbc = fm_rstd(fr, [(pb[0], 128)], 512, KVR)
                        kb.tt(DVE, ckvn.t[:], pb[0].t[:, :], rbc.t[:], ALU.mult, reads=[pb[0].r, rbc.r], writes=[ckvn.r])
                        for dc in range(8):
                            kb.mm(pb[2].t[0:64, :], Wi.t[:, dc, 128:192], hTb.t[:, dc, :], dc == 0, dc == 7,
                                  reads=[Wi.r, hTb.r], writes=[pb[2].r])
                        kb.tt(DVE, kr2.t[:], pb[2].t[0:64, :], kc_.t[:], ALU.mult, reads=[pb[2].r, kc_.r], writes=[kr2.r])
                        Y_ = Yb[blk % 2]
                        for g in range(FG):
                            p_ = pb[3 + g % 2]
                            for dc in range(8):
                                kb.mm(p_.t[:, :], Wi.t[:, dc, 192 + g * 128:192 + (g + 1) * 128], hTb.t[:, dc, :],
                                      dc == 0, dc == 7, reads=[Wi.r, hTb.r], writes=[p_.r])
                            xf_ = xfT[g % 2]
                            kb.cp(ACT, xf_.t[:], p_.t[:, :], reads=[p_.r], writes=[xf_.r])
                            for i2 in range(2):
                                p2 = pb[5 + i2]
                                for i3 in range(2):
                                    i = i2 * 2 + i3
                                    kb.mm(p2.t[:, i3 * 256:(i3 + 1) * 256], xf_.t[:, i * 128:(i + 1) * 128], cs128.t[:],
                                          True, True, reads=[xf_.r, cs128.r], writes=[p2.r], signal=(i3 == 1))
                                kb.cp(DVE, Y_.t[:, g, :, i2 * 2:i2 * 2 + 2, :],
                                      p2.t[:, :].rearrange("p (i r c) -> p r i c", i=2, r=2),
                                      reads=[p2.r], writes=[Y_.r])
                        for g in range(FG):
                            for ri in range(2):
                                for q in range(R):
                                    n2s = slice(blk * 4 * R + q, blk * 4 * R + q + 3 * R + 1, R) if R > 1 else \
                                        slice(blk * 4, blk * 4 + 4)
                                    kb.dma(POOL, Yv[g][ri][:, n2s, :], Y_.t[q * N:(q + 1) * N, g, ri, :, :],
                                           reads=[Y_.r])
                        KT_ = KTb[blk % 2]
                        for h in range(H):
                            p_ = pb[3 + h % 2]
                            kb.mm(p_.t[0:QK, :], Wkv.t[:, h * QK:(h + 1) * QK], ckvn.t[:], True, False,
                                  reads=[Wkv.r, ckvn.r], writes=[p_.r])
                            kb.mm(p_.t[0:QK, :], eplace.t[:, :], kr2.t[:], False, True,
                                  reads=[eplace.r, kr2.r], writes=[p_.r])
                            kb.cp(ACT if h % 2 else DVE, KT_.t[:, h, :], p_.t[0:QK, :], reads=[p_.r], writes=[KT_.r])
                        kb.dma(POOL, KT_s[:, :, blk * 512:(blk + 1) * 512].rearrange("h c s -> c h s"), KT_.t[:],
                               reads=[KT_.r])
                        V_ = Vb[blk % 2]
                        for i in range(4):
                            p_ = pb[5 + i % 2]
                            kb.mm(p_.t[:, :], ckvn.t[:, i * 128:(i + 1) * 128], Wkv.t[:, H * QK:H * QK + H * VD],
                                  True, True, reads=[ckvn.r, Wkv.r], writes=[p_.r])
                            kb.cp(ACT if i % 2 else DVE, V_.t[:, :, i, 0:VD],
                                  p_.t[:, :].rearrange("p (h c) -> p h c", c=VD), reads=[p_.r], writes=[V_.r])
                        kb.dma(POOL, V_s[:, :, blk * 4:(blk + 1) * 4, :].rearrange("h p c e -> p h c e"), V_.t[:],
                               reads=[V_.r])
                        prep_a(blk + 1, "tr")
                    kb.barrier()

            if "F" in phases:
                kb.phase = 1 + 6 * seq_idx + 1
                with ExitStack() as es:
                    Yt = sb(es, "Yt", [128, 2, N * 128], BF16)
                    At = sb(es, "At", [128, 2, 128 * N], BF16)
                    ca = sb(es, "ca", [128, 2 * N], BF16)
                    cbm = sb(es, "cbm", [128, 2 * N], BF16)
                    ecs = sb(es, "ecs", [128, 2, nq], BF16)
                    FTb = [sb(es, f"FTb{i}", [128, nq], BF16) for i in range(2)]
                    kb.dma(SP, ca.t[0:N, :], tb["ca"][:, :], writes=[ca.r])
                    kb.dma(SP, cbm.t[0:N, :], tb["cb"][:, :], writes=[cbm.r])
                    kb.dma(SP, ecs.t[0:N, :, :], tb["ecs"][:, :, :], writes=[ecs.r])
                    K2L = n_own // N
                    cpb = 512 // (2 * N)
                    kpb = 512 // K2L
                    fscale = 1.0 / float(np.sqrt(S * 128.0))
                    A4 = At.t[:, :, :].rearrange("p r (c k) -> p r c k", k=N)
                    Y4 = Yt.t[:, :, :].rearrange("p r (n c) -> p r n c", c=128)
                    for g in range(FG):
                        for ri in range(2):
                            kb.dma(SP, Yt.t[0:N, ri, :], Y_s[g, ri, 0:N * N * 128].rearrange("(n r) -> n r", n=N),
                                   writes=[Yt.r], accum=(ri > 0))
                        for cg in range(128 // cpb):
                            p_ = pb[cg % 3]
                            for ci in range(cpb):
                                c_ = cg * cpb + ci
                                o_ = p_.t[0:N, ci * 2 * N:(ci + 1) * 2 * N]
                                kb.mm(o_, Y4[0:N, 0, :, c_], ca.t[0:N, :], True, False, reads=[Yt.r, ca.r], writes=[p_.r])
                                kb.mm(o_, Y4[0:N, 1, :, c_], cbm.t[0:N, :], False, True, reads=[Yt.r, cbm.r], writes=[p_.r],
                                      signal=(ci == cpb - 1))
                            kb.cp(ACT if cg % 2 else DVE, A4[0:N, :, cg * cpb:(cg + 1) * cpb, :],
                                  p_.t[0:N, :].rearrange("p (c r k) -> p r c k", c=cpb, r=2),
                                  reads=[p_.r], writes=[At.r])
                        F_ = FTb[g % 2]
                        Fv = F_.t[:, 0:n_own].rearrange("p (k2 k1) -> p k1 k2", k1=N)
                        for kg in range(N // kpb):
                            p_ = pb[3 + kg % 3]
                            for ki in range(kpb):
                                k1 = kg * kpb + ki
                                o_ = p_.t[:, ki * K2L:(ki + 1) * K2L]
                                kb.mm(o_, A4[0:N, 0, :, k1], ecs.t[0:N, 0, k1:n_own:N], True, False,
                                      reads=[At.r, ecs.r], writes=[p_.r])
                                kb.mm(o_, A4[0:N, 1, :, k1], ecs.t[0:N, 1, k1:n_own:N], False, True,
                                      reads=[At.r, ecs.r], writes=[p_.r], signal=(ki == kpb - 1))
                            kb.act(Fv[:, kg * kpb:(kg + 1) * kpb, :], p_.t[:, :].rearrange("p (k c) -> p k c", c=K2L),
                                   AF.Copy, reads=[p_.r], writes=[F_.r], scale=fscale)
                        if halo:
                            p_ = pb[6]
                            for hi, k1 in enumerate((N - 1, 0)):
                                o_ = p_.t[:, hi:hi + 1]
                                kb.mm(o_, A4[0:N, 0, :, k1], ecs.t[0:N, 0, n_own + hi:n_own + hi + 1], True, False,
                                      reads=[At.r, ecs.r], writes=[p_.r])
                                kb.mm(o_, A4[0:N, 1, :, k1], ecs.t[0:N, 1, n_own + hi:n_own + hi + 1], False, True,
                                      reads=[At.r, ecs.r], writes=[p_.r], signal=(hi == 1))
                            kb.act(F_.t[:, n_own:n_own + 2], p_.t[:, 0:2], AF.Copy, reads=[p_.r], writes=[F_.r],
                                   scale=fscale)
                        kb.dma(POOL, FT_s[g, :, 0:nq], F_.t[:, 0:nq], reads=[F_.r])
                    kb.barrier()

            if "Q" in phases:
                kb.phase = 1 + 6 * seq_idx + 2
                with ExitStack() as es:
                    Wi = sb(es, "Wi_q", [128, 8, 256], BF16)
                    for dc in range(8):
                        kb.dma(SP, Wi.t[:, dc, :], Wi_s[dc, :, 0:256], writes=[Wi.r], accum=True)
                    Wuq = sb(es, "Wuq", [128, 2, 2 * H * QK], BF16)
                    for kc in range(2):
                        kb.dma(SP, Wuq.t[:, kc, :], Wuq_s[kc], writes=[Wuq.r], accum=True)
                    xt = [sb(es, f"xq{i}", [128, D], F32) for i in range(5)]
                    nt = [mk_norm_tiles(es, "q")]
                    for k_ in range(1, 4):
                        nt.append((nt[0][0], sb(es, f"ssq{k_}", [128, 1], F32), sb(es, f"rstdq{k_}", [128, 1], F32),
                                   sb(es, f"hbq{k_}", [128, D], BF16)))
                    hT = [sb(es, f"hTq{i}", [128, 8, 512], BF16) for i in range(2)]
                    fr = (sb(es, "sq_q", [128, 2, 512], F32), pb[2], sb(es, "rs_q", [128, 512], F32),
                          sb(es, "rbc_q", [128, 512], F32))
                    cqn = sb(es, "cqn", [128, 2, 512], BF16)
                    qc = [sb(es, f"qc{i}", [QK, 512], F32) for i in range(2)]
                    qs = [sb(es, f"qs{i}", [QK, 512], F32) for i in range(2)]
                    t1 = [sb(es, f"t1{i}", [QK, 512], F32) for i in range(3)]
                    t2 = [sb(es, f"t2{i}", [QK, 512], F32) for i in range(3)]
                    Qb = [sb(es, f"Qb{i}", [QK, H, 512], BF16) for i in range(2)]
                    tiq = [0]

                    xq_of = {}

                    def prep_q(bi, part):
                        if bi >= len(qblocks):
                            return
                        c0, W = qblocks[bi]
                        hTb = hT[bi % 2]
                        ntile = (W + 127) // 128
                        if part == "norm":
                            xq_of[bi] = []
                        for i in range(ntile):
                            tw = min(128, W - i * 128)
                            if part == "norm":
                                x_ = xt[tiq[0] % 5]
                                tiq[0] += 1
                                xq_of[bi].append(x_)
                                kb.dma(SP, x_.t[0:tw, :], x_own[c0 + i * 128:c0 + i * 128 + tw, :], writes=[x_.r])
                            norm_transpose(nt[i], xq_of[bi][i], tw, hTb, i * 128, part=part)
                    prep_q(0, "norm")
                    prep_q(0, "tr")
                    for bi, (c0, W) in enumerate(qblocks):
                        hTb = hT[bi % 2]
                        prep_q(bi + 1, "norm")
                        qc_, qs_ = qc[bi % 2], qs[bi % 2]
                        kb.dma(SP, qc_.t[:, 0:W], tb["qcos"][:, c0:c0 + W], writes=[qc_.r])
                        kb.dma(SP, qs_.t[:, 0:W], tb["qsin"][:, c0:c0 + W], writes=[qs_.r])
                        for kc in range(2):
                            for dc in range(8):
                                kb.mm(pb[kc].t[:, 0:W], Wi.t[:, dc, kc * 128:(kc + 1) * 128], hTb.t[:, dc, 0:W],
                                      dc == 0, dc == 7, reads=[Wi.r, hTb.r], writes=[pb[kc].r])
                        rbc = fm_rstd(fr, [(pb[0], 128), (pb[1], 128)], W, QR)
                        for kc in range(2):
                            kb.tt(DVE, cqn.t[:, kc, 0:W], pb[kc].t[:, 0:W], rbc.t[:, 0:W], ALU.mult,
                                  reads=[pb[kc].r, rbc.r], writes=[cqn.r])
                        Q_ = Qb[bi % 2]
                        prep_q(bi + 1, "tr")
                        for h in range(H):
                            pq, pr = ((pb[3], pb[4]), (pb[5], pb[6]), (pb[0], pb[1]))[h % 3]
                            for kc in range(2):
                                kb.mm(pq.t[0:QK, 0:W], Wuq.t[:, kc, h * QK:(h + 1) * QK], cqn.t[:, kc, 0:W],
                                      kc == 0, kc == 1, reads=[Wuq.r, cqn.r], writes=[pq.r])
                            for kc in range(2):
                                kb.mm(pr.t[0:QK, 0:W], Wuq.t[:, kc, H * QK + h * QK:H * QK + (h + 1) * QK],
                                      cqn.t[:, kc, 0:W], kc == 0, kc == 1, reads=[Wuq.r, cqn.r], writes=[pr.r])
                            a_, b_ = t1[h % 3], t2[h % 3]
                            kb.tt(DVE, a_.t[:, 0:W], pq.t[0:QK, 0:W], qc_.t[:, 0:W], ALU.mult,
                                  reads=[pq.r, qc_.r], writes=[a_.r])
                            kb.tt(DVE, b_.t[:, 0:W], pr.t[0:QK, 0:W], qs_.t[:, 0:W], ALU.mult,
                                  reads=[pr.r, qs_.r], writes=[b_.r])
                            kb.tt(POOL, Q_.t[:, h, 0:W], a_.t[:, 0:W], b_.t[:, 0:W], ALU.add,
                                  reads=[a_.r, b_.r], writes=[Q_.r])
                        kb.dma(POOL, QT_s[:, :, c0:c0 + W].rearrange("h c s -> c h s"), Q_.t[:, :, 0:W], reads=[Q_.r])
                    kb.barrier()

            if "T" in phases:
                kb.phase = 1 + 6 * seq_idx + 3
                with ExitStack() as es:
                    nch = S // 128
                    KT = [sb(es, f"KT{i}", [QK, S], BF16) for i in range(2)]
                    Vt = [sb(es, f"Vt{i}", [128, nch, VD + 1], BF16) for i in range(2)]
                    Qt = [sb(es, f"Qt{i}", [QK, 512], BF16) for i in range(2)]
                    Osb = [sb(es, f"Osb{i}", [VD + 1, 512], F32) for i in range(2)]
                    rinv = [sb(es, f"rinv{i}", [VD + 1, 512], F32) for i in range(2)]
                    Ab = [sb(es, f"Ab{i}", [VD, 512], BF16) for i in range(2)]
                    Pt2 = [sb(es, f"Pp{i}", [128, 2, 512], BF16) for i in range(3)]
                    SB = []
                    for i in range(3):
                        t_ = T(PSALL[:, 2 * i:2 * i + 2, :])
                        SB.append(t_)
                    psO = [pb[6], pb[7]]
                    npair = nch // 2
                    blocks = [(h, qi) for h in range(H) for qi in range(len(qblocks))]
                    ptasks = [(bi, j) for bi in range(len(blocks)) for j in range(npair)]
                    bst = {}
                    cur_head = [-1]
                    loaded = set()

                    def load_head(hh):
                        if hh < H and hh not in loaded:
                            loaded.add(hh)
                            kb.dma(SP, KT[hh % 2].t[:, :], KT_s[hh, :, 0:S], writes=[KT[hh % 2].r])
                            kb.dma(SP, Vt[hh % 2].t[:, :, :], V_s[hh, :, 0:nch, :], writes=[Vt[hh % 2].r])

                    qloaded = set()

                    def load_q(bi):
                        if bi < len(blocks) and bi not in qloaded:
                            qloaded.add(bi)
                            h_, qi_ = blocks[bi]
                            c0_, W_ = qblocks[qi_]
                            kb.dma(SP, Qt[bi % 2].t[:, 0:W_], QT_s[h_, :, c0_:c0_ + W_], writes=[Qt[bi % 2].r])

                    def ensure_block(bi):
                        if bi in bst:
                            return bst[bi]
                        h, qi = blocks[bi]
                        c0, W = qblocks[qi]
                        KT_, V_ = KT[h % 2], Vt[h % 2]
                        Q_ = Qt[bi % 2]
                        load_q(bi)
                        load_head(h)
                        bst[bi] = dict(h=h, c0=c0, W=W, KT=KT_, V=V_, Q=Q_, pO=psO[bi % 2])
                        return bst[bi]

                    def score_pair(gi):
                        bi, j = ptasks[gi]
                        st = ensure_block(bi)
                        p_ = SB[gi % 3]
                        W = st["W"]
                        for c in range(2):
                            kc = 2 * j + c
                            kb.mm(p_.t[:, c, 0:W], st["KT"].t[:, kc * 128:(kc + 1) * 128], st["Q"].t[:, 0:W], True, True,
                                  reads=[st["KT"].r, st["Q"].r], writes=[p_.r], signal=(c == 1))

                    ntask = len(ptasks)
                    for gi in range(min(2, ntask)):
                        score_pair(gi)
                    pend = []
                    for gi in range(ntask):
                        if gi + 2 < ntask:
                            score_pair(gi + 2)
                        while pend and pend[0][0] <= gi:
                            pend.pop(0)[1]()
                        bi, j = ptasks[gi]
                        st = bst[bi]
                        if j == 0:
                            load_q(bi + 1)
                        if blocks[bi][1] == 0 and j == min(2, npair - 1):
                            load_head(blocks[bi][0] + 1)
                        W, pO, V_ = st["W"], st["pO"], st["V"]
                        p_ = SB[gi % 3]
                        P_ = Pt2[gi % 3]
                        kb.act(P_.t[:, :, 0:W], p_.t[:, :, 0:W], AF.Exp, reads=[p_.r], writes=[P_.r])
                        for c in range(2):
                            kc = 2 * j + c
                            kb.mm(pO.t[0:VD + 1, 0:W], V_.t[:, kc, :], P_.t[:, c, 0:W], kc == 0, kc == nch - 1,
                                  reads=[V_.r, P_.r], writes=[pO.r], signal=(c == 1))
                        if j == npair - 1:
                            O_ = Osb[bi % 2]
                            r_ = rinv[bi % 2]
                            kb.cp(DVE, O_.t[:, 0:W], pO.t[0:VD + 1, 0:W], reads=[pO.r], writes=[O_.r])

                            def part2(bi=bi, st=st, O_=O_, r_=r_, W=W, pO=pO):
                                h, c0 = st["h"], st["c0"]
                                kb.mm(pO.t[0:VD, 0:W], onesf.t[VD:VD + 1, 0:VD], O_.t[VD:VD + 1, 0:W], True, True,
                                      reads=[onesf.r, O_.r], writes=[pO.r])
                                kb.recip(r_.t[0:VD, 0:W], pO.t[0:VD, 0:W], reads=[pO.r], writes=[r_.r])
                                A_ = Ab[bi % 2]
                                kb.tt(DVE, A_.t[:, 0:W], r_.t[0:VD, 0:W], O_.t[0:VD, 0:W], ALU.mult,
                                      reads=[r_.r, O_.r], writes=[A_.r])
                                kb.dma(POOL, AT_s[h * VD:(h + 1) * VD, c0:c0 + W], A_.t[:, 0:W], reads=[A_.r])
                            pend.append((gi + 2, part2))
                            del bst[bi]
                    for _, f_ in pend:
                        f_()
                    kb.barrier()

            if "M" in phases:
                kb.phase = 1 + 6 * seq_idx + 4
                with ExitStack() as es:
                    Wg = sb(es, "Wg", [128, 8, 2048], BF16)
                    for dc in range(8):
                        kb.dma(SP, Wg.t[:, dc, :], Wi_s[dc, :, C_G:WI_COLS], writes=[Wg.r], accum=True)
                    Wao = sb(es, "Wao", [128, 4, D], BF16)
                    Wfo = sb(es, "Wfo", [128, 4, D], BF16)
                    Wo = sb(es, "Wo", [128, 8, D], BF16)
                    for c in range(4):
                        kb.dma(SP, Wao.t[:, c, :], Wao_s[c], writes=[Wao.r], accum=True)
                        kb.dma(SP, Wfo.t[:, c, :], Wfo_s[c], writes=[Wfo.r], accum=True)
                    for c in range(8):
                        kb.dma(SP, Wo.t[:, c, :], Wo_s[c], writes=[Wo.r], accum=True)
                    xt = [sb(es, f"xm{i}", [128, D], F32) for i in range(8)]
                    nt = [mk_norm_tiles(es, "m")]
                    for k_ in range(1, 4):
                        nt.append((nt[0][0], sb(es, f"ssm{k_}", [128, 1], F32), sb(es, f"rstdm{k_}", [128, 1], F32),
                                   sb(es, f"hbm{k_}", [128, D], BF16)))
                    nt2 = [mk_norm_tiles(es, "m2", junk=nt[0][0])]
                    for k_ in range(1, 4):
                        nt2.append((nt2[0][0], sb(es, f"ss2{k_}", [128, 1], F32), sb(es, f"rstd2{k_}", [128, 1], F32),
                                    sb(es, f"hb2{k_}", [128, D], BF16)))
                    hT = [sb(es, f"hTm{i}", [128, 8, 512], BF16) for i in range(2)]
                    Gt = sb(es, "Gt", [128, 16, 512], BF16)
                    ATt = [sb(es, f"ATt{i}", [128, 4, 512], BF16) for i in range(1)]
                    FTt = [sb(es, f"FTt{i}", [128, 4, 512], BF16) for i in range(1)]
                    ma = [sb(es, f"ma{i}", [128, 512], F32) for i in range(2)]
                    mf = [sb(es, f"mf{i}", [128, 512], F32) for i in range(2)]
                    mg = sb(es, "mg", [128, 8, 512], BF16)
                    x1 = [sb(es, f"x1{i}", [128, D], F32) for i in range(4)]
                    h2T = [sb(es, f"h2T{i}", [128, 8, 512], BF16) for i in range(2)]
                    if not halo:
                        for dc in range(8):
                            kb.dma(POOL, H2T_s[dc, :, 0:1], zeros.t[:, 0:1], reads=[zeros.r], slow=True)
                            kb.dma(POOL, H2T_s[dc, :, n_own + 1:n_own + 2], zeros.t[:, 0:1], reads=[zeros.r], slow=True)
                    tim = [0]
                    xts_of = {}

                    def prep_m(bi, part):
                        if bi >= len(qblocks):
                            return
                        c0, W = qblocks[bi]
                        hTb = hT[bi % 2]
                        ntile = (W + 127) // 128
                        if part == "norm":
                            xts_of[bi] = []
                        for i in range(ntile):
                            tw = min(128, W - i * 128)
                            if part == "norm":
                                x_ = xt[tim[0] % 8]
                                tim[0] += 1
                                xts_of[bi].append((x_, tw))
                                kb.dma(SP, x_.t[0:tw, :], x_own[c0 + i * 128:c0 + i * 128 + tw, :], writes=[x_.r])
                            norm_transpose(nt[i], xts_of[bi][i][0], tw, hTb, i * 128, part=part)
                    prep_m(0, "norm")
                    prep_m(0, "tr")
                    for bi, (c0, W) in enumerate(qblocks):
                        is_halo = halo and bi == len(qblocks) - 1
                        hTb = hT[bi % 2]
                        prep_m(bi + 1, "norm")
                        xts = xts_of[bi]
                        AT_, FT_ = ATt[0], FTt[0]
                        for c in range(4):
                            kb.dma(SP, AT_.t[:, c, 0:W], AT_s[c * 128:(c + 1) * 128, c0:c0 + W], writes=[AT_.r], accum=(c > 0))
                            kb.dma(SP, FT_.t[:, c, 0:W], FT_s[c, :, c0:c0 + W], writes=[FT_.r], accum=(c > 0))
                        for gc in range(16):
                            p_ = pb[gc % 3]
                            for dc in range(8):
                                kb.mm(p_.t[:, 0:W], Wg.t[:, dc, gc * 128:(gc + 1) * 128], hTb.t[:, dc, 0:W],
                                      dc == 0, dc == 7, reads=[Wg.r, hTb.r], writes=[p_.r])
                            kb.act(Gt.t[:, gc, 0:W], p_.t[:, 0:W], AF.Sigmoid, reads=[p_.r], writes=[Gt.r])
                        for oc in range(8):
                            pa, pf = pb[3 + 2 * (oc % 2)], pb[4 + 2 * (oc % 2)]
                            for c in range(4):
                                kb.mm(pa.t[:, 0:W], Wao.t[:, c, oc * 128:(oc + 1) * 128], AT_.t[:, c, 0:W],
                                      c == 0, c == 3, reads=[Wao.r, AT_.r], writes=[pa.r])
                            for c in range(4):
                                kb.mm(pf.t[:, 0:W], Wfo.t[:, c, oc * 128:(oc + 1) * 128], FT_.t[:, c, 0:W],
                                      c == 0, c == 3, reads=[Wfo.r, FT_.r], writes=[pf.r])
                            a_, f_ = ma[oc % 2], mf[oc % 2]
                            kb.tt(DVE, a_.t[:, 0:W], pa.t[:, 0:W], Gt.t[:, oc, 0:W], ALU.mult,
                                  reads=[pa.r, Gt.r], writes=[a_.r])
                            kb.tt(DVE, f_.t[:, 0:W], pf.t[:, 0:W], Gt.t[:, 8 + oc, 0:W], ALU.mult,
                                  reads=[pf.r, Gt.r], writes=[f_.r])
                            kb.tt(POOL, mg.t[:, oc, 0:W], a_.t[:, 0:W], f_.t[:, 0:W], ALU.add,
                                  reads=[a_.r, f_.r], writes=[mg.r])
                        h2b = h2T[bi % 2]
                        for i, (x_, tw) in enumerate(xts):
                            x1_ = x1[i % 4]
                            for hf in range(2):
                                p_ = pb[hf + 2 * (i % 2)]
                                for kc in range(8):
                                    kb.mm(p_.t[0:tw, :], mg.t[:, kc, i * 128:i * 128 + tw], Wo.t[:, kc, hf * 512:(hf + 1) * 512],
                                          kc == 0, kc == 7, reads=[mg.r, Wo.r], writes=[p_.r])
                                kb.tt(DVE, x1_.t[0:tw, hf * 512:(hf + 1) * 512], p_.t[0:tw, :],
                                      x_.t[0:tw, hf * 512:(hf + 1) * 512], ALU.add, reads=[p_.r, x_.r], writes=[x1_.r])
                            if not is_halo:
                                kb.dma(POOL, X1_s[c0 + i * 128:c0 + i * 128 + tw, :], x1_.t[0:tw, :], reads=[x1_.r])
                            norm_transpose(nt2[i % 4], x1_, tw, h2b, i * 128, part="norm")
                        prep_m(bi + 1, "tr")
                        for i, (x_, tw) in enumerate(xts):
                            norm_transpose(nt2[i % 4], x1[i % 4], tw, h2b, i * 128, part="tr")
                        if is_halo:
                            kb.tt(DVE, h2b.t[:, :, 0:2], h2b.t[:, :, 0:2],
                                  mask.t[:, :].unsqueeze(1).to_broadcast([128, 8, 2]), ALU.mult,
                                  reads=[h2b.r, mask.r], writes=[h2b.r])
                            kb.dma(POOL, H2T_s[:, :, 0:1].rearrange("c p s -> p c s"), h2b.t[:, :, 0:1], reads=[h2b.r], slow=True)
                            kb.dma(POOL, H2T_s[:, :, n_own + 1:n_own + 2].rearrange("c p s -> p c s"), h2b.t[:, :, 1:2],
                                   reads=[h2b.r], slow=True)
                        else:
                            kb.dma(POOL, H2T_s[:, :, 1 + c0:1 + c0 + W].rearrange("c p s -> p c s"), h2b.t[:, :, 0:W],
                                   reads=[h2b.r])
                    kb.barrier()

            if "N" in phases:
                kb.phase = 1 + 6 * seq_idx + 5
                with ExitStack() as es:
                    Wup = sb(es, "Wup", [128, 8, 2 * DFF], BF16)
                    Wdn = sb(es, "Wdn", [128, NHC, D], BF16)
                    NWB = 11
                    wup_r = [Res() for _ in range(NWB)]
                    order = []
                    for cp_ in range(NHC):
                        for blk_ in (cp_ // 4, (cp_ + NHC) // 4):
                            if blk_ not in order:
                                order.append(blk_)
                    for blk_ in order:
                        for c in range(8):
                            kb.dma(SP, Wup.t[:, c, blk_ * 512:(blk_ + 1) * 512], Wup_s[c, :, blk_ * 512:(blk_ + 1) * 512],
                                   writes=[wup_r[blk_]], accum=True)
                    for c in range(NHC):
                        kb.dma(SP, Wdn.t[:, c, :], Wdn_s[c], writes=[Wdn.r], accum=True)
                    cw = sb(es, "cw", [128, NFC, 3], F32)
                    cbias = sb(es, "cbias", [128, NFC], F32)
                    gfin = sb(es, "gfin", [128, D], F32)
                    kb.dma(SP, cw.t[:], convw_d[:, :, :], writes=[cw.r])
                    kb.dma(SP, cbias.t[:], convb_d[:, :], writes=[cbias.r])
                    kb.dma(SP, gfin.t[:], gfin_d[:, :], writes=[gfin.r])
                    h2 = [sb(es, f"h2n{i}", [128, 8, 512], BF16) for i in range(2)]
                    acc = [sb(es, f"acc{i}", [128, 512], F32) for i in range(4)]
                    sg = [sb(es, f"sg{i}", [128, 512], F32) for i in range(2)]
                    actT = sb(es, "actT", [128, NHC, 512], BF16)
                    x1t = [sb(es, f"x1n{i}", [128, D], F32) for i in range(2)]
                    yo = [sb(es, f"yo{i}", [128, D], F32) for i in range(2)]
                    ss = [sb(es, f"ssn{i}", [128, 1], F32) for i in range(2)]
                    rstd = [sb(es, f"rstdn{i}", [128, 1], F32) for i in range(2)]
                    nb = (n_own + FB - 1) // FB
                    ti = 0
                    ai = 0
                    for b in range(nb):
                        t0 = b * FB
                        Wb = min(FB, n_own - t0)
                        h2_ = h2[b % 2]
                        kb.dma(SP, h2_.t[:, :, 0:Wb + 2], H2T_s[:, :, t0:t0 + Wb + 2].rearrange("c p s -> p c s"),
                               writes=[h2_.r])
                        for cp_ in range(NHC):
                            accs = []
                            for half, ch in enumerate((cp_, cp_ + NHC)):
                                p_ = pb[(2 * cp_ + half) % 4]
                                for dc in range(8):
                                    kb.mm(p_.t[:, 0:Wb + 2], Wup.t[:, dc, ch * 128:(ch + 1) * 128], h2_.t[:, dc, 0:Wb + 2],
                                          dc == 0, dc == 7, reads=[wup_r[ch // 4], h2_.r], writes=[p_.r])
                                a_ = acc[ai % 4]
                                ai += 1
                                kb.act(a_.t[:, 0:Wb], p_.t[:, 1:Wb + 1], AF.Identity, reads=[p_.r, cw.r, cbias.r],
                                       writes=[a_.r], scale=cw.t[:, ch, 1:2], bias=cbias.t[:, ch:ch + 1])
                                kb.stt(DVE, a_.t[:, 0:Wb], p_.t[:, 0:Wb], cw.t[:, ch, 0:1], a_.t[:, 0:Wb], ALU.mult, ALU.add,
                                       reads=[p_.r, a_.r, cw.r], writes=[a_.r])
                                kb.stt(DVE, a_.t[:, 0:Wb], p_.t[:, 2:Wb + 2], cw.t[:, ch, 2:3], a_.t[:, 0:Wb], ALU.mult, ALU.add,
                                       reads=[p_.r, a_.r, cw.r], writes=[a_.r])
                                accs.append(a_)
                            s_ = sg[cp_ % 2]
                            kb.act(s_.t[:, 0:Wb], accs[0].t[:, 0:Wb], AF.Silu, reads=[accs[0].r], writes=[s_.r])
                            kb.tt(POOL, actT.t[:, cp_, 0:Wb], s_.t[:, 0:Wb], accs[1].t[:, 0:Wb], ALU.mult,
                                  reads=[s_.r, accs[1].r], writes=[actT.r])
                        ntile = (Wb + 127) // 128
                        for i in range(ntile):
                            tw = min(128, Wb - i * 128)
                            r0 = t0 + i * 128
                            x1_ = x1t[ti % 2]
                            x2_ = x1_
                            y_ = yo[ti % 2]
                            ss_ = ss[ti % 2]
                            rs_ = rstd[ti % 2]
                            kb.dma(SP, x1_.t[0:tw, :], X1_s[r0:r0 + tw, :], writes=[x1_.r])
                            for hf in range(2):
                                p_ = pb[4 + hf]
                                for c in range(NHC):
                                    kb.mm(p_.t[0:tw, :], actT.t[:, c, i * 128:i * 128 + tw], Wdn.t[:, c, hf * 512:(hf + 1) * 512],
                                          c == 0, c == NHC - 1, reads=[actT.r, Wdn.r], writes=[p_.r])
                                kb.tt(DVE, x2_.t[0:tw, hf * 512:(hf + 1) * 512], p_.t[0:tw, :],
                                      x1_.t[0:tw, hf * 512:(hf + 1) * 512], ALU.add, reads=[p_.r, x1_.r], writes=[x2_.r])
                            kb.act(y_.t[0:tw, :], x2_.t[0:tw, :], AF.Square, reads=[x2_.r], writes=[y_.r, ss_.r],
                                   accum_out=ss_.t[0:tw, :])
                            kb.act(ss_.t[0:tw, :], ss_.t[0:tw, :], AF.Sqrt, reads=[ss_.r, epsq.r], writes=[ss_.r],
                                   bias=epsq.t[0:tw, 0:1], scale=1.0 / D)
                            kb.recip(rs_.t[0:tw, :], ss_.t[0:tw, :], reads=[ss_.r], writes=[rs_.r])
                            kb.stt(DVE, y_.t[0:tw, :], x2_.t[0:tw, :], rs_.t[0:tw, 0:1], gfin.t[0:tw, :], ALU.mult, ALU.mult,
                                   reads=[x2_.r, rs_.r, gfin.r], writes=[y_.r])
                            kb.dma(POOL, y_out[r0:r0 + tw, :], y_.t[0:tw, :], reads=[y_.r])
                            ti += 1
                    kb.barrier()

        if cfg.get("do_prompt", True):
            run_sequence(0, "p", SP_, NP_, xp_full, xp_own, OWNP, True, yp)
        for si in range(cfg.get("n_run_seq", NSEQ)):
            run_sequence(1 + si, "s", SS_, NS_, xs[si], xs[si], SS_, False, ys[si])
        kb.barrier()
    return nc


def _bf(a):
    return np.ascontiguousarray(a.astype(ml_dtypes.bfloat16))


def host_tables(S, N, own_pos, scale):
    inv = (10000.0 ** (-np.arange(0, ROPE, 2, dtype=np.float32) / ROPE)).astype(np.float32)
    tq = np.arange(S)
    posA = (tq % N) * N + tq // N
    angK = posA[None, :].astype(np.float32) * np.concatenate([inv, inv])[:, None]
    kcs = np.concatenate([np.cos(angK), np.sin(angK)], 0).astype(np.float32)
    angQ = np.asarray(own_pos, np.float32)[None, :] * np.concatenate([inv, inv])[:, None]
    nq = len(own_pos)
    qcos = np.concatenate([np.full((NOPE, nq), scale, np.float32), scale * np.cos(angQ)], 0).astype(np.float32)
    qsin = np.concatenate([np.zeros((NOPE, nq), np.float32), scale * np.sin(angQ)], 0).astype(np.float32)
    n2 = np.arange(N, dtype=np.float64)[:, None]
    ph = 2 * np.pi * n2 * np.asarray(own_pos, np.float64)[None, :] / S
    ecs = np.stack([np.cos(ph), np.sin(ph)], 1)
    th = 2 * np.pi * np.outer(np.arange(N), np.arange(N)) / N
    ca = np.concatenate([np.cos(th), -np.sin(th)], 1)
    cb = np.concatenate([np.sin(th), np.cos(th)], 1)
    return dict(kcs=kcs, qcos=qcos, qsin=qsin, ecs=_bf(ecs), ca=_bf(ca), cb=_bf(cb))


def host_consts():
    th = 2 * np.pi * np.outer(np.arange(128), np.arange(128)) / 128
    cs128 = np.concatenate([np.cos(th), -np.sin(th)], 1)
    eplace = np.zeros((64, QK), np.float32)
    for r in range(32):
        eplace[r, 64 + r] = 1.0
        eplace[32 + r, 64 + r] = 1.0
    return dict(cs128=_bf(cs128), ident=_bf(np.eye(128, dtype=np.float32)), eplace=_bf(eplace))


def weight_maps(g_mix, w_in, g_q, w_uq, g_kv, w_ukv, w_attn_out, w_fourier_out, w_out, g_ffn, w_up,
                conv_w, conv_b, w_down, g_final):
    c = np.ascontiguousarray
    f = np.float32
    return dict(
        w_in=c(w_in[0], f), w_uq=c(w_uq[0], f), w_ukv=c(w_ukv[0], f), w_ao=c(w_attn_out[0], f),
        w_fo=c(w_fourier_out[0], f), w_out=c(w_out[0], f), w_up=c(w_up[0], f), w_dn=c(w_down[0], f),
        g_mix8=c(g_mix[0].reshape(8, 128).T, f), g_q2=c(g_q[0].reshape(2, 128).T, f),
        g_kv1=c(g_kv[0].reshape(1, 128).T, f), g_ffn8=c(g_ffn[0].reshape(8, 128).T, f),
        convw=c(conv_w[0].reshape(3, NFC, 128).transpose(2, 1, 0), f),
        convb=c(conv_b[0].reshape(NFC, 128).T, f),
        gfin=c(np.broadcast_to(g_final.reshape(1, D), (128, D)), f))


_NC_CACHE = {}


def run(cfg, x_prompt, x_sample, wm, n_cores=8):
    SP_, SS_, NSEQ = cfg["SP"], cfg["SS"], cfg["NSEQ"]
    OWNP = SP_ // 4
    key = tuple(sorted((k, str(v)) for k, v in cfg.items()))
    if key not in _NC_CACHE:
        _NC_CACHE[key] = build(cfg)
    nc = _NC_CACHE[key]
    scale = float(QK) ** -0.5
    consts = host_consts()
    ts = host_tables(SS_, cfg["NS"], np.arange(SS_), scale)
    in_maps = []
    for c in range(n_cores):
        b, j = c // 4, c % 4
        own_pos = list(range(OWNP * j, OWNP * (j + 1))) + [OWNP * j - 1, OWNP * (j + 1)]
        tp = host_tables(SP_, cfg["NP"], own_pos, scale)
        xo = np.zeros((OWNP + 2, D), np.float32)
        xo[:OWNP] = x_prompt[b, OWNP * j:OWNP * (j + 1)]
        mask = np.zeros((128, 2), np.float32)
        if j > 0:
            xo[OWNP] = x_prompt[b, OWNP * j - 1]
            mask[:, 0] = 1.0
        if j < 3:
            xo[OWNP + 1] = x_prompt[b, OWNP * (j + 1)]
            mask[:, 1] = 1.0
        m = dict(xp_full=np.ascontiguousarray(x_prompt[b]), xp_own=xo,
                 xs=np.ascontiguousarray(x_sample[NSEQ * c:NSEQ * (c + 1)]), mask=mask)
        for k, v in tp.items():
            m[f"{k}_p"] = v
        for k, v in ts.items():
            m[f"{k}_s"] = v
        m.update(consts)
        m.update(wm)
        in_maps.append(m)
    res = run_bass_kernel_spmd(nc, in_maps, core_ids=list(range(n_cores)))
    return res.results


FULL_CFG = dict(SP=16384, NP=128, SS=4096, NS=64, NSEQ=2)


def kernel(x_prompt, x_sample, g_mix, w_in, g_q, w_uq, g_kv, w_ukv, w_attn_out, w_fourier_out, w_out,
           g_ffn, w_up, conv_w, conv_b, w_down, g_final):
    cfg = FULL_CFG
    x_prompt = np.asarray(x_prompt, np.float32)
    x_sample = np.asarray(x_sample, np.float32)
    wm = weight_maps(*[np.asarray(a, np.float32) for a in (g_mix, w_in, g_q, w_uq, g_kv, w_ukv, w_attn_out,
                                                            w_fourier_out, w_out, g_ffn, w_up, conv_w, conv_b,
                                                            w_down, g_final)])
    results = run(cfg, x_prompt, x_sample, wm)
    OWNP = cfg["SP"] // 4
    yp = np.zeros_like(x_prompt)
    ysm = np.zeros_like(x_sample)
    for c in range(8):
        b, j = c // 4, c % 4
        yp[b, OWNP * j:OWNP * (j + 1)] = results[c]["yp"]
        ysm[cfg["NSEQ"] * c:cfg["NSEQ"] * (c + 1)] = results[c]["ys"]
    return (yp, ysm)
```

```python
import numpy as np
import ml_dtypes
from contextlib import ExitStack
import concourse.bass as bass
import concourse.mybir as mybir
from concourse.bass_utils import run_bass_kernel_spmd

F32 = mybir.dt.float32
BF16 = mybir.dt.bfloat16
AF = mybir.ActivationFunctionType
ALU = mybir.AluOpType

D = 1024
H = 8
QR = 256
KVR = 128
NOPE = 64
ROPE = 32
VD = 64
QK = NOPE + ROPE
FG = 4
DFF = 2816
NFC = 2 * DFF // 128
NHC = DFF // 128
EPS = 1e-6
IN_COLS = 2976
C_CQ, C_CKV, C_KR, C_KRR, C_XF, C_G, WI_COLS = 0, 256, 384, 416, 448, 960, 3008
FB = 510


from functools import partial

TRAMP = [
    lambda f: f(),
    lambda f: f(),
    lambda f: f(),
    lambda f: f(),
    lambda f: f(),
    lambda f: f(),
    lambda f: f(),
    lambda f: f(),
    lambda f: f(),
    lambda f: f(),
    lambda f: f(),
    lambda f: f(),
    lambda f: f(),
    lambda f: f(),
    lambda f: f(),
    lambda f: f(),
    lambda f: f(),
    lambda f: f(),
    lambda f: f(),
    lambda f: f(),
]


class Res:
    __slots__ = ("w", "r")

    def __init__(self):
        self.w = {}
        self.r = {}


class Eng:
    def __init__(self, nc, es, name, eng, sem=True):
        self.e = eng
        self.key = name
        self.sem = es.enter_context(nc.semaphore(name)) if sem else None
        self.cnt = 0
        self.seen = {}
        self.dsem = []
        self.di = 0


class DSem:
    def __init__(self, nc, es, name):
        self.key = name
        self.sem = es.enter_context(nc.semaphore(name))
        self.cnt = 0


class KB:
    def __init__(self, nc, es, ndma=20):
        self.nc = nc
        self.phase = 0
        self.PE = Eng(nc, es, "sPE", nc.tensor)
        self.ACT = Eng(nc, es, "sACT", nc.scalar)
        self.DVE = Eng(nc, es, "sDVE", nc.vector)
        self.POOL = Eng(nc, es, "sPOOL", nc.gpsimd)
        self.SP = Eng(nc, es, "sSP", nc.sync, sem=False)
        self.engs = [self.PE, self.ACT, self.DVE, self.POOL, self.SP]
        for q in (self.SP, self.POOL):
            q.dsem = [DSem(nc, es, f"d{q.key}{i}") for i in range(ndma)]

    def _wait(self, E, deps):
        for key, (sem, val) in deps.items():
            if key == E.key and val > E.cnt:
                continue
            if E.seen.get(key, 0) < val:
                E.e.wait_ge(sem, val)
                E.seen[key] = val

    @staticmethod
    def _gather(reads, writes, accum):
        deps = {}

        def add(d):
            for k, ev in d.items():
                if k not in deps or deps[k][1] < ev[1]:
                    deps[k] = ev
        for r in reads:
            add(r.w)
        for w in writes:
            if not accum:
                add(w.w)
            add(w.r)
        return deps

    def op(self, E, fn, reads=(), writes=(), signal=True):
        deps = self._gather(reads, writes, False)
        self._wait(E, deps)
        ins = TRAMP[self.phase](fn)
        val = E.cnt + 1
        if signal:
            ins.then_inc(E.sem, 1)
            E.cnt = val
        ev = (E.sem, val)
        for r in reads:
            if r.r.get(E.key, (None, 0))[1] < val:
                r.r[E.key] = ev
        for w in writes:
            w.w = {E.key: ev}
            w.r = {}
        return ins

    def dma(self, Q, out, in_, reads=(), writes=(), accum=False, slow=False):
        slot = Q.dsem[Q.di % len(Q.dsem)]
        Q.di += 1
        deps = self._gather(reads, writes, accum)
        if slot.cnt > 0:
            deps[slot.key] = (slot.sem, slot.cnt)
        self._wait(Q, deps)
        ins = TRAMP[self.phase](partial(Q.e.dma_start, out=out, in_=in_, allow_slow_non_contiguous=True) if slow
                                else partial(Q.e.dma_start, out=out, in_=in_))
        slot.cnt += 16
        ins.then_inc(slot.sem, 16)
        ev = (slot.sem, slot.cnt)
        for r in reads:
            r.r[slot.key] = ev
        for w in writes:
            if accum:
                w.w[slot.key] = ev
            else:
                w.w = {slot.key: ev}
                w.r = {}
        return ins

    def barrier(self):
        tg = {}
        for E in self.engs:
            if E.sem is not None and E.cnt > 0:
                tg[E.key] = (E.sem, E.cnt)
            for s in E.dsem:
                if s.cnt > 0:
                    tg[s.key] = (s.sem, s.cnt)
        for E in self.engs:
            self._wait(E, tg)

    def mm(self, out, lhsT, rhs, start, stop, reads=(), writes=(), signal=None):
        if signal is None:
            signal = stop
        return self.op(self.PE, partial(self.nc.tensor.matmul, out, lhsT=lhsT, rhs=rhs, start=start, stop=stop),
                       reads, writes, signal)

    def tr(self, out, in_, ident, reads=(), writes=(), signal=True):
        return self.op(self.PE, partial(self.nc.tensor.transpose, out, in_, ident), reads, writes, signal)

    def act(self, out, in_, func, reads=(), writes=(), **kw):
        return self.op(self.ACT, partial(self.nc.scalar.activation, out=out, in_=in_, func=func, **kw), reads, writes)

    def tt(self, E, out, in0, in1, op, reads=(), writes=()):
        return self.op(E, partial(E.e.tensor_tensor, out=out, in0=in0, in1=in1, op=op), reads, writes)

    def ts(self, E, out, in0, s1, s2, op0, op1=None, reads=(), writes=()):
        if op1 is None:
            return self.op(E, partial(E.e.tensor_scalar, out=out, in0=in0, scalar1=s1, scalar2=None, op0=op0),
                           reads, writes)
        return self.op(E, partial(E.e.tensor_scalar, out=out, in0=in0, scalar1=s1, scalar2=s2, op0=op0, op1=op1),
                       reads, writes)

    def stt(self, E, out, in0, scalar, in1, op0, op1, reads=(), writes=()):
        return self.op(E, partial(E.e.scalar_tensor_tensor, out=out, in0=in0, scalar=scalar, in1=in1, op0=op0, op1=op1),
                       reads, writes)

    def cp(self, E, out, in_, reads=(), writes=()):
        if E is self.ACT:
            return self.op(E, partial(E.e.activation, out=out, in_=in_, func=AF.Copy), reads, writes)
        return self.op(E, partial(E.e.tensor_copy, out=out, in_=in_), reads, writes)

    def memset(self, E, ap, v, writes=()):
        return self.op(E, partial(E.e.memset, ap, v), (), writes)

    def recip(self, out, in_, reads=(), writes=()):
        return self.op(self.DVE, partial(self.nc.vector.reciprocal, out=out, in_=in_), reads, writes)


class T:
    def __init__(self, t):
        self.t = t
        self.r = Res()


def build(cfg):
    SP_, NP_, SS_, NS_, NSEQ = cfg["SP"], cfg["NP"], cfg["SS"], cfg["NS"], cfg["NSEQ"]
    OWNP = SP_ // 4
    debug = cfg.get("debug", False)
    phases = cfg.get("phases", "AFQTMN")
    nc = bass.Bass("TRN2", target_bir_lowering=False)

    def din(name, shape, dt=F32):
        return nc.dram_tensor(name, list(shape), dt, kind="ExternalInput").ap()

    def dscr(name, shape, dt=BF16):
        return nc.dram_tensor(name, list(shape), dt, kind=("ExternalOutput" if debug else "Internal")).ap()

    xp_full = din("xp_full", [SP_, D])
    xp_own = din("xp_own", [OWNP + 2, D])
    xs = din("xs", [NSEQ, SS_, D])
    tabs = {}
    for tag, S, N, nq in (("p", SP_, NP_, OWNP + 2), ("s", SS_, NS_, SS_)):
        tabs[tag] = dict(
            kcs=din(f"kcs_{tag}", [64, S]), qcos=din(f"qcos_{tag}", [QK, nq]), qsin=din(f"qsin_{tag}", [QK, nq]),
            ecs=din(f"ecs_{tag}", [N, 2, nq], BF16),
            ca=din(f"ca_{tag}", [N, 2 * N], BF16), cb=din(f"cb_{tag}", [N, 2 * N], BF16))
    cs128_d = din("cs128", [128, 256], BF16)
    ident_d = din("ident", [128, 128], BF16)
    eplace_d = din("eplace", [64, QK], BF16)
    mask_d = din("mask", [128, 2])
    w_in_d = din("w_in", [D, IN_COLS])
    w_uq_d = din("w_uq", [QR, H * QK])
    w_ukv_d = din("w_ukv", [KVR, H * 128])
    w_ao_d = din("w_ao", [512, D])
    w_fo_d = din("w_fo", [512, D])
    w_out_d = din("w_out", [D, D])
    w_up_d = din("w_up", [D, 2 * DFF])
    w_dn_d = din("w_dn", [DFF, D])
    g_mix_d = din("g_mix8", [128, 8])
    g_q_d = din("g_q2", [128, 2])
    g_kv_d = din("g_kv1", [128, 1])
    g_ffn_d = din("g_ffn8", [128, 8])
    convw_d = din("convw", [128, NFC, 3])
    convb_d = din("convb", [128, NFC])
    gfin_d = din("gfin", [128, D])

    yp = nc.dram_tensor("yp", [OWNP, D], F32, kind="ExternalOutput").ap()
    ys = nc.dram_tensor("ys", [NSEQ, SS_, D], F32, kind="ExternalOutput").ap()

    Wi_s = dscr("Wi_s", [8, 128, WI_COLS])
    Wuq_s = dscr("Wuq_s", [2, 128, 2 * H * QK])
    Wkv_s = dscr("Wkv_s", [128, H * QK + H * VD])
    Wao_s = dscr("Wao_s", [4, 128, D])
    Wfo_s = dscr("Wfo_s", [4, 128, D])
    Wo_s = dscr("Wo_s", [8, 128, D])
    Wup_s = dscr("Wup_s", [8, 128, 2 * DFF])
    Wdn_s = dscr("Wdn_s", [NHC, 128, D])
    SMAX = max(SP_, SS_)
    NQMAX = max(OWNP + 2, SS_ + 2)
    KT_s = dscr("KT_s", [H, QK, SMAX])
    V_s = dscr("V_s", [H, 128, SMAX // 128, VD + 1])
    Y_s = dscr("Y_s", [FG, 2, 128 * SMAX])
    FT_s = dscr("FT_s", [FG, 128, NQMAX])
    QT_s = dscr("QT_s", [H, QK, NQMAX])
    AT_s = dscr("AT_s", [H * VD, NQMAX])
    X1_s = dscr("X1_s", [NQMAX, D], F32)
    H2T_s = dscr("H2T_s", [8, 128, NQMAX])

    es_top = ExitStack()
    with es_top:
        kb = KB(nc, es_top)
        PE, ACT, DVE, POOL, SP = kb.PE, kb.ACT, kb.DVE, kb.POOL, kb.SP

        uid = [0]

        def sb(es, name, shape, dt):
            uid[0] += 1
            return T(es.enter_context(nc.sbuf_tensor(f"sb{uid[0]}_{name}", list(shape), dt)))

        def ps(es, name, shape, dt):
            uid[0] += 1
            return T(es.enter_context(nc.psum_tensor(f"ps{uid[0]}_{name}", list(shape), dt)))

        ident = sb(es_top, "ident", [128, 128], BF16)
        eplace = sb(es_top, "eplace", [64, QK], BF16)
        cs128 = sb(es_top, "cs128", [128, 256], BF16)
        onesf = sb(es_top, "onesf", [128, 128], F32)
        epsq = sb(es_top, "epsq", [128, 1], F32)
        mask = sb(es_top, "mask", [128, 2], F32)
        zeros = sb(es_top, "zeros", [128, 16], BF16)
        kb.dma(SP, ident.t[:], ident_d[:, :], writes=[ident.r])
        kb.dma(SP, eplace.t[:], eplace_d[:, :], writes=[eplace.r])
        kb.dma(SP, cs128.t[:], cs128_d[:, :], writes=[cs128.r])
        kb.dma(SP, mask.t[:], mask_d[:, :], writes=[mask.r])
        kb.memset(DVE, onesf.t[:], 1.0, writes=[onesf.r])
        kb.memset(DVE, epsq.t[:], EPS, writes=[epsq.r])
        kb.memset(DVE, zeros.t[:], 0.0, writes=[zeros.r])
        PSALL = es_top.enter_context(nc.psum_tensor("psall", [128, 8, 512], F32))
        pb = [T(PSALL[:, i, :]) for i in range(8)]
        pT = T(PSALL[:, 7, :].bitcast(BF16))
        pT.r = pb[7].r
        kb.barrier()

        if "W" in phases or True:
            with ExitStack() as es:
                stg = [sb(es, f"stg{i}", [128, 2 * DFF], F32) for i in range(2)]
                wo = [sb(es, f"wo{i}", [128, 2 * DFF], BF16) for i in range(2)]
                gm = sb(es, "gm", [128, 8], F32)
                gq = sb(es, "gq", [128, 2], F32)
                gkv = sb(es, "gkv", [128, 1], F32)
                gf = sb(es, "gf", [128, 8], F32)
                for t_, d_ in ((gm, g_mix_d), (gq, g_q_d), (gkv, g_kv_d), (gf, g_ffn_d)):
                    kb.dma(SP, t_.t[:], d_[:, :], writes=[t_.r])
                cnt = [0]

                def conv(src, ncols, dst, fn):
                    i = cnt[0] % 2
                    cnt[0] += 1
                    kb.dma(SP, stg[i].t[:, 0:ncols], src, writes=[stg[i].r])
                    fn(stg[i], wo[i])
                    kb.dma(POOL, dst, wo[i].t[:, 0:dst.shape[-1]], reads=[wo[i].r])

                def scaled(E, o, i_, sc, rd, wr, neg=False):
                    if sc is None:
                        kb.cp(E, o, i_, reads=rd, writes=wr)
                    elif neg:
                        kb.ts(E, o, i_, sc, -1.0, ALU.mult, ALU.mult, reads=rd, writes=wr)
                    else:
                        kb.ts(E, o, i_, sc, None, ALU.mult, reads=rd, writes=wr)

                for dc in range(8):
                    def f_in(s_, o_, dc=dc):
                        sc = gm.t[:, dc:dc + 1]
                        rd, wr = [s_.r, gm.r], [o_.r]
                        scaled(DVE, o_.t[:, 0:416], s_.t[:, 0:416], sc, rd, wr)
                        scaled(DVE, o_.t[:, C_KRR:C_KRR + 16], s_.t[:, 400:416], sc, rd, wr, neg=True)
                        scaled(DVE, o_.t[:, C_KRR + 16:C_KRR + 32], s_.t[:, 384:400], sc, rd, wr)
                        scaled(DVE, o_.t[:, C_XF:WI_COLS], s_.t[:, 416:IN_COLS], sc, rd, wr)
                    conv(w_in_d[dc * 128:(dc + 1) * 128, :], IN_COLS, Wi_s[dc], f_in)
                for kc in range(2):
                    def f_uq(s_, o_, kc=kc):
                        sc = gq.t[:, kc:kc + 1]
                        rd, wr = [s_.r, gq.r], [o_.r]
                        n = H * QK
                        scaled(DVE, o_.t[:, 0:n], s_.t[:, 0:n], sc, rd, wr)
                        kb.memset(DVE, o_.t[:, n:2 * n], 0.0, writes=wr)
                        s3 = s_.t[:, 0:n].rearrange("p (h c) -> p h c", c=QK)
                        o3 = o_.t[:, n:2 * n].rearrange("p (h c) -> p h c", c=QK)
                        scaled(DVE, o3[:, :, 64:80], s3[:, :, 80:96], sc, rd, wr, neg=True)
                        scaled(DVE, o3[:, :, 80:96], s3[:, :, 64:80], sc, rd, wr)
                    conv(w_uq_d[kc * 128:(kc + 1) * 128, :], H * QK, Wuq_s[kc], f_uq)

                def f_kv(s_, o_):
                    sc = gkv.t[:, 0:1]
                    rd, wr = [s_.r, gkv.r], [o_.r]
                    kb.memset(DVE, o_.t[:, 0:H * QK], 0.0, writes=wr)
                    s3 = s_.t[:, 0:H * 128].rearrange("p (h c) -> p h c", c=128)
                    ok = o_.t[:, 0:H * QK].rearrange("p (h c) -> p h c", c=QK)
                    ov = o_.t[:, H * QK:H * QK + H * VD].rearrange("p (h c) -> p h c", c=VD)
                    scaled(DVE, ok[:, :, 0:64], s3[:, :, 0:64], sc, rd, wr)
                    scaled(DVE, ov, s3[:, :, 64:128], sc, rd, wr)
                conv(w_ukv_d[:, :], H * 128, Wkv_s[:, :], f_kv)

                def plain(sc_t=None, k=None):
                    def f(s_, o_):
                        n = o_.cur
                        if sc_t is None:
                            kb.cp(DVE, o_.t[:, 0:n], s_.t[:, 0:n], reads=[s_.r], writes=[o_.r])
                        else:
                            scaled(DVE, o_.t[:, 0:n], s_.t[:, 0:n], sc_t.t[:, k:k + 1], [s_.r, sc_t.r], [o_.r])
                    return f
                for src, dst, nch, ncols, sct in ((w_ao_d, Wao_s, 4, D, None), (w_fo_d, Wfo_s, 4, D, None),
                                                   (w_out_d, Wo_s, 8, D, None), (w_up_d, Wup_s, 8, 2 * DFF, gf),
                                                   (w_dn_d, Wdn_s, NHC, D, None)):
                    for c in range(nch):
                        for w_ in wo:
                            w_.cur = ncols
                        conv(src[c * 128:(c + 1) * 128, :], ncols, dst[c], plain(sct, c))
                kb.barrier()

        def norm_transpose(es_tiles, xt, tw, hT, col0, part="both"):
            junk, ss, rstd, hb = es_tiles
            if part in ("both", "norm"):
                kb.act(junk.t[0:tw, :], xt.t[0:tw, :], AF.Square, reads=[xt.r], writes=[junk.r, ss.r],
                       accum_out=ss.t[0:tw, :])
                kb.act(ss.t[0:tw, :], ss.t[0:tw, :], AF.Sqrt, reads=[ss.r, epsq.r], writes=[ss.r],
                       bias=epsq.t[0:tw, 0:1], scale=1.0 / D)
                kb.recip(rstd.t[0:tw, :], ss.t[0:tw, :], reads=[ss.r], writes=[rstd.r])
                kb.ts(DVE, hb.t[0:tw, :], xt.t[0:tw, :], rstd.t[0:tw, 0:1], None, ALU.mult,
                      reads=[xt.r, rstd.r], writes=[hb.r])
            if part == "norm":
                return rstd
            for dc in range(8):
                kb.tr(pT.t[:, dc * 128:dc * 128 + tw], hb.t[0:tw, dc * 128:(dc + 1) * 128], ident.t[0:tw, 0:tw],
                      reads=[hb.r, ident.r], writes=[pT.r], signal=(dc == 7))
            kb.cp(DVE, hT.t[:, :, col0:col0 + tw],
                  pT.t[:, :].rearrange("p (c t) -> p c t", t=128)[:, :, 0:tw],
                  reads=[pT.r], writes=[hT.r])
            return rstd

        def mk_norm_tiles(es, tag, junk=None):
            return (junk if junk is not None else sb(es, f"junk{tag}", [128, D], BF16), sb(es, f"ss{tag}", [128, 1], F32),
                    sb(es, f"rstd{tag}", [128, 1], F32), sb(es, f"hb{tag}", [128, D], BF16))

        def fm_rstd(es_t, src_list, W, nfeat):
            sq, pbc, rs, rbc = es_t
            for i, (p_, npart) in enumerate(src_list):
                kb.act(sq.t[0:npart, i, 0:W], p_.t[0:npart, 0:W], AF.Square, reads=[p_.r], writes=[sq.r])
            for i, (p_, npart) in enumerate(src_list):
                kb.mm(pbc.t[:, 0:W], onesf.t[0:npart, :], sq.t[0:npart, i, 0:W], i == 0, i == len(src_list) - 1,
                      reads=[sq.r, onesf.r], writes=[pbc.r])
            kb.act(rs.t[:, 0:W], pbc.t[:, 0:W], AF.Sqrt, reads=[pbc.r, epsq.r], writes=[rs.r],
                   bias=epsq.t[:, 0:1], scale=1.0 / nfeat)
            kb.recip(rbc.t[:, 0:W], rs.t[:, 0:W], reads=[rs.r], writes=[rbc.r])
            return rbc

        def run_sequence(seq_idx, tag, S, N, x_full, x_own, n_own, halo, y_out):
            tb = tabs[tag]
            nq = n_own + (2 if halo else 0)
            nblk_a = S // 512
            R = 128 // N
            qblocks = [(b * 512, 512) for b in range(n_own // 512)]
            if halo:
                qblocks.append((n_own, 2))
            xv = x_full.rearrange("(n1 n2) d -> n2 n1 d", n2=N)
            Yv = [[Y_s[g, ri, 0:N * N * 128].rearrange("(n1 n2 c) -> n1 n2 c", n2=N, c=128) for ri in range(2)]
                  for g in range(FG)]

            if "A" in phases:
                kb.phase = 1 + 6 * seq_idx + 0
                with ExitStack() as es:
                    Wi = sb(es, "Wi_a", [128, 8, 704], BF16)
                    for dc in range(8):
                        kb.dma(SP, Wi.t[:, dc, :], Wi_s[dc, :, C_CKV:C_G], writes=[Wi.r], accum=True)
                    Wkv = sb(es, "Wkv_a", [128, H * QK + H * VD], BF16)
                    kb.dma(SP, Wkv.t[:], Wkv_s[:, :], writes=[Wkv.r])
                    xt = [sb(es, f"xa{i}", [128, D], F32) for i in range(5)]
                    nt = [mk_norm_tiles(es, "a")]
                    for k_ in range(1, 4):
                        nt.append((nt[0][0], sb(es, f"ssa{k_}", [128, 1], F32), sb(es, f"rstda{k_}", [128, 1], F32),
                                   sb(es, f"hba{k_}", [128, D], BF16)))
                    hT = [sb(es, f"hTa{i}", [128, 8, 512], BF16) for i in range(2)]
                    fr = (sb(es, "sq_a", [128, 1, 512], F32), pb[1], sb(es, "rs_a", [128, 512], F32),
                          sb(es, "rbc_a", [128, 512], F32))
                    ckvn = sb(es, "ckvn", [128, 512], BF16)
                    kcs = [sb(es, f"kcs{i}", [64, 512], F32) for i in range(2)]
                    kr2 = sb(es, "kr2", [64, 512], BF16)
                    KTb = [sb(es, f"KTb{i}", [QK, H, 512], BF16) for i in range(2)]
                    Vb = [sb(es, f"Vb{i}", [128, H, 4, VD + 1], BF16) for i in range(2)]
                    for v_ in Vb:
                        kb.memset(DVE, v_.t[:], 1.0, writes=[v_.r])
                    xfT = [sb(es, f"xfT{i}", [128, 512], BF16) for i in range(2)]
                    Yb = [sb(es, f"Yb{i}", [128, FG, 2, 4, 128], BF16) for i in range(2)]
                    tia = [0]

                    xa_of = {}

                    def prep_a(blk, part):
                        if blk >= nblk_a:
                            return
                        hTb = hT[blk % 2]
                        if part == "norm":
                            xa_of[blk] = []
                        for i in range(4):
                            if part == "norm":
                                x_ = xt[tia[0] % 5]
                                tia[0] += 1
                                xa_of[blk].append(x_)
                                tile_idx = blk * 4 + i
                                for q in range(R):
                                    kb.dma(SP, x_.t[q * N:(q + 1) * N, :], xv[tile_idx * R + q], writes=[x_.r], accum=(q > 0))
                            norm_transpose(nt[i], xa_of[blk][i], 128, hTb, i * 128, part=part)
                    prep_a(0, "norm")
                    prep_a(0, "tr")
                    for blk in range(nblk_a):
                        hTb = hT[blk % 2]
                        prep_a(blk + 1, "norm")
                        kc_ = kcs[blk % 2]
                        kb.dma(SP, kc_.t[:], tb["kcs"][:, blk * 512:(blk + 1) * 512], writes=[kc_.r])
                        for dc in range(8):
                            kb.mm(pb[0].t[:, :], Wi.t[:, dc, 0:128], hTb.t[:, dc, :], dc == 0, dc == 7,
                                  reads=[Wi.r, hTb.r], writes=[pb[0].r])
                        rbc = fm_rstd(fr, [(pb[0], 128)], 512, KVR)
                        kb.tt(DVE, ckvn.t[:], pb[0].t[:, :], rbc.t[:], ALU.mult, reads=[pb[0].r, rbc.r], writes=[ckvn.r])
                        for dc in range(8):
                            kb.mm(pb[2].t[0:64, :], Wi.t[:, dc, 128:192], hTb.t[:, dc, :], dc == 0, dc == 7,
                                  reads=[Wi.r, hTb.r], writes=[pb[2].r])
                        kb.tt(DVE, kr2.t[:], pb[2].t[0:64, :], kc_.t[:], ALU.mult, reads=[pb[2].r, kc_.r], writes=[kr2.r])
                        Y_ = Yb[blk % 2]
                        for g in range(FG):
                            p_ = pb[3 + g % 2]
                            for dc in range(8):
                                kb.mm(p_.t[:, :], Wi.t[:, dc, 192 + g * 128:192 + (g + 1) * 128], hTb.t[:, dc, :],
                                      dc == 0, dc == 7, reads=[Wi.r, hTb.r], writes=[p_.r])
                            xf_ = xfT[g % 2]
                            kb.cp(ACT, xf_.t[:], p_.t[:, :], reads=[p_.r], writes=[xf_.r])
                            for i2 in range(2):
                                p2 = pb[5 + i2]
                                for i3 in range(2):
                                    i = i2 * 2 + i3
                                    kb.mm(p2.t[:, i3 * 256:(i3 + 1) * 256], xf_.t[:, i * 128:(i + 1) * 128], cs128.t[:],
                                          True, True, reads=[xf_.r, cs128.r], writes=[p2.r], signal=(i3 == 1))
                                kb.cp(DVE, Y_.t[:, g, :, i2 * 2:i2 * 2 + 2, :],
                                      p2.t[:, :].rearrange("p (i r c) -> p r i c", i=2, r=2),
                                      reads=[p2.r], writes=[Y_.r])
                        for g in range(FG):
                            for ri in range(2):
                                for q in range(R):
                                    n2s = slice(blk * 4 * R + q, blk * 4 * R + q + 3 * R + 1, R) if R > 1 else \
                                        slice(blk * 4, blk * 4 + 4)
                                    kb.dma(POOL, Yv[g][ri][:, n2s, :], Y_.t[q * N:(q + 1) * N, g, ri, :, :],
                                           reads=[Y_.r])
                        KT_ = KTb[blk % 2]
                        for h in range(H):
                            p_ = pb[3 + h % 2]
                            kb.mm(p_.t[0:QK, :], Wkv.t[:, h * QK:(h + 1) * QK], ckvn.t[:], True, False,
                                  reads=[Wkv.r, ckvn.r], writes=[p_.r])
                            kb.mm(p_.t[0:QK, :], eplace.t[:, :], kr2.t[:], False, True,
                                  reads=[eplace.r, kr2.r], writes=[p_.r])
                            kb.cp(ACT if h % 2 else DVE, KT_.t[:, h, :], p_.t[0:QK, :], reads=[p_.r], writes=[KT_.r])
                        kb.dma(POOL, KT_s[:, :, blk * 512:(blk + 1) * 512].rearrange("h c s -> c h s"), KT_.t[:],
                               reads=[KT_.r])
                        V_ = Vb[blk % 2]
                        for i in range(4):
                            p_ = pb[5 + i % 2]
                            kb.mm(p_.t[:, :], ckvn.t[:, i * 128:(i + 1) * 128], Wkv.t[:, H * QK:H * QK + H * VD],
                                  True, True, reads=[ckvn.r, Wkv.r], writes=[p_.r])
                            kb.cp(ACT if i % 2 else DVE, V_.t[:, :, i, 0:VD],
                                  p_.t[:, :].rearrange("p (h c) -> p h c", c=VD), reads=[p_.r], writes=[V_.r])
                        kb.dma(POOL, V_s[:, :, blk * 4:(blk + 1) * 4, :].rearrange("h p c e -> p h c e"), V_.t[:],
                               reads=[V_.r])
                        prep_a(blk + 1, "tr")
                    kb.barrier()

            if "F" in phases:
                kb.phase = 1 + 6 * seq_idx + 1
                with ExitStack() as es:
                    Yt = sb(es, "Yt", [128, 2, N * 128], BF16)
                    At = sb(es, "At", [128, 2, 128 * N], BF16)
                    ca = sb(es, "ca", [128, 2 * N], BF16)
                    cbm = sb(es, "cbm", [128, 2 * N], BF16)
                    ecs = sb(es, "ecs", [128, 2, nq], BF16)
                    FTb = [sb(es, f"FTb{i}", [128, nq], BF16) for i in range(2)]
                    kb.dma(SP, ca.t[0:N, :], tb["ca"][:, :], writes=[ca.r])
                    kb.dma(SP, cbm.t[0:N, :], tb["cb"][:, :], writes=[cbm.r])
                    kb.dma(SP, ecs.t[0:N, :, :], tb["ecs"][:, :, :], writes=[ecs.r])
                    K2L = n_own // N
                    cpb = 512 // (2 * N)
                    kpb = 512 // K2L
                    fscale = 1.0 / float(np.sqrt(S * 128.0))
                    A4 = At.t[:, :, :].rearrange("p r (c k) -> p r c k", k=N)
                    Y4 = Yt.t[:, :, :].rearrange("p r (n c) -> p r n c", c=128)
                    for g in range(FG):
                        for ri in range(2):
                            kb.dma(SP, Yt.t[0:N, ri, :], Y_s[g, ri, 0:N * N * 128].rearrange("(n r) -> n r", n=N),
                                   writes=[Yt.r], accum=(ri > 0))
                        for cg in range(128 // cpb):
                            p_ = pb[cg % 3]
                            for ci in range(cpb):
                                c_ = cg * cpb + ci
                                o_ = p_.t[0:N, ci * 2 * N:(ci + 1) * 2 * N]
                                kb.mm(o_, Y4[0:N, 0, :, c_], ca.t[0:N, :], True, False, reads=[Yt.r, ca.r], writes=[p_.r])
                                kb.mm(o_, Y4[0:N, 1, :, c_], cbm.t[0:N, :], False, True, reads=[Yt.r, cbm.r], writes=[p_.r],
                                      signal=(ci == cpb - 1))
                            kb.cp(ACT if cg % 2 else DVE, A4[0:N, :, cg * cpb:(cg + 1) * cpb, :],
                                  p_.t[0:N, :].rearrange("p (c r k) -> p r c k", c=cpb, r=2),
                                  reads=[p_.r], writes=[At.r])
                        F_ = FTb[g % 2]
                        Fv = F_.t[:, 0:n_own].rearrange("p (k2 k1) -> p k1 k2", k1=N)
                        for kg in range(N // kpb):
                            p_ = pb[3 + kg % 3]
                            for ki in range(kpb):
                                k1 = kg * kpb + ki
                                o_ = p_.t[:, ki * K2L:(ki + 1) * K2L]
                                kb.mm(o_, A4[0:N, 0, :, k1], ecs.t[0:N, 0, k1:n_own:N], True, False,
                                      reads=[At.r, ecs.r], writes=[p_.r])
                                kb.mm(o_, A4[0:N, 1, :, k1], ecs.t[0:N, 1, k1:n_own:N], False, True,
                                      reads=[At.r, ecs.r], writes=[p_.r], signal=(ki == kpb - 1))
                            kb.act(Fv[:, kg * kpb:(kg + 1) * kpb, :], p_.t[:, :].rearrange("p (k c) -> p k c", c=K2L),
                                   AF.Copy, reads=[p_.r], writes=[F_.r], scale=fscale)
                        if halo:
                            p_ = pb[6]
                            for hi, k1 in enumerate((N - 1, 0)):
                                o_ = p_.t[:, hi:hi + 1]
                                kb.mm(o_, A4[0:N, 0, :, k1], ecs.t[0:N, 0, n_own + hi:n_own + hi + 1], True, False,
                                      reads=[At.r, ecs.r], writes=[p_.r])
                                kb.mm(o_, A4[0:N, 1, :, k1], ecs.t[0:N, 1, n_own + hi:n_own + hi + 1], False, True,
                                      reads=[At.r, ecs.r], writes=[p_.r], signal=(hi == 1))
                            kb.act(F_.t[:, n_own:n_own + 2], p_.t[:, 0:2], AF.Copy, reads=[p_.r], writes=[F_.r],
                                   scale=fscale)
                        kb.dma(POOL, FT_s[g, :, 0:nq], F_.t[:, 0:nq], reads=[F_.r])
                    kb.barrier()

            if "Q" in phases:
                kb.phase = 1 + 6 * seq_idx + 2
                with ExitStack() as es:
                    Wi = sb(es, "Wi_q", [128, 8, 256], BF16)
                    for dc in range(8):
                        kb.dma(SP, Wi.t[:, dc, :], Wi_s[dc, :, 0:256], writes=[Wi.r], accum=True)
                    Wuq = sb(es, "Wuq", [128, 2, 2 * H * QK], BF16)
                    for kc in range(2):
                        kb.dma(SP, Wuq.t[:, kc, :], Wuq_s[kc], writes=[Wuq.r], accum=True)
                    xt = [sb(es, f"xq{i}", [128, D], F32) for i in range(5)]
                    nt = [mk_norm_tiles(es, "q")]
                    for k_ in range(1, 4):
                        nt.append((nt[0][0], sb(es, f"ssq{k_}", [128, 1], F32), sb(es, f"rstdq{k_}", [128, 1], F32),
                                   sb(es, f"hbq{k_}", [128, D], BF16)))
                    hT = [sb(es, f"hTq{i}", [128, 8, 512], BF16) for i in range(2)]
                    fr = (sb(es, "sq_q", [128, 2, 512], F32), pb[2], sb(es, "rs_q", [128, 512], F32),
                          sb(es, "rbc_q", [128, 512], F32))
                    cqn = sb(es, "cqn", [128, 2, 512], BF16)
                    qc = [sb(es, f"qc{i}", [QK, 512], F32) for i in range(2)]
                    qs = [sb(es, f"qs{i}", [QK, 512], F32) for i in range(2)]
                    t1 = [sb(es, f"t1{i}", [QK, 512], F32) for i in range(3)]
                    t2 = [sb(es, f"t2{i}", [QK, 512], F32) for i in range(3)]
                    Qb = [sb(es, f"Qb{i}", [QK, H, 512], BF16) for i in range(2)]
                    tiq = [0]

                    xq_of = {}

                    def prep_q(bi, part):
                        if bi >= len(qblocks):
                            return
                        c0, W = qblocks[bi]
                        hTb = hT[bi % 2]
                        ntile = (W + 127) // 128
                        if part == "norm":
                            xq_of[bi] = []
                        for i in range(ntile):
                            tw = min(128, W - i * 128)
                            if part == "norm":
                                x_ = xt[tiq[0] % 5]
                                tiq[0] += 1
                                xq_of[bi].append(x_)
                                kb.dma(SP, x_.t[0:tw, :], x_own[c0 + i * 128:c0 + i * 128 + tw, :], writes=[x_.r])
                            norm_transpose(nt[i], xq_of[bi][i], tw, hTb, i * 128, part=part)
                    prep_q(0, "norm")
                    prep_q(0, "tr")
                    for bi, (c0, W) in enumerate(qblocks):
                        hTb = hT[bi % 2]
                        prep_q(bi + 1, "norm")
                        qc_, qs_ = qc[bi % 2], qs[bi % 2]
                        kb.dma(SP, qc_.t[:, 0:W], tb["qcos"][:, c0:c0 + W], writes=[qc_.r])
                        kb.dma(SP, qs_.t[:, 0:W], tb["qsin"][:, c0:c0 + W], writes=[qs_.r])
                        for kc in range(2):
                            for dc in range(8):
                                kb.mm(pb[kc].t[:, 0:W], Wi.t[:, dc, kc * 128:(kc + 1) * 128], hTb.t[:, dc, 0:W],
                                      dc == 0, dc == 7, reads=[Wi.r, hTb.r], writes=[pb[kc].r])
                        rbc = fm_rstd(fr, [(pb[0], 128), (pb[1], 128)], W, QR)
                        for kc in range(2):
                            kb.tt(DVE, cqn.t[:, kc, 0:W], pb[kc].t[:, 0:W], rbc.t[:, 0:W], ALU.mult,
                                  reads=[pb[kc].r, rbc.r], writes=[cqn.r])
                        Q_ = Qb[bi % 2]
                        for h in range(H):
                            pq, pr = ((pb[3], pb[4]), (pb[5], pb[6]), (pb[0], pb[1]))[h % 3]
                            for kc in range(2):
                                kb.mm(pq.t[0:QK, 0:W], Wuq.t[:, kc, h * QK:(h + 1) * QK], cqn.t[:, kc, 0:W],
                                      kc == 0, kc == 1, reads=[Wuq.r, cqn.r], writes=[pq.r])
                            for kc in range(2):
                                kb.mm(pr.t[0:QK, 0:W], Wuq.t[:, kc, H * QK + h * QK:H * QK + (h + 1) * QK],
                                      cqn.t[:, kc, 0:W], kc == 0, kc == 1, reads=[Wuq.r, cqn.r], writes=[pr.r])
                            a_, b_ = t1[h % 3], t2[h % 3]
                            kb.tt(DVE, a_.t[:, 0:W], pq.t[0:QK, 0:W], qc_.t[:, 0:W], ALU.mult,
                                  reads=[pq.r, qc_.r], writes=[a_.r])
                            kb.tt(DVE, b_.t[:, 0:W], pr.t[0:QK, 0:W], qs_.t[:, 0:W], ALU.mult,
                                  reads=[pr.r, qs_.r], writes=[b_.r])
                            kb.tt(POOL, Q_.t[:, h, 0:W], a_.t[:, 0:W], b_.t[:, 0:W], ALU.add,
                                  reads=[a_.r, b_.r], writes=[Q_.r])
                        kb.dma(POOL, QT_s[:, :, c0:c0 + W].rearrange("h c s -> c h s"), Q_.t[:, :, 0:W], reads=[Q_.r])
                        prep_q(bi + 1, "tr")
                    kb.barrier()

            if "T" in phases:
                kb.phase = 1 + 6 * seq_idx + 3
                with ExitStack() as es:
                    nch = S // 128
                    KT = [sb(es, f"KT{i}", [QK, S], BF16) for i in range(2)]
                    Vt = [sb(es, f"Vt{i}", [128, nch, VD + 1], BF16) for i in range(2)]
                    Qt = [sb(es, f"Qt{i}", [QK, 512], BF16) for i in range(2)]
                    Osb = [sb(es, f"Osb{i}", [VD + 1, 512], F32) for i in range(2)]
                    rinv = [sb(es, f"rinv{i}", [VD + 1, 512], F32) for i in range(2)]
                    Ab = [sb(es, f"Ab{i}", [VD, 512], BF16) for i in range(2)]
                    Pt2 = [sb(es, f"Pp{i}", [128, 2, 512], BF16) for i in range(3)]
                    SB = []
                    for i in range(3):
                        t_ = T(PSALL[:, 2 * i:2 * i + 2, :])
                        SB.append(t_)
                    psO = [pb[6], pb[7]]
                    npair = nch // 2
                    blocks = [(h, qi) for h in range(H) for qi in range(len(qblocks))]
                    ptasks = [(bi, j) for bi in range(len(blocks)) for j in range(npair)]
                    bst = {}
                    cur_head = [-1]
                    loaded = set()

                    def load_head(hh):
                        if hh < H and hh not in loaded:
                            loaded.add(hh)
                            kb.dma(SP, KT[hh % 2].t[:, :], KT_s[hh, :, 0:S], writes=[KT[hh % 2].r])
                            kb.dma(SP, Vt[hh % 2].t[:, :, :], V_s[hh, :, 0:nch, :], writes=[Vt[hh % 2].r])

                    qloaded = set()

                    def load_q(bi):
                        if bi < len(blocks) and bi not in qloaded:
                            qloaded.add(bi)
                            h_, qi_ = blocks[bi]
                            c0_, W_ = qblocks[qi_]
                            kb.dma(SP, Qt[bi % 2].t[:, 0:W_], QT_s[h_, :, c0_:c0_ + W_], writes=[Qt[bi % 2].r])

                    def ensure_block(bi):
                        if bi in bst:
                            return bst[bi]
                        h, qi = blocks[bi]
                        c0, W = qblocks[qi]
                        KT_, V_ = KT[h % 2], Vt[h % 2]
                        Q_ = Qt[bi % 2]
                        load_q(bi)
                        load_head(h)
                        bst[bi] = dict(h=h, c0=c0, W=W, KT=KT_, V=V_, Q=Q_, pO=psO[bi % 2])
                        return bst[bi]

                    def score_pair(gi):
                        bi, j = ptasks[gi]
                        st = ensure_block(bi)
                        p_ = SB[gi % 3]
                        W = st["W"]
                        for c in range(2):
                            kc = 2 * j + c
                            kb.mm(p_.t[:, c, 0:W], st["KT"].t[:, kc * 128:(kc + 1) * 128], st["Q"].t[:, 0:W], True, True,
                                  reads=[st["KT"].r, st["Q"].r], writes=[p_.r], signal=(c == 1))

                    ntask = len(ptasks)
                    for gi in range(min(2, ntask)):
                        score_pair(gi)
                    pend = []
                    for gi in range(ntask):
                        if gi + 2 < ntask:
                            score_pair(gi + 2)
                        while pend and pend[0][0] <= gi:
                            pend.pop(0)[1]()
                        bi, j = ptasks[gi]
                        st = bst[bi]
                        if j == 0:
                            load_q(bi + 1)
                        if blocks[bi][1] == 0 and j == min(2, npair - 1):
                            load_head(blocks[bi][0] + 1)
                        W, pO, V_ = st["W"], st["pO"], st["V"]
                        p_ = SB[gi % 3]
                        P_ = Pt2[gi % 3]
                        kb.act(P_.t[:, :, 0:W], p_.t[:, :, 0:W], AF.Exp, reads=[p_.r], writes=[P_.r])
                        for c in range(2):
                            kc = 2 * j + c
                            kb.mm(pO.t[0:VD + 1, 0:W], V_.t[:, kc, :], P_.t[:, c, 0:W], kc == 0, kc == nch - 1,
                                  reads=[V_.r, P_.r], writes=[pO.r], signal=(c == 1))
                        if j == npair - 1:
                            O_ = Osb[bi % 2]
                            r_ = rinv[bi % 2]
                            kb.cp(DVE, O_.t[:, 0:W], pO.t[0:VD + 1, 0:W], reads=[pO.r], writes=[O_.r])
                            kb.recip(r_.t[VD:VD + 1, 0:W], O_.t[VD:VD + 1, 0:W], reads=[O_.r], writes=[r_.r])

                            def part2(bi=bi, st=st, O_=O_, r_=r_, W=W, pO=pO):
                                h, c0 = st["h"], st["c0"]
                                kb.mm(pO.t[0:VD, 0:W], onesf.t[VD:VD + 1, 0:VD], r_.t[VD:VD + 1, 0:W], True, True,
                                      reads=[onesf.r, r_.r], writes=[pO.r])
                                A_ = Ab[bi % 2]
                                kb.tt(DVE, A_.t[:, 0:W], pO.t[0:VD, 0:W], O_.t[0:VD, 0:W], ALU.mult,
                                      reads=[pO.r, O_.r], writes=[A_.r])
                                kb.dma(POOL, AT_s[h * VD:(h + 1) * VD, c0:c0 + W], A_.t[:, 0:W], reads=[A_.r])
                            pend.append((gi + 5, part2))
                            del bst[bi]
                    for _, f_ in pend:
                        f_()
                    kb.barrier()

            if "M" in phases:
                kb.phase = 1 + 6 * seq_idx + 4
                with ExitStack() as es:
                    Wg = sb(es, "Wg", [128, 8, 2048], BF16)
                    for dc in range(8):
                        kb.dma(SP, Wg.t[:, dc, :], Wi_s[dc, :, C_G:WI_COLS], writes=[Wg.r], accum=True)
                    Wao = sb(es, "Wao", [128, 4, D], BF16)
                    Wfo = sb(es, "Wfo", [128, 4, D], BF16)
                    Wo = sb(es, "Wo", [128, 8, D], BF16)
                    for c in range(4):
                        kb.dma(SP, Wao.t[:, c, :], Wao_s[c], writes=[Wao.r], accum=True)
                        kb.dma(SP, Wfo.t[:, c, :], Wfo_s[c], writes=[Wfo.r], accum=True)
                    for c in range(8):
                        kb.dma(SP, Wo.t[:, c, :], Wo_s[c], writes=[Wo.r], accum=True)
                    xt = [sb(es, f"xm{i}", [128, D], F32) for i in range(8)]
                    nt = [mk_norm_tiles(es, "m")]
                    for k_ in range(1, 4):
                        nt.append((nt[0][0], sb(es, f"ssm{k_}", [128, 1], F32), sb(es, f"rstdm{k_}", [128, 1], F32),
                                   sb(es, f"hbm{k_}", [128, D], BF16)))
                    nt2 = [mk_norm_tiles(es, "m2", junk=nt[0][0])]
                    for k_ in range(1, 4):
                        nt2.append((nt2[0][0], sb(es, f"ss2{k_}", [128, 1], F32), sb(es, f"rstd2{k_}", [128, 1], F32),
                                    sb(es, f"hb2{k_}", [128, D], BF16)))
                    hT = [sb(es, f"hTm{i}", [128, 8, 512], BF16) for i in range(2)]
                    Gt = sb(es, "Gt", [128, 16, 512], BF16)
                    ATt = [sb(es, f"ATt{i}", [128, 4, 512], BF16) for i in range(1)]
                    FTt = [sb(es, f"FTt{i}", [128, 4, 512], BF16) for i in range(1)]
                    ma = [sb(es, f"ma{i}", [128, 512], F32) for i in range(2)]
                    mf = [sb(es, f"mf{i}", [128, 512], F32) for i in range(2)]
                    mg = sb(es, "mg", [128, 8, 512], BF16)
                    x1 = [sb(es, f"x1{i}", [128, D], F32) for i in range(4)]
                    h2T = [sb(es, f"h2T{i}", [128, 8, 512], BF16) for i in range(2)]
                    if not halo:
                        for dc in range(8):
                            kb.dma(POOL, H2T_s[dc, :, 0:1], zeros.t[:, 0:1], reads=[zeros.r], slow=True)
                            kb.dma(POOL, H2T_s[dc, :, n_own + 1:n_own + 2], zeros.t[:, 0:1], reads=[zeros.r], slow=True)
                    tim = [0]
                    xts_of = {}

                    def prep_m(bi, part):
                        if bi >= len(qblocks):
                            return
                        c0, W = qblocks[bi]
                        hTb = hT[bi % 2]
                        ntile = (W + 127) // 128
                        if part == "norm":
                            xts_of[bi] = []
                        for i in range(ntile):
                            tw = min(128, W - i * 128)
                            if part == "norm":
                                x_ = xt[tim[0] % 8]
                                tim[0] += 1
                                xts_of[bi].append((x_, tw))
                                kb.dma(SP, x_.t[0:tw, :], x_own[c0 + i * 128:c0 + i * 128 + tw, :], writes=[x_.r])
                            norm_transpose(nt[i], xts_of[bi][i][0], tw, hTb, i * 128, part=part)
                    prep_m(0, "norm")
                    prep_m(0, "tr")
                    for bi, (c0, W) in enumerate(qblocks):
                        is_halo = halo and bi == len(qblocks) - 1
                        hTb = hT[bi % 2]
                        prep_m(bi + 1, "norm")
                        xts = xts_of[bi]
                        AT_, FT_ = ATt[0], FTt[0]
                        for c in range(4):
                            kb.dma(SP, AT_.t[:, c, 0:W], AT_s[c * 128:(c + 1) * 128, c0:c0 + W], writes=[AT_.r], accum=(c > 0))
                            kb.dma(SP, FT_.t[:, c, 0:W], FT_s[c, :, c0:c0 + W], writes=[FT_.r], accum=(c > 0))
                        for gc in range(16):
                            p_ = pb[gc % 3]
                            for dc in range(8):
                                kb.mm(p_.t[:, 0:W], Wg.t[:, dc, gc * 128:(gc + 1) * 128], hTb.t[:, dc, 0:W],
                                      dc == 0, dc == 7, reads=[Wg.r, hTb.r], writes=[p_.r])
                            kb.act(Gt.t[:, gc, 0:W], p_.t[:, 0:W], AF.Sigmoid, reads=[p_.r], writes=[Gt.r])
                        for oc in range(8):
                            pa, pf = pb[3 + 2 * (oc % 2)], pb[4 + 2 * (oc % 2)]
                            for c in range(4):
                                kb.mm(pa.t[:, 0:W], Wao.t[:, c, oc * 128:(oc + 1) * 128], AT_.t[:, c, 0:W],
                                      c == 0, c == 3, reads=[Wao.r, AT_.r], writes=[pa.r])
                            for c in range(4):
                                kb.mm(pf.t[:, 0:W], Wfo.t[:, c, oc * 128:(oc + 1) * 128], FT_.t[:, c, 0:W],
                                      c == 0, c == 3, reads=[Wfo.r, FT_.r], writes=[pf.r])
                            a_, f_ = ma[oc % 2], mf[oc % 2]
                            kb.tt(DVE, a_.t[:, 0:W], pa.t[:, 0:W], Gt.t[:, oc, 0:W], ALU.mult,
                                  reads=[pa.r, Gt.r], writes=[a_.r])
                            kb.tt(DVE, f_.t[:, 0:W], pf.t[:, 0:W], Gt.t[:, 8 + oc, 0:W], ALU.mult,
                                  reads=[pf.r, Gt.r], writes=[f_.r])
                            kb.tt(POOL, mg.t[:, oc, 0:W], a_.t[:, 0:W], f_.t[:, 0:W], ALU.add,
                                  reads=[a_.r, f_.r], writes=[mg.r])
                        h2b = h2T[bi % 2]
                        for i, (x_, tw) in enumerate(xts):
                            x1_ = x1[i % 4]
                            for hf in range(2):
                                p_ = pb[hf + 2 * (i % 2)]
                                for kc in range(8):
                                    kb.mm(p_.t[0:tw, :], mg.t[:, kc, i * 128:i * 128 + tw], Wo.t[:, kc, hf * 512:(hf + 1) * 512],
                                          kc == 0, kc == 7, reads=[mg.r, Wo.r], writes=[p_.r])
                                kb.tt(DVE, x1_.t[0:tw, hf * 512:(hf + 1) * 512], p_.t[0:tw, :],
                                      x_.t[0:tw, hf * 512:(hf + 1) * 512], ALU.add, reads=[p_.r, x_.r], writes=[x1_.r])
                            if not is_halo:
                                kb.dma(POOL, X1_s[c0 + i * 128:c0 + i * 128 + tw, :], x1_.t[0:tw, :], reads=[x1_.r])
                            norm_transpose(nt2[i % 4], x1_, tw, h2b, i * 128, part="norm")
                        prep_m(bi + 1, "tr")
                        for i, (x_, tw) in enumerate(xts):
                            norm_transpose(nt2[i % 4], x1[i % 4], tw, h2b, i * 128, part="tr")
                        if is_halo:
                            kb.tt(DVE, h2b.t[:, :, 0:2], h2b.t[:, :, 0:2],
                                  mask.t[:, :].unsqueeze(1).to_broadcast([128, 8, 2]), ALU.mult,
                                  reads=[h2b.r, mask.r], writes=[h2b.r])
                            kb.dma(POOL, H2T_s[:, :, 0:1].rearrange("c p s -> p c s"), h2b.t[:, :, 0:1], reads=[h2b.r], slow=True)
                            kb.dma(POOL, H2T_s[:, :, n_own + 1:n_own + 2].rearrange("c p s -> p c s"), h2b.t[:, :, 1:2],
                                   reads=[h2b.r], slow=True)
                        else:
                            kb.dma(POOL, H2T_s[:, :, 1 + c0:1 + c0 + W].rearrange("c p s -> p c s"), h2b.t[:, :, 0:W],
                                   reads=[h2b.r])
                    kb.barrier()

            if "N" in phases:
                kb.phase = 1 + 6 * seq_idx + 5
                with ExitStack() as es:
                    Wup = sb(es, "Wup", [128, 8, 2 * DFF], BF16)
                    Wdn = sb(es, "Wdn", [128, NHC, D], BF16)
                    NWB = 11
                    wup_r = [Res() for _ in range(NWB)]
                    order = []
                    for cp_ in range(NHC):
                        for blk_ in (cp_ // 4, (cp_ + NHC) // 4):
                            if blk_ not in order:
                                order.append(blk_)
                    for blk_ in order:
                        for c in range(8):
                            kb.dma(SP, Wup.t[:, c, blk_ * 512:(blk_ + 1) * 512], Wup_s[c, :, blk_ * 512:(blk_ + 1) * 512],
                                   writes=[wup_r[blk_]], accum=True)
                    for c in range(NHC):
                        kb.dma(SP, Wdn.t[:, c, :], Wdn_s[c], writes=[Wdn.r], accum=True)
                    cw = sb(es, "cw", [128, NFC, 3], F32)
                    cbias = sb(es, "cbias", [128, NFC], F32)
                    gfin = sb(es, "gfin", [128, D], F32)
                    kb.dma(SP, cw.t[:], convw_d[:, :, :], writes=[cw.r])
                    kb.dma(SP, cbias.t[:], convb_d[:, :], writes=[cbias.r])
                    kb.dma(SP, gfin.t[:], gfin_d[:, :], writes=[gfin.r])
                    h2 = [sb(es, f"h2n{i}", [128, 8, 512], BF16) for i in range(2)]
                    acc = [sb(es, f"acc{i}", [128, 512], F32) for i in range(4)]
                    sg = [sb(es, f"sg{i}", [128, 512], F32) for i in range(2)]
                    actT = sb(es, "actT", [128, NHC, 512], BF16)
                    x1t = [sb(es, f"x1n{i}", [128, D], F32) for i in range(2)]
                    yo = [sb(es, f"yo{i}", [128, D], F32) for i in range(2)]
                    ss = [sb(es, f"ssn{i}", [128, 1], F32) for i in range(2)]
                    rstd = [sb(es, f"rstdn{i}", [128, 1], F32) for i in range(2)]
                    nb = (n_own + FB - 1) // FB
                    ti = 0
                    ai = 0
                    for b in range(nb):
                        t0 = b * FB
                        Wb = min(FB, n_own - t0)
                        h2_ = h2[b % 2]
                        kb.dma(SP, h2_.t[:, :, 0:Wb + 2], H2T_s[:, :, t0:t0 + Wb + 2].rearrange("c p s -> p c s"),
                               writes=[h2_.r])
                        for cp_ in range(NHC):
                            accs = []
                            for half, ch in enumerate((cp_, cp_ + NHC)):
                                p_ = pb[(2 * cp_ + half) % 4]
                                for dc in range(8):
                                    kb.mm(p_.t[:, 0:Wb + 2], Wup.t[:, dc, ch * 128:(ch + 1) * 128], h2_.t[:, dc, 0:Wb + 2],
                                          dc == 0, dc == 7, reads=[wup_r[ch // 4], h2_.r], writes=[p_.r])
                                a_ = acc[ai % 4]
                                ai += 1
                                kb.act(a_.t[:, 0:Wb], p_.t[:, 1:Wb + 1], AF.Identity, reads=[p_.r, cw.r, cbias.r],
                                       writes=[a_.r], scale=cw.t[:, ch, 1:2], bias=cbias.t[:, ch:ch + 1])
                                kb.stt(DVE, a_.t[:, 0:Wb], p_.t[:, 0:Wb], cw.t[:, ch, 0:1], a_.t[:, 0:Wb], ALU.mult, ALU.add,
                                       reads=[p_.r, a_.r, cw.r], writes=[a_.r])
                                kb.stt(DVE, a_.t[:, 0:Wb], p_.t[:, 2:Wb + 2], cw.t[:, ch, 2:3], a_.t[:, 0:Wb], ALU.mult, ALU.add,
                                       reads=[p_.r, a_.r, cw.r], writes=[a_.r])
                                accs.append(a_)
                            s_ = sg[cp_ % 2]
                            kb.act(s_.t[:, 0:Wb], accs[0].t[:, 0:Wb], AF.Silu, reads=[accs[0].r], writes=[s_.r])
                            kb.tt(POOL, actT.t[:, cp_, 0:Wb], s_.t[:, 0:Wb], accs[1].t[:, 0:Wb], ALU.mult,
                                  reads=[s_.r, accs[1].r], writes=[actT.r])
                        ntile = (Wb + 127) // 128
                        for i in range(ntile):
                            tw = min(128, Wb - i * 128)
                            r0 = t0 + i * 128
                            x1_ = x1t[ti % 2]
                            x2_ = x1_
                            y_ = yo[ti % 2]
                            ss_ = ss[ti % 2]
                            rs_ = rstd[ti % 2]
                            kb.dma(SP, x1_.t[0:tw, :], X1_s[r0:r0 + tw, :], writes=[x1_.r])
                            for hf in range(2):
                                p_ = pb[4 + hf]
                                for c in range(NHC):
                                    kb.mm(p_.t[0:tw, :], actT.t[:, c, i * 128:i * 128 + tw], Wdn.t[:, c, hf * 512:(hf + 1) * 512],
                                          c == 0, c == NHC - 1, reads=[actT.r, Wdn.r], writes=[p_.r])
                                kb.tt(DVE, x2_.t[0:tw, hf * 512:(hf + 1) * 512], p_.t[0:tw, :],
                                      x1_.t[0:tw, hf * 512:(hf + 1) * 512], ALU.add, reads=[p_.r, x1_.r], writes=[x2_.r])
                            kb.act(y_.t[0:tw, :], x2_.t[0:tw, :], AF.Square, reads=[x2_.r], writes=[y_.r, ss_.r],
                                   accum_out=ss_.t[0:tw, :])
                            kb.act(ss_.t[0:tw, :], ss_.t[0:tw, :], AF.Sqrt, reads=[ss_.r, epsq.r], writes=[ss_.r],
                                   bias=epsq.t[0:tw, 0:1], scale=1.0 / D)
                            kb.recip(rs_.t[0:tw, :], ss_.t[0:tw, :], reads=[ss_.r], writes=[rs_.r])
                            kb.stt(DVE, y_.t[0:tw, :], x2_.t[0:tw, :], rs_.t[0:tw, 0:1], gfin.t[0:tw, :], ALU.mult, ALU.mult,
                                   reads=[x2_.r, rs_.r, gfin.r], writes=[y_.r])
                            kb.dma(POOL, y_out[r0:r0 + tw, :], y_.t[0:tw, :], reads=[y_.r])
                            ti += 1
                    kb.barrier()

        if cfg.get("do_prompt", True):
            run_sequence(0, "p", SP_, NP_, xp_full, xp_own, OWNP, True, yp)
        for si in range(cfg.get("n_run_seq", NSEQ)):
            run_sequence(1 + si, "s", SS_, NS_, xs[si], xs[si], SS_, False, ys[si])
        kb.barrier()
    return nc


def _bf(a):
    return np.ascontiguousarray(a.astype(ml_dtypes.bfloat16))


def host_tables(S, N, own_pos, scale):
    inv = (10000.0 ** (-np.arange(0, ROPE, 2, dtype=np.float32) / ROPE)).astype(np.float32)
    tq = np.arange(S)
    posA = (tq % N) * N + tq // N
    angK = posA[None, :].astype(np.float32) * np.concatenate([inv, inv])[:, None]
    kcs = np.concatenate([np.cos(angK), np.sin(angK)], 0).astype(np.float32)
    angQ = np.asarray(own_pos, np.float32)[None, :] * np.concatenate([inv, inv])[:, None]
    nq = len(own_pos)
    qcos = np.concatenate([np.full((NOPE, nq), scale, np.float32), scale * np.cos(angQ)], 0).astype(np.float32)
    qsin = np.concatenate([np.zeros((NOPE, nq), np.float32), scale * np.sin(angQ)], 0).astype(np.float32)
    n2 = np.arange(N, dtype=np.float64)[:, None]
    ph = 2 * np.pi * n2 * np.asarray(own_pos, np.float64)[None, :] / S
    ecs = np.stack([np.cos(ph), np.sin(ph)], 1)
    th = 2 * np.pi * np.outer(np.arange(N), np.arange(N)) / N
    ca = np.concatenate([np.cos(th), -np.sin(th)], 1)
    cb = np.concatenate([np.sin(th), np.cos(th)], 1)
    return dict(kcs=kcs, qcos=qcos, qsin=qsin, ecs=_bf(ecs), ca=_bf(ca), cb=_bf(cb))


def host_consts():
    th = 2 * np.pi * np.outer(np.arange(128), np.arange(128)) / 128
    cs128 = np.concatenate([np.cos(th), -np.sin(th)], 1)
    eplace = np.zeros((64, QK), np.float32)
    for r in range(32):
        eplace[r, 64 + r] = 1.0
        eplace[32 + r, 64 + r] = 1.0
    return dict(cs128=_bf(cs128), ident=_bf(np.eye(128, dtype=np.float32)), eplace=_bf(eplace))


def weight_maps(g_mix, w_in, g_q, w_uq, g_kv, w_ukv, w_attn_out, w_fourier_out, w_out, g_ffn, w_up,
                conv_w, conv_b, w_down, g_final):
    c = np.ascontiguousarray
    f = np.float32
    return dict(
        w_in=c(w_in[0], f), w_uq=c(w_uq[0], f), w_ukv=c(w_ukv[0], f), w_ao=c(w_attn_out[0], f),
        w_fo=c(w_fourier_out[0], f), w_out=c(w_out[0], f), w_up=c(w_up[0], f), w_dn=c(w_down[0], f),
        g_mix8=c(g_mix[0].reshape(8, 128).T, f), g_q2=c(g_q[0].reshape(2, 128).T, f),
        g_kv1=c(g_kv[0].reshape(1, 128).T, f), g_ffn8=c(g_ffn[0].reshape(8, 128).T, f),
        convw=c(conv_w[0].reshape(3, NFC, 128).transpose(2, 1, 0), f),
        convb=c(conv_b[0].reshape(NFC, 128).T, f),
        gfin=c(np.broadcast_to(g_final.reshape(1, D), (128, D)), f))


_NC_CACHE = {}


def run(cfg, x_prompt, x_sample, wm, n_cores=8):
    SP_, SS_, NSEQ = cfg["SP"], cfg["SS"], cfg["NSEQ"]
    OWNP = SP_ // 4
    key = tuple(sorted((k, str(v)) for k, v in cfg.items()))
    if key not in _NC_CACHE:
        _NC_CACHE[key] = build(cfg)
    nc = _NC_CACHE[key]
    scale = float(QK) ** -0.5
    consts = host_consts()
    ts = host_tables(SS_, cfg["NS"], np.arange(SS_), scale)
    in_maps = []
    for c in range(n_cores):
        b, j = c // 4, c % 4
        own_pos = list(range(OWNP * j, OWNP * (j + 1))) + [OWNP * j - 1, OWNP * (j + 1)]
        tp = host_tables(SP_, cfg["NP"], own_pos, scale)
        xo = np.zeros((OWNP + 2, D), np.float32)
        xo[:OWNP] = x_prompt[b, OWNP * j:OWNP * (j + 1)]
        mask = np.zeros((128, 2), np.float32)
        if j > 0:
            xo[OWNP] = x_prompt[b, OWNP * j - 1]
            mask[:, 0] = 1.0
        if j < 3:
            xo[OWNP + 1] = x_prompt[b, OWNP * (j + 1)]
            mask[:, 1] = 1.0
        m = dict(xp_full=np.ascontiguousarray(x_prompt[b]), xp_own=xo,
                 xs=np.ascontiguousarray(x_sample[NSEQ * c:NSEQ * (c + 1)]), mask=mask)
        for k, v in tp.items():
            m[f"{k}_p"] = v
        for k, v in ts.items():
            m[f"{k}_s"] = v
        m.update(consts)
        m.update(wm)
        in_maps.append(m)
    res = run_bass_kernel_spmd(nc, in_maps, core_ids=list(range(n_cores)))
    return res.results


FULL_CFG = dict(SP=16384, NP=128, SS=4096, NS=64, NSEQ=2)


def kernel(x_prompt, x_sample, g_mix, w_in, g_q, w_uq, g_kv, w_ukv, w_attn_out, w_fourier_out, w_out,
           g_ffn, w_up, conv_w, conv_b, w_down, g_final):
    cfg = FULL_CFG
    x_prompt = np.asarray(x_prompt, np.float32)
    x_sample = np.asarray(x_sample, np.float32)
    wm = weight_maps(*[np.asarray(a, np.float32) for a in (g_mix, w_in, g_q, w_uq, g_kv, w_ukv, w_attn_out,
                                                            w_fourier_out, w_out, g_ffn, w_up, conv_w, conv_b,
                                                            w_down, g_final)])
    results = run(cfg, x_prompt, x_sample, wm)
    OWNP = cfg["SP"] // 4
    yp = np.zeros_like(x_prompt)
    ysm = np.zeros_like(x_sample)
    for c in range(8):
        b, j = c // 4, c % 4
        yp[b, OWNP * j:OWNP * (j + 1)] = results[c]["yp"]
        ysm[cfg["NSEQ"] * c:cfg["NSEQ"] * (c + 1)] = results[c]["ys"]
    return (yp, ysm)
```

```python
import numpy as np
import ml_dtypes
from contextlib import ExitStack
import concourse.bass as bass
import concourse.mybir as mybir
from concourse.bass_utils import run_bass_kernel_spmd

F32 = mybir.dt.float32
BF16 = mybir.dt.bfloat16
AF = mybir.ActivationFunctionType
ALU = mybir.AluOpType

D = 1024
H = 8
QR = 256
KVR = 128
NOPE = 64
ROPE = 32
VD = 64
QK = NOPE + ROPE
FG = 4
DFF = 2816
NFC = 2 * DFF // 128
NHC = DFF // 128
EPS = 1e-6
IN_COLS = 2976
C_CQ, C_CKV, C_KR, C_KRR, C_XF, C_G, WI_COLS = 0, 256, 384, 416, 448, 960, 3008
FB = 510


from functools import partial

TRAMP = [
    lambda f: f(),
    lambda f: f(),
    lambda f: f(),
    lambda f: f(),
    lambda f: f(),
    lambda f: f(),
    lambda f: f(),
    lambda f: f(),
    lambda f: f(),
    lambda f: f(),
    lambda f: f(),
    lambda f: f(),
    lambda f: f(),
    lambda f: f(),
    lambda f: f(),
    lambda f: f(),
    lambda f: f(),
    lambda f: f(),
    lambda f: f(),
    lambda f: f(),
]


class Res:
    __slots__ = ("w", "r")

    def __init__(self):
        self.w = {}
        self.r = {}


class Eng:
    def __init__(self, nc, es, name, eng, sem=True):
        self.e = eng
        self.key = name
        self.sem = es.enter_context(nc.semaphore(name)) if sem else None
        self.cnt = 0
        self.seen = {}
        self.dsem = []
        self.di = 0


class DSem:
    def __init__(self, nc, es, name):
        self.key = name
        self.sem = es.enter_context(nc.semaphore(name))
        self.cnt = 0


class KB:
    def __init__(self, nc, es, ndma=20):
        self.nc = nc
        self.phase = 0
        self.PE = Eng(nc, es, "sPE", nc.tensor)
        self.ACT = Eng(nc, es, "sACT", nc.scalar)
        self.DVE = Eng(nc, es, "sDVE", nc.vector)
        self.POOL = Eng(nc, es, "sPOOL", nc.gpsimd)
        self.SP = Eng(nc, es, "sSP", nc.sync, sem=False)
        self.engs = [self.PE, self.ACT, self.DVE, self.POOL, self.SP]
        for q in (self.SP, self.POOL):
            q.dsem = [DSem(nc, es, f"d{q.key}{i}") for i in range(ndma)]

    def _wait(self, E, deps):
        for key, (sem, val) in deps.items():
            if key == E.key and val > E.cnt:
                continue
            if E.seen.get(key, 0) < val:
                E.e.wait_ge(sem, val)
                E.seen[key] = val

    @staticmethod
    def _gather(reads, writes, accum):
        deps = {}

        def add(d):
            for k, ev in d.items():
                if k not in deps or deps[k][1] < ev[1]:
                    deps[k] = ev
        for r in reads:
            add(r.w)
        for w in writes:
            if not accum:
                add(w.w)
            add(w.r)
        return deps

    def op(self, E, fn, reads=(), writes=(), signal=True):
        deps = self._gather(reads, writes, False)
        self._wait(E, deps)
        ins = TRAMP[self.phase](fn)
        val = E.cnt + 1
        if signal:
            ins.then_inc(E.sem, 1)
            E.cnt = val
        ev = (E.sem, val)
        for r in reads:
            if r.r.get(E.key, (None, 0))[1] < val:
                r.r[E.key] = ev
        for w in writes:
            w.w = {E.key: ev}
            w.r = {}
        return ins

    def dma(self, Q, out, in_, reads=(), writes=(), accum=False, slow=False):
        slot = Q.dsem[Q.di % len(Q.dsem)]
        Q.di += 1
        deps = self._gather(reads, writes, accum)
        if slot.cnt > 0:
            deps[slot.key] = (slot.sem, slot.cnt)
        self._wait(Q, deps)
        ins = TRAMP[self.phase](partial(Q.e.dma_start, out=out, in_=in_, allow_slow_non_contiguous=True) if slow
                                else partial(Q.e.dma_start, out=out, in_=in_))
        slot.cnt += 16
        ins.then_inc(slot.sem, 16)
        ev = (slot.sem, slot.cnt)
        for r in reads:
            r.r[slot.key] = ev
        for w in writes:
            if accum:
                w.w[slot.key] = ev
            else:
                w.w = {slot.key: ev}
                w.r = {}
        return ins

    def barrier(self):
        tg = {}
        for E in self.engs:
            if E.sem is not None and E.cnt > 0:
                tg[E.key] = (E.sem, E.cnt)
            for s in E.dsem:
                if s.cnt > 0:
                    tg[s.key] = (s.sem, s.cnt)
        for E in self.engs:
            self._wait(E, tg)

    def mm(self, out, lhsT, rhs, start, stop, reads=(), writes=(), signal=None):
        if signal is None:
            signal = stop
        return self.op(self.PE, partial(self.nc.tensor.matmul, out, lhsT=lhsT, rhs=rhs, start=start, stop=stop),
                       reads, writes, signal)

    def tr(self, out, in_, ident, reads=(), writes=(), signal=True):
        return self.op(self.PE, partial(self.nc.tensor.transpose, out, in_, ident), reads, writes, signal)

    def act(self, out, in_, func, reads=(), writes=(), **kw):
        return self.op(self.ACT, partial(self.nc.scalar.activation, out=out, in_=in_, func=func, **kw), reads, writes)

    def tt(self, E, out, in0, in1, op, reads=(), writes=()):
        return self.op(E, partial(E.e.tensor_tensor, out=out, in0=in0, in1=in1, op=op), reads, writes)

    def ts(self, E, out, in0, s1, s2, op0, op1=None, reads=(), writes=()):
        if op1 is None:
            return self.op(E, partial(E.e.tensor_scalar, out=out, in0=in0, scalar1=s1, scalar2=None, op0=op0),
                           reads, writes)
        return self.op(E, partial(E.e.tensor_scalar, out=out, in0=in0, scalar1=s1, scalar2=s2, op0=op0, op1=op1),
                       reads, writes)

    def stt(self, E, out, in0, scalar, in1, op0, op1, reads=(), writes=()):
        return self.op(E, partial(E.e.scalar_tensor_tensor, out=out, in0=in0, scalar=scalar, in1=in1, op0=op0, op1=op1),
                       reads, writes)

    def cp(self, E, out, in_, reads=(), writes=()):
        if E is self.ACT:
            return self.op(E, partial(E.e.activation, out=out, in_=in_, func=AF.Copy), reads, writes)
        return self.op(E, partial(E.e.tensor_copy, out=out, in_=in_), reads, writes)

    def memset(self, E, ap, v, writes=()):
        return self.op(E, partial(E.e.memset, ap, v), (), writes)

    def recip(self, out, in_, reads=(), writes=()):
        return self.op(self.DVE, partial(self.nc.vector.reciprocal, out=out, in_=in_), reads, writes)


class T:
    def __init__(self, t):
        self.t = t
        self.r = Res()


def build(cfg):
    SP_, NP_, SS_, NS_, NSEQ = cfg["SP"], cfg["NP"], cfg["SS"], cfg["NS"], cfg["NSEQ"]
    OWNP = SP_ // 4
    debug = cfg.get("debug", False)
    phases = cfg.get("phases", "AFQTMN")
    nc = bass.Bass("TRN2", target_bir_lowering=False)

    def din(name, shape, dt=F32):
        return nc.dram_tensor(name, list(shape), dt, kind="ExternalInput").ap()

    def dscr(name, shape, dt=BF16):
        return nc.dram_tensor(name, list(shape), dt, kind=("ExternalOutput" if debug else "Internal")).ap()

    xp_full = din("xp_full", [SP_, D])
    xp_own = din("xp_own", [OWNP + 2, D])
    xs = din("xs", [NSEQ, SS_, D])
    tabs = {}
    for tag, S, N, nq in (("p", SP_, NP_, OWNP + 2), ("s", SS_, NS_, SS_)):
        tabs[tag] = dict(
            kcs=din(f"kcs_{tag}", [64, S]), qcos=din(f"qcos_{tag}", [QK, nq]), qsin=din(f"qsin_{tag}", [QK, nq]),
            ecs=din(f"ecs_{tag}", [N, 2, nq], BF16),
            ca=din(f"ca_{tag}", [N, 2 * N], BF16), cb=din(f"cb_{tag}", [N, 2 * N], BF16))
    cs128_d = din("cs128", [128, 256], BF16)
    ident_d = din("ident", [128, 128], BF16)
    eplace_d = din("eplace", [64, QK], BF16)
    mask_d = din("mask", [128, 2])
    w_in_d = din("w_in", [D, IN_COLS])
    w_uq_d = din("w_uq", [QR, H * QK])
    w_ukv_d = din("w_ukv", [KVR, H * 128])
    w_ao_d = din("w_ao", [512, D])
    w_fo_d = din("w_fo", [512, D])
    w_out_d = din("w_out", [D, D])
    w_up_d = din("w_up", [D, 2 * DFF])
    w_dn_d = din("w_dn", [DFF, D])
    g_mix_d = din("g_mix8", [128, 8])
    g_q_d = din("g_q2", [128, 2])
    g_kv_d = din("g_kv1", [128, 1])
    g_ffn_d = din("g_ffn8", [128, 8])
    convw_d = din("convw", [128, NFC, 3])
    convb_d = din("convb", [128, NFC])
    gfin_d = din("gfin", [128, D])

    yp = nc.dram_tensor("yp", [OWNP, D], F32, kind="ExternalOutput").ap()
    ys = nc.dram_tensor("ys", [NSEQ, SS_, D], F32, kind="ExternalOutput").ap()

    Wi_s = dscr("Wi_s", [8, 128, WI_COLS])
    Wuq_s = dscr("Wuq_s", [2, 128, 2 * H * QK])
    Wkv_s = dscr("Wkv_s", [128, H * QK + H * VD])
    Wao_s = dscr("Wao_s", [4, 128, D])
    Wfo_s = dscr("Wfo_s", [4, 128, D])
    Wo_s = dscr("Wo_s", [8, 128, D])
    Wup_s = dscr("Wup_s", [8, 128, 2 * DFF])
    Wdn_s = dscr("Wdn_s", [NHC, 128, D])
    SMAX = max(SP_, SS_)
    NQMAX = max(OWNP + 2, SS_ + 2)
    KT_s = dscr("KT_s", [H, QK, SMAX])
    V_s = dscr("V_s", [H, 128, SMAX // 128, VD + 1])
    Y_s = dscr("Y_s", [FG, 2, 128 * SMAX])
    FT_s = dscr("FT_s", [FG, 128, NQMAX])
    QT_s = dscr("QT_s", [H, QK, NQMAX])
    AT_s = dscr("AT_s", [H * VD, NQMAX])
    X1_s = dscr("X1_s", [NQMAX, D], F32)
    H2T_s = dscr("H2T_s", [8, 128, NQMAX])

    es_top = ExitStack()
    with es_top:
        kb = KB(nc, es_top)
        PE, ACT, DVE, POOL, SP = kb.PE, kb.ACT, kb.DVE, kb.POOL, kb.SP

        uid = [0]

        def sb(es, name, shape, dt):
            uid[0] += 1
            return T(es.enter_context(nc.sbuf_tensor(f"sb{uid[0]}_{name}", list(shape), dt)))

        def ps(es, name, shape, dt):
            uid[0] += 1
            return T(es.enter_context(nc.psum_tensor(f"ps{uid[0]}_{name}", list(shape), dt)))

        ident = sb(es_top, "ident", [128, 128], BF16)
        eplace = sb(es_top, "eplace", [64, QK], BF16)
        cs128 = sb(es_top, "cs128", [128, 256], BF16)
        onesf = sb(es_top, "onesf", [128, 128], F32)
        epsq = sb(es_top, "epsq", [128, 1], F32)
        mask = sb(es_top, "mask", [128, 2], F32)
        zeros = sb(es_top, "zeros", [128, 16], BF16)
        kb.dma(SP, ident.t[:], ident_d[:, :], writes=[ident.r])
        kb.dma(SP, eplace.t[:], eplace_d[:, :], writes=[eplace.r])
        kb.dma(SP, cs128.t[:], cs128_d[:, :], writes=[cs128.r])
        kb.dma(SP, mask.t[:], mask_d[:, :], writes=[mask.r])
        kb.memset(DVE, onesf.t[:], 1.0, writes=[onesf.r])
        kb.memset(DVE, epsq.t[:], EPS, writes=[epsq.r])
        kb.memset(DVE, zeros.t[:], 0.0, writes=[zeros.r])
        PSALL = es_top.enter_context(nc.psum_tensor("psall", [128, 8, 512], F32))
        pb = [T(PSALL[:, i, :]) for i in range(8)]
        pT = T(PSALL[:, 7, :].bitcast(BF16))
        pT.r = pb[7].r
        kb.barrier()

        if "W" in phases or True:
            with ExitStack() as es:
                stg = [sb(es, f"stg{i}", [128, 2 * DFF], F32) for i in range(2)]
                wo = [sb(es, f"wo{i}", [128, 2 * DFF], BF16) for i in range(2)]
                gm = sb(es, "gm", [128, 8], F32)
                gq = sb(es, "gq", [128, 2], F32)
                gkv = sb(es, "gkv", [128, 1], F32)
                gf = sb(es, "gf", [128, 8], F32)
                for t_, d_ in ((gm, g_mix_d), (gq, g_q_d), (gkv, g_kv_d), (gf, g_ffn_d)):
                    kb.dma(SP, t_.t[:], d_[:, :], writes=[t_.r])
                cnt = [0]

                def conv(src, ncols, dst, fn):
                    i = cnt[0] % 2
                    cnt[0] += 1
                    kb.dma(SP, stg[i].t[:, 0:ncols], src, writes=[stg[i].r])
                    fn(stg[i], wo[i])
                    kb.dma(POOL, dst, wo[i].t[:, 0:dst.shape[-1]], reads=[wo[i].r])

                def scaled(E, o, i_, sc, rd, wr, neg=False):
                    if sc is None:
                        kb.cp(E, o, i_, reads=rd, writes=wr)
                    elif neg:
                        kb.ts(E, o, i_, sc, -1.0, ALU.mult, ALU.mult, reads=rd, writes=wr)
                    else:
                        kb.ts(E, o, i_, sc, None, ALU.mult, reads=rd, writes=wr)

                for dc in range(8):
                    def f_in(s_, o_, dc=dc):
                        sc = gm.t[:, dc:dc + 1]
                        rd, wr = [s_.r, gm.r], [o_.r]
                        scaled(DVE, o_.t[:, 0:416], s_.t[:, 0:416], sc, rd, wr)
                        scaled(DVE, o_.t[:, C_KRR:C_KRR + 16], s_.t[:, 400:416], sc, rd, wr, neg=True)
                        scaled(DVE, o_.t[:, C_KRR + 16:C_KRR + 32], s_.t[:, 384:400], sc, rd, wr)
                        scaled(DVE, o_.t[:, C_XF:WI_COLS], s_.t[:, 416:IN_COLS], sc, rd, wr)
                    conv(w_in_d[dc * 128:(dc + 1) * 128, :], IN_COLS, Wi_s[dc], f_in)
                for kc in range(2):
                    def f_uq(s_, o_, kc=kc):
                        sc = gq.t[:, kc:kc + 1]
                        rd, wr = [s_.r, gq.r], [o_.r]
                        n = H * QK
                        scaled(DVE, o_.t[:, 0:n], s_.t[:, 0:n], sc, rd, wr)
                        kb.memset(DVE, o_.t[:, n:2 * n], 0.0, writes=wr)
                        s3 = s_.t[:, 0:n].rearrange("p (h c) -> p h c", c=QK)
                        o3 = o_.t[:, n:2 * n].rearrange("p (h c) -> p h c", c=QK)
                        scaled(DVE, o3[:, :, 64:80], s3[:, :, 80:96], sc, rd, wr, neg=True)
                        scaled(DVE, o3[:, :, 80:96], s3[:, :, 64:80], sc, rd, wr)
                    conv(w_uq_d[kc * 128:(kc + 1) * 128, :], H * QK, Wuq_s[kc], f_uq)

                def f_kv(s_, o_):
                    sc = gkv.t[:, 0:1]
                    rd, wr = [s_.r, gkv.r], [o_.r]
                    kb.memset(DVE, o_.t[:, 0:H * QK], 0.0, writes=wr)
                    s3 = s_.t[:, 0:H * 128].rearrange("p (h c) -> p h c", c=128)
                    ok = o_.t[:, 0:H * QK].rearrange("p (h c) -> p h c", c=QK)
                    ov = o_.t[:, H * QK:H * QK + H * VD].rearrange("p (h c) -> p h c", c=VD)
                    scaled(DVE, ok[:, :, 0:64], s3[:, :, 0:64], sc, rd, wr)
                    scaled(DVE, ov, s3[:, :, 64:128], sc, rd, wr)
                conv(w_ukv_d[:, :], H * 128, Wkv_s[:, :], f_kv)

                def plain(sc_t=None, k=None):
                    def f(s_, o_):
                        n = o_.cur
                        if sc_t is None:
                            kb.cp(DVE, o_.t[:, 0:n], s_.t[:, 0:n], reads=[s_.r], writes=[o_.r])
                        else:
                            scaled(DVE, o_.t[:, 0:n], s_.t[:, 0:n], sc_t.t[:, k:k + 1], [s_.r, sc_t.r], [o_.r])
                    return f
                for src, dst, nch, ncols, sct in ((w_ao_d, Wao_s, 4, D, None), (w_fo_d, Wfo_s, 4, D, None),
                                                   (w_out_d, Wo_s, 8, D, None), (w_up_d, Wup_s, 8, 2 * DFF, gf),
                                                   (w_dn_d, Wdn_s, NHC, D, None)):
                    for c in range(nch):
                        for w_ in wo:
                            w_.cur = ncols
                        conv(src[c * 128:(c + 1) * 128, :], ncols, dst[c], plain(sct, c))
                kb.barrier()

        def norm_transpose(es_tiles, xt, tw, hT, col0, part="both"):
            junk, ss, rstd, hb = es_tiles
            if part in ("both", "norm"):
                kb.act(junk.t[0:tw, :], xt.t[0:tw, :], AF.Square, reads=[xt.r], writes=[junk.r, ss.r],
                       accum_out=ss.t[0:tw, :])
                kb.act(ss.t[0:tw, :], ss.t[0:tw, :], AF.Sqrt, reads=[ss.r, epsq.r], writes=[ss.r],
                       bias=epsq.t[0:tw, 0:1], scale=1.0 / D)
                kb.recip(rstd.t[0:tw, :], ss.t[0:tw, :], reads=[ss.r], writes=[rstd.r])
                kb.ts(DVE, hb.t[0:tw, :], xt.t[0:tw, :], rstd.t[0:tw, 0:1], None, ALU.mult,
                      reads=[xt.r, rstd.r], writes=[hb.r])
            if part == "norm":
                return rstd
            for dc in range(8):
                kb.tr(pT.t[:, dc * 128:dc * 128 + tw], hb.t[0:tw, dc * 128:(dc + 1) * 128], ident.t[0:tw, 0:tw],
                      reads=[hb.r, ident.r], writes=[pT.r], signal=(dc == 7))
            kb.cp(DVE, hT.t[:, :, col0:col0 + tw],
                  pT.t[:, :].rearrange("p (c t) -> p c t", t=128)[:, :, 0:tw],
                  reads=[pT.r], writes=[hT.r])
            return rstd

        def mk_norm_tiles(es, tag, junk=None):
            return (junk if junk is not None else sb(es, f"junk{tag}", [128, D], BF16), sb(es, f"ss{tag}", [128, 1], F32),
                    sb(es, f"rstd{tag}", [128, 1], F32), sb(es, f"hb{tag}", [128, D], BF16))

        def fm_rstd(es_t, src_list, W, nfeat):
            sq, pbc, rs, rbc = es_t
            for i, (p_, npart) in enumerate(src_list):
                kb.act(sq.t[0:npart, i, 0:W], p_.t[0:npart, 0:W], AF.Square, reads=[p_.r], writes=[sq.r])
            for i, (p_, npart) in enumerate(src_list):
                kb.mm(pbc.t[:, 0:W], onesf.t[0:npart, :], sq.t[0:npart, i, 0:W], i == 0, i == len(src_list) - 1,
                      reads=[sq.r, onesf.r], writes=[pbc.r])
            kb.act(rs.t[:, 0:W], pbc.t[:, 0:W], AF.Sqrt, reads=[pbc.r, epsq.r], writes=[rs.r],
                   bias=epsq.t[:, 0:1], scale=1.0 / nfeat)
            kb.recip(rbc.t[:, 0:W], rs.t[:, 0:W], reads=[rs.r], writes=[rbc.r])
            return rbc

        def run_sequence(seq_idx, tag, S, N, x_full, x_own, n_own, halo, y_out):
            tb = tabs[tag]
            nq = n_own + (2 if halo else 0)
            nblk_a = S // 512
            R = 128 // N
            qblocks = [(b * 512, 512) for b in range(n_own // 512)]
            if halo:
                qblocks.append((n_own, 2))
            xv = x_full.rearrange("(n1 n2) d -> n2 n1 d", n2=N)
            Yv = [[Y_s[g, ri, 0:N * N * 128].rearrange("(n1 n2 c) -> n1 n2 c", n2=N, c=128) for ri in range(2)]
                  for g in range(FG)]

            if "A" in phases:
                kb.phase = 1 + 6 * seq_idx + 0
                with ExitStack() as es:
                    Wi = sb(es, "Wi_a", [128, 8, 704], BF16)
                    for dc in range(8):
                        kb.dma(SP, Wi.t[:, dc, :], Wi_s[dc, :, C_CKV:C_G], writes=[Wi.r], accum=True)
                    Wkv = sb(es, "Wkv_a", [128, H * QK + H * VD], BF16)
                    kb.dma(SP, Wkv.t[:], Wkv_s[:, :], writes=[Wkv.r])
                    xt = [sb(es, f"xa{i}", [128, D], F32) for i in range(5)]
                    nt = [mk_norm_tiles(es, "a")]
                    for k_ in range(1, 4):
                        nt.append((nt[0][0], sb(es, f"ssa{k_}", [128, 1], F32), sb(es, f"rstda{k_}", [128, 1], F32),
                                   sb(es, f"hba{k_}", [128, D], BF16)))
                    hT = [sb(es, f"hTa{i}", [128, 8, 512], BF16) for i in range(2)]
                    fr = (sb(es, "sq_a", [128, 1, 512], F32), pb[1], sb(es, "rs_a", [128, 512], F32),
                          sb(es, "rbc_a", [128, 512], F32))
                    ckvn = sb(es, "ckvn", [128, 512], BF16)
                    kcs = [sb(es, f"kcs{i}", [64, 512], F32) for i in range(2)]
                    kr2 = sb(es, "kr2", [64, 512], BF16)
                    KTb = [sb(es, f"KTb{i}", [QK, H, 512], BF16) for i in range(2)]
                    Vb = [sb(es, f"Vb{i}", [128, H, 4, VD + 1], BF16) for i in range(2)]
                    for v_ in Vb:
                        kb.memset(DVE, v_.t[:], 1.0, writes=[v_.r])
                    xfT = [sb(es, f"xfT{i}", [128, 512], BF16) for i in range(2)]
                    Yb = [sb(es, f"Yb{i}", [128, FG, 2, 4, 128], BF16) for i in range(2)]
                    tia = [0]

                    xa_of = {}

                    def prep_a(blk, part):
                        if blk >= nblk_a:
                            return
                        hTb = hT[blk % 2]
                        if part == "norm":
                            xa_of[blk] = []
                        for i in range(4):
                            if part == "norm":
                                x_ = xt[tia[0] % 5]
                                tia[0] += 1
                                xa_of[blk].append(x_)
                                tile_idx = blk * 4 + i
                                for q in range(R):
                                    kb.dma(SP, x_.t[q * N:(q + 1) * N, :], xv[tile_idx * R + q], writes=[x_.r], accum=(q > 0))
                            norm_transpose(nt[i], xa_of[blk][i], 128, hTb, i * 128, part=part)
                    prep_a(0, "norm")
                    prep_a(0, "tr")
                    for blk in range(nblk_a):
                        hTb = hT[blk % 2]
                        prep_a(blk + 1, "norm")
                        kc_ = kcs[blk % 2]
                        kb.dma(SP, kc_.t[:], tb["kcs"][:, blk * 512:(blk + 1) * 512], writes=[kc_.r])
                        for dc in range(8):
                            kb.mm(pb[0].t[:, :], Wi.t[:, dc, 0:128], hTb.t[:, dc, :], dc == 0, dc == 7,
                                  reads=[Wi.r, hTb.r], writes=[pb[0].r])
                        rbc = fm_rstd(fr, [(pb[0], 128)], 512, KVR)
                        kb.tt(DVE, ckvn.t[:], pb[0].t[:, :], rbc.t[:], ALU.mult, reads=[pb[0].r, rbc.r], writes=[ckvn.r])
                        for dc in range(8):
                            kb.mm(pb[2].t[0:64, :], Wi.t[:, dc, 128:192], hTb.t[:, dc, :], dc == 0, dc == 7,
                                  reads=[Wi.r, hTb.r], writes=[pb[2].r])
                        kb.tt(DVE, kr2.t[:], pb[2].t[0:64, :], kc_.t[:], ALU.mult, reads=[pb[2].r, kc_.r], writes=[kr2.r])
                        Y_ = Yb[blk % 2]
                        for g in range(FG):
                            p_ = pb[3 + g % 2]
                            for dc in range(8):
                                kb.mm(p_.t[:, :], Wi.t[:, dc, 192 + g * 128:192 + (g + 1) * 128], hTb.t[:, dc, :],
                                      dc == 0, dc == 7, reads=[Wi.r, hTb.r], writes=[p_.r])
                            xf_ = xfT[g % 2]
                            kb.cp(ACT, xf_.t[:], p_.t[:, :], reads=[p_.r], writes=[xf_.r])
                            for i2 in range(2):
                                p2 = pb[5 + i2]
                                for i3 in range(2):
                                    i = i2 * 2 + i3
                                    kb.mm(p2.t[:, i3 * 256:(i3 + 1) * 256], xf_.t[:, i * 128:(i + 1) * 128], cs128.t[:],
                                          True, True, reads=[xf_.r, cs128.r], writes=[p2.r], signal=(i3 == 1))
                                kb.cp(DVE, Y_.t[:, g, :, i2 * 2:i2 * 2 + 2, :],
                                      p2.t[:, :].rearrange("p (i r c) -> p r i c", i=2, r=2),
                                      reads=[p2.r], writes=[Y_.r])
                        for g in range(FG):
                            for ri in range(2):
                                for q in range(R):
                                    n2s = slice(blk * 4 * R + q, blk * 4 * R + q + 3 * R + 1, R) if R > 1 else \
                                        slice(blk * 4, blk * 4 + 4)
                                    kb.dma(POOL, Yv[g][ri][:, n2s, :], Y_.t[q * N:(q + 1) * N, g, ri, :, :],
                                           reads=[Y_.r])
                        KT_ = KTb[blk % 2]
                        for h in range(H):
                            p_ = pb[3 + h % 2]
                            kb.mm(p_.t[0:QK, :], Wkv.t[:, h * QK:(h + 1) * QK], ckvn.t[:], True, False,
                                  reads=[Wkv.r, ckvn.r], writes=[p_.r])
                            kb.mm(p_.t[0:QK, :], eplace.t[:, :], kr2.t[:], False, True,
                                  reads=[eplace.r, kr2.r], writes=[p_.r])
                            kb.cp(ACT if h % 2 else DVE, KT_.t[:, h, :], p_.t[0:QK, :], reads=[p_.r], writes=[KT_.r])
                        kb.dma(POOL, KT_s[:, :, blk * 512:(blk + 1) * 512].rearrange("h c s -> c h s"), KT_.t[:],
                               reads=[KT_.r])
                        V_ = Vb[blk % 2]
                        for i in range(4):
                            p_ = pb[5 + i % 2]
                            kb.mm(p_.t[:, :], ckvn.t[:, i * 128:(i + 1) * 128], Wkv.t[:, H * QK:H * QK + H * VD],
                                  True, True, reads=[ckvn.r, Wkv.r], writes=[p_.r])
                            kb.cp(ACT if i % 2 else DVE, V_.t[:, :, i, 0:VD],
                                  p_.t[:, :].rearrange("p (h c) -> p h c", c=VD), reads=[p_.r], writes=[V_.r])
                        kb.dma(POOL, V_s[:, :, blk * 4:(blk + 1) * 4, :].rearrange("h p c e -> p h c e"), V_.t[:],
                               reads=[V_.r])
                        prep_a(blk + 1, "tr")
                    kb.barrier()

            if "F" in phases:
                kb.phase = 1 + 6 * seq_idx + 1
                with ExitStack() as es:
                    Yt = sb(es, "Yt", [128, 2, N * 128], BF16)
                    At = sb(es, "At", [128, 2, 128 * N], BF16)
                    ca = sb(es, "ca", [128, 2 * N], BF16)
                    cbm = sb(es, "cbm", [128, 2 * N], BF16)
                    ecs = sb(es, "ecs", [128, 2, nq], BF16)
                    FTb = [sb(es, f"FTb{i}", [128, nq], BF16) for i in range(2)]
                    kb.dma(SP, ca.t[0:N, :], tb["ca"][:, :], writes=[ca.r])
                    kb.dma(SP, cbm.t[0:N, :], tb["cb"][:, :], writes=[cbm.r])
                    kb.dma(SP, ecs.t[0:N, :, :], tb["ecs"][:, :, :], writes=[ecs.r])
                    K2L = n_own // N
                    cpb = 512 // (2 * N)
                    kpb = 512 // K2L
                    fscale = 1.0 / float(np.sqrt(S * 128.0))
                    A4 = At.t[:, :, :].rearrange("p r (c k) -> p r c k", k=N)
                    Y4 = Yt.t[:, :, :].rearrange("p r (n c) -> p r n c", c=128)
                    for g in range(FG):
                        for ri in range(2):
                            kb.dma(SP, Yt.t[0:N, ri, :], Y_s[g, ri, 0:N * N * 128].rearrange("(n r) -> n r", n=N),
                                   writes=[Yt.r], accum=(ri > 0))
                        for cg in range(128 // cpb):
                            p_ = pb[cg % 3]
                            for ci in range(cpb):
                                c_ = cg * cpb + ci
                                o_ = p_.t[0:N, ci * 2 * N:(ci + 1) * 2 * N]
                                kb.mm(o_, Y4[0:N, 0, :, c_], ca.t[0:N, :], True, False, reads=[Yt.r, ca.r], writes=[p_.r])
                                kb.mm(o_, Y4[0:N, 1, :, c_], cbm.t[0:N, :], False, True, reads=[Yt.r, cbm.r], writes=[p_.r],
                                      signal=(ci == cpb - 1))
                            kb.cp(ACT if cg % 2 else DVE, A4[0:N, :, cg * cpb:(cg + 1) * cpb, :],
                                  p_.t[0:N, :].rearrange("p (c r k) -> p r c k", c=cpb, r=2),
                                  reads=[p_.r], writes=[At.r])
                        F_ = FTb[g % 2]
                        Fv = F_.t[:, 0:n_own].rearrange("p (k2 k1) -> p k1 k2", k1=N)
                        for kg in range(N // kpb):
                            p_ = pb[3 + kg % 3]
                            for ki in range(kpb):
                                k1 = kg * kpb + ki
                                o_ = p_.t[:, ki * K2L:(ki + 1) * K2L]
                                kb.mm(o_, A4[0:N, 0, :, k1], ecs.t[0:N, 0, k1:n_own:N], True, False,
                                      reads=[At.r, ecs.r], writes=[p_.r])
                                kb.mm(o_, A4[0:N, 1, :, k1], ecs.t[0:N, 1, k1:n_own:N], False, True,
                                      reads=[At.r, ecs.r], writes=[p_.r], signal=(ki == kpb - 1))
                            kb.act(Fv[:, kg * kpb:(kg + 1) * kpb, :], p_.t[:, :].rearrange("p (k c) -> p k c", c=K2L),
                                   AF.Copy, reads=[p_.r], writes=[F_.r], scale=fscale)
                        if halo:
                            p_ = pb[6]
                            for hi, k1 in enumerate((N - 1, 0)):
                                o_ = p_.t[:, hi:hi + 1]
                                kb.mm(o_, A4[0:N, 0, :, k1], ecs.t[0:N, 0, n_own + hi:n_own + hi + 1], True, False,
                                      reads=[At.r, ecs.r], writes=[p_.r])
                                kb.mm(o_, A4[0:N, 1, :, k1], ecs.t[0:N, 1, n_own + hi:n_own + hi + 1], False, True,
                                      reads=[At.r, ecs.r], writes=[p_.r], signal=(hi == 1))
                            kb.act(F_.t[:, n_own:n_own + 2], p_.t[:, 0:2], AF.Copy, reads=[p_.r], writes=[F_.r],
                                   scale=fscale)
                        kb.dma(POOL, FT_s[g, :, 0:nq], F_.t[:, 0:nq], reads=[F_.r])
                    kb.barrier()

            if "Q" in phases:
                kb.phase = 1 + 6 * seq_idx + 2
                with ExitStack() as es:
                    Wi = sb(es, "Wi_q", [128, 8, 256], BF16)
                    for dc in range(8):
                        kb.dma(SP, Wi.t[:, dc, :], Wi_s[dc, :, 0:256], writes=[Wi.r], accum=True)
                    Wuq = sb(es, "Wuq", [128, 2, 2 * H * QK], BF16)
                    for kc in range(2):
                        kb.dma(SP, Wuq.t[:, kc, :], Wuq_s[kc], writes=[Wuq.r], accum=True)
                    xt = [sb(es, f"xq{i}", [128, D], F32) for i in range(5)]
                    nt = [mk_norm_tiles(es, "q")]
                    for k_ in range(1, 4):
                        nt.append((nt[0][0], sb(es, f"ssq{k_}", [128, 1], F32), sb(es, f"rstdq{k_}", [128, 1], F32),
                                   sb(es, f"hbq{k_}", [128, D], BF16)))
                    hT = [sb(es, f"hTq{i}", [128, 8, 512], BF16) for i in range(2)]
                    fr = (sb(es, "sq_q", [128, 2, 512], F32), pb[2], sb(es, "rs_q", [128, 512], F32),
                          sb(es, "rbc_q", [128, 512], F32))
                    cqn = sb(es, "cqn", [128, 2, 512], BF16)
                    qc = [sb(es, f"qc{i}", [QK, 512], F32) for i in range(2)]
                    qs = [sb(es, f"qs{i}", [QK, 512], F32) for i in range(2)]
                    t1 = [sb(es, f"t1{i}", [QK, 512], F32) for i in range(3)]
                    t2 = [sb(es, f"t2{i}", [QK, 512], F32) for i in range(3)]
                    Qb = [sb(es, f"Qb{i}", [QK, H, 512], BF16) for i in range(2)]
                    tiq = [0]

                    xq_of = {}

                    def prep_q(bi, part):
                        if bi >= len(qblocks):
                            return
                        c0, W = qblocks[bi]
                        hTb = hT[bi % 2]
                        ntile = (W + 127) // 128
                        if part == "norm":
                            xq_of[bi] = []
                        for i in range(ntile):
                            tw = min(128, W - i * 128)
                            if part == "norm":
                                x_ = xt[tiq[0] % 5]
                                tiq[0] += 1
                                xq_of[bi].append(x_)
                                kb.dma(SP, x_.t[0:tw, :], x_own[c0 + i * 128:c0 + i * 128 + tw, :], writes=[x_.r])
                            norm_transpose(nt[i], xq_of[bi][i], tw, hTb, i * 128, part=part)
                    prep_q(0, "norm")
                    prep_q(0, "tr")
                    for bi, (c0, W) in enumerate(qblocks):
                        hTb = hT[bi % 2]
                        prep_q(bi + 1, "norm")
                        qc_, qs_ = qc[bi % 2], qs[bi % 2]
                        kb.dma(SP, qc_.t[:, 0:W], tb["qcos"][:, c0:c0 + W], writes=[qc_.r])
                        kb.dma(SP, qs_.t[:, 0:W], tb["qsin"][:, c0:c0 + W], writes=[qs_.r])
                        for kc in range(2):
                            for dc in range(8):
                                kb.mm(pb[kc].t[:, 0:W], Wi.t[:, dc, kc * 128:(kc + 1) * 128], hTb.t[:, dc, 0:W],
                                      dc == 0, dc == 7, reads=[Wi.r, hTb.r], writes=[pb[kc].r])
                        rbc = fm_rstd(fr, [(pb[0], 128), (pb[1], 128)], W, QR)
                        for kc in range(2):
                            kb.tt(DVE, cqn.t[:, kc, 0:W], pb[kc].t[:, 0:W], rbc.t[:, 0:W], ALU.mult,
                                  reads=[pb[kc].r, rbc.r], writes=[cqn.r])
                        Q_ = Qb[bi % 2]
                        for h in range(H):
                            pq, pr = ((pb[3], pb[4]), (pb[5], pb[6]), (pb[0], pb[1]))[h % 3]
                            for kc in range(2):
                                kb.mm(pq.t[0:QK, 0:W], Wuq.t[:, kc, h * QK:(h + 1) * QK], cqn.t[:, kc, 0:W],
                                      kc == 0, kc == 1, reads=[Wuq.r, cqn.r], writes=[pq.r])
                            for kc in range(2):
                                kb.mm(pr.t[0:QK, 0:W], Wuq.t[:, kc, H * QK + h * QK:H * QK + (h + 1) * QK],
                                      cqn.t[:, kc, 0:W], kc == 0, kc == 1, reads=[Wuq.r, cqn.r], writes=[pr.r])
                            a_, b_ = t1[h % 3], t2[h % 3]
                            kb.tt(DVE, a_.t[:, 0:W], pq.t[0:QK, 0:W], qc_.t[:, 0:W], ALU.mult,
                                  reads=[pq.r, qc_.r], writes=[a_.r])
                            kb.tt(DVE, b_.t[:, 0:W], pr.t[0:QK, 0:W], qs_.t[:, 0:W], ALU.mult,
                                  reads=[pr.r, qs_.r], writes=[b_.r])
                            kb.tt(POOL, Q_.t[:, h, 0:W], a_.t[:, 0:W], b_.t[:, 0:W], ALU.add,
                                  reads=[a_.r, b_.r], writes=[Q_.r])
                        kb.dma(POOL, QT_s[:, :, c0:c0 + W].rearrange("h c s -> c h s"), Q_.t[:, :, 0:W], reads=[Q_.r])
                        prep_q(bi + 1, "tr")
                    kb.barrier()

            if "T" in phases:
                kb.phase = 1 + 6 * seq_idx + 3
                with ExitStack() as es:
                    nch = S // 128
                    KT = [sb(es, f"KT{i}", [QK, S], BF16) for i in range(2)]
                    Vt = [sb(es, f"Vt{i}", [128, nch, VD + 1], BF16) for i in range(2)]
                    Qt = [sb(es, f"Qt{i}", [QK, 512], BF16) for i in range(2)]
                    Osb = [sb(es, f"Osb{i}", [VD + 1, 512], F32) for i in range(2)]
                    rinv = [sb(es, f"rinv{i}", [VD + 1, 512], F32) for i in range(2)]
                    Ab = [sb(es, f"Ab{i}", [VD, 512], BF16) for i in range(2)]
                    Pt2 = [sb(es, f"Pp{i}", [128, 2, 512], BF16) for i in range(3)]
                    SB = []
                    for i in range(3):
                        t_ = T(PSALL[:, 2 * i:2 * i + 2, :])
                        SB.append(t_)
                    psO = [pb[6], pb[7]]
                    npair = nch // 2
                    blocks = [(h, qi) for h in range(H) for qi in range(len(qblocks))]
                    ptasks = [(bi, j) for bi in range(len(blocks)) for j in range(npair)]
                    bst = {}
                    cur_head = [-1]
                    loaded = set()

                    def load_head(hh):
                        if hh < H and hh not in loaded:
                            loaded.add(hh)
                            kb.dma(SP, KT[hh % 2].t[:, :], KT_s[hh, :, 0:S], writes=[KT[hh % 2].r])
                            kb.dma(SP, Vt[hh % 2].t[:, :, :], V_s[hh, :, 0:nch, :], writes=[Vt[hh % 2].r])

                    qloaded = set()

                    def load_q(bi):
                        if bi < len(blocks) and bi not in qloaded:
                            qloaded.add(bi)
                            h_, qi_ = blocks[bi]
                            c0_, W_ = qblocks[qi_]
                            kb.dma(SP, Qt[bi % 2].t[:, 0:W_], QT_s[h_, :, c0_:c0_ + W_], writes=[Qt[bi % 2].r])

                    def ensure_block(bi):
                        if bi in bst:
                            return bst[bi]
                        h, qi = blocks[bi]
                        c0, W = qblocks[qi]
                        KT_, V_ = KT[h % 2], Vt[h % 2]
                        Q_ = Qt[bi % 2]
                        load_q(bi)
                        load_head(h)
                        bst[bi] = dict(h=h, c0=c0, W=W, KT=KT_, V=V_, Q=Q_, pO=psO[bi % 2])
                        return bst[bi]

                    def score_pair(gi):
                        bi, j = ptasks[gi]
                        st = ensure_block(bi)
                        p_ = SB[gi % 3]
                        W = st["W"]
                        for c in range(2):
                            kc = 2 * j + c
                            kb.mm(p_.t[:, c, 0:W], st["KT"].t[:, kc * 128:(kc + 1) * 128], st["Q"].t[:, 0:W], True, True,
                                  reads=[st["KT"].r, st["Q"].r], writes=[p_.r], signal=(c == 1))

                    ntask = len(ptasks)
                    for gi in range(min(2, ntask)):
                        score_pair(gi)
                    pend = []
                    for gi in range(ntask):
                        if gi + 2 < ntask:
                            score_pair(gi + 2)
                        while pend and pend[0][0] <= gi:
                            pend.pop(0)[1]()
                        bi, j = ptasks[gi]
                        st = bst[bi]
                        if j == 0:
                            load_q(bi + 1)
                        if blocks[bi][1] == 0 and j == min(2, npair - 1):
                            load_head(blocks[bi][0] + 1)
                        W, pO, V_ = st["W"], st["pO"], st["V"]
                        p_ = SB[gi % 3]
                        P_ = Pt2[gi % 3]
                        kb.act(P_.t[:, :, 0:W], p_.t[:, :, 0:W], AF.Exp, reads=[p_.r], writes=[P_.r])
                        for c in range(2):
                            kc = 2 * j + c
                            kb.mm(pO.t[0:VD + 1, 0:W], V_.t[:, kc, :], P_.t[:, c, 0:W], kc == 0, kc == nch - 1,
                                  reads=[V_.r, P_.r], writes=[pO.r], signal=(c == 1))
                        if j == npair - 1:
                            O_ = Osb[bi % 2]
                            r_ = rinv[bi % 2]
                            kb.cp(DVE, O_.t[:, 0:W], pO.t[0:VD + 1, 0:W], reads=[pO.r], writes=[O_.r])
                            kb.recip(r_.t[VD:VD + 1, 0:W], O_.t[VD:VD + 1, 0:W], reads=[O_.r], writes=[r_.r])

                            def part2(bi=bi, st=st, O_=O_, r_=r_, W=W, pO=pO):
                                h, c0 = st["h"], st["c0"]
                                kb.mm(pO.t[0:VD, 0:W], onesf.t[VD:VD + 1, 0:VD], r_.t[VD:VD + 1, 0:W], True, True,
                                      reads=[onesf.r, r_.r], writes=[pO.r])
                                A_ = Ab[bi % 2]
                                kb.tt(DVE, A_.t[:, 0:W], pO.t[0:VD, 0:W], O_.t[0:VD, 0:W], ALU.mult,
                                      reads=[pO.r, O_.r], writes=[A_.r])
                                kb.dma(POOL, AT_s[h * VD:(h + 1) * VD, c0:c0 + W], A_.t[:, 0:W], reads=[A_.r])
                            pend.append((gi + 5, part2))
                            del bst[bi]
                    for _, f_ in pend:
                        f_()
                    kb.barrier()

            if "M" in phases:
                kb.phase = 1 + 6 * seq_idx + 4
                with ExitStack() as es:
                    Wg = sb(es, "Wg", [128, 8, 2048], BF16)
                    for dc in range(8):
                        kb.dma(SP, Wg.t[:, dc, :], Wi_s[dc, :, C_G:WI_COLS], writes=[Wg.r], accum=True)
                    Wao = sb(es, "Wao", [128, 4, D], BF16)
                    Wfo = sb(es, "Wfo", [128, 4, D], BF16)
                    Wo = sb(es, "Wo", [128, 8, D], BF16)
                    for c in range(4):
                        kb.dma(SP, Wao.t[:, c, :], Wao_s[c], writes=[Wao.r], accum=True)
                        kb.dma(SP, Wfo.t[:, c, :], Wfo_s[c], writes=[Wfo.r], accum=True)
                    for c in range(8):
                        kb.dma(SP, Wo.t[:, c, :], Wo_s[c], writes=[Wo.r], accum=True)
                    xt = [sb(es, f"xm{i}", [128, D], F32) for i in range(8)]
                    nt = [mk_norm_tiles(es, "m")]
                    for k_ in range(1, 4):
                        nt.append((nt[0][0], sb(es, f"ssm{k_}", [128, 1], F32), sb(es, f"rstdm{k_}", [128, 1], F32),
                                   sb(es, f"hbm{k_}", [128, D], BF16)))
                    nt2 = [mk_norm_tiles(es, "m2", junk=nt[0][0])]
                    for k_ in range(1, 4):
                        nt2.append((nt2[0][0], sb(es, f"ss2{k_}", [128, 1], F32), sb(es, f"rstd2{k_}", [128, 1], F32),
                                    sb(es, f"hb2{k_}", [128, D], BF16)))
                    hT = [sb(es, f"hTm{i}", [128, 8, 512], BF16) for i in range(2)]
                    Gt = sb(es, "Gt", [128, 16, 512], BF16)
                    ATt = [sb(es, f"ATt{i}", [128, 4, 512], BF16) for i in range(1)]
                    FTt = [sb(es, f"FTt{i}", [128, 4, 512], BF16) for i in range(1)]
                    ma = [sb(es, f"ma{i}", [128, 512], F32) for i in range(2)]
                    mf = [sb(es, f"mf{i}", [128, 512], F32) for i in range(2)]
                    mg = sb(es, "mg", [128, 8, 512], BF16)
                    x1 = [sb(es, f"x1{i}", [128, D], F32) for i in range(4)]
                    h2T = [sb(es, f"h2T{i}", [128, 8, 512], BF16) for i in range(2)]
                    if not halo:
                        for dc in range(8):
                            kb.dma(POOL, H2T_s[dc, :, 0:1], zeros.t[:, 0:1], reads=[zeros.r], slow=True)
                            kb.dma(POOL, H2T_s[dc, :, n_own + 1:n_own + 2], zeros.t[:, 0:1], reads=[zeros.r], slow=True)
                    tim = [0]
                    xts_of = {}

                    def prep_m(bi, part):
                        if bi >= len(qblocks):
                            return
                        c0, W = qblocks[bi]
                        hTb = hT[bi % 2]
                        ntile = (W + 127) // 128
                        if part == "norm":
                            xts_of[bi] = []
                        for i in range(ntile):
                            tw = min(128, W - i * 128)
                            if part == "norm":
                                x_ = xt[tim[0] % 8]
                                tim[0] += 1
                                xts_of[bi].append((x_, tw))
                                kb.dma(SP, x_.t[0:tw, :], x_own[c0 + i * 128:c0 + i * 128 + tw, :], writes=[x_.r])
                            norm_transpose(nt[i], xts_of[bi][i][0], tw, hTb, i * 128, part=part)
                    prep_m(0, "norm")
                    prep_m(0, "tr")
                    for bi, (c0, W) in enumerate(qblocks):
                        is_halo = halo and bi == len(qblocks) - 1
                        hTb = hT[bi % 2]
                        prep_m(bi + 1, "norm")
                        xts = xts_of[bi]
                        AT_, FT_ = ATt[0], FTt[0]
                        for c in range(4):
                            kb.dma(SP, AT_.t[:, c, 0:W], AT_s[c * 128:(c + 1) * 128, c0:c0 + W], writes=[AT_.r], accum=(c > 0))
                            kb.dma(SP, FT_.t[:, c, 0:W], FT_s[c, :, c0:c0 + W], writes=[FT_.r], accum=(c > 0))
                        for gc in range(16):
                            p_ = pb[gc % 3]
                            for dc in range(8):
                                kb.mm(p_.t[:, 0:W], Wg.t[:, dc, gc * 128:(gc + 1) * 128], hTb.t[:, dc, 0:W],
                                      dc == 0, dc == 7, reads=[Wg.r, hTb.r], writes=[p_.r])
                            kb.act(Gt.t[:, gc, 0:W], p_.t[:, 0:W], AF.Sigmoid, reads=[p_.r], writes=[Gt.r])
                        for oc in range(8):
                            pa, pf = pb[3 + 2 * (oc % 2)], pb[4 + 2 * (oc % 2)]
                            for c in range(4):
                                kb.mm(pa.t[:, 0:W], Wao.t[:, c, oc * 128:(oc + 1) * 128], AT_.t[:, c, 0:W],
                                      c == 0, c == 3, reads=[Wao.r, AT_.r], writes=[pa.r])
                            for c in range(4):
                                kb.mm(pf.t[:, 0:W], Wfo.t[:, c, oc * 128:(oc + 1) * 128], FT_.t[:, c, 0:W],
                                      c == 0, c == 3, reads=[Wfo.r, FT_.r], writes=[pf.r])
                            a_, f_ = ma[oc % 2], mf[oc % 2]
                            kb.tt(DVE, a_.t[:, 0:W], pa.t[:, 0:W], Gt.t[:, oc, 0:W], ALU.mult,
                                  reads=[pa.r, Gt.r], writes=[a_.r])
                            kb.tt(DVE, f_.t[:, 0:W], pf.t[:, 0:W], Gt.t[:, 8 + oc, 0:W], ALU.mult,
                                  reads=[pf.r, Gt.r], writes=[f_.r])
                            kb.tt(POOL, mg.t[:, oc, 0:W], a_.t[:, 0:W], f_.t[:, 0:W], ALU.add,
                                  reads=[a_.r, f_.r], writes=[mg.r])
                        h2b = h2T[bi % 2]
                        for i, (x_, tw) in enumerate(xts):
                            x1_ = x1[i % 4]
                            for hf in range(2):
                                p_ = pb[hf + 2 * (i % 2)]
                                for kc in range(8):
                                    kb.mm(p_.t[0:tw, :], mg.t[:, kc, i * 128:i * 128 + tw], Wo.t[:, kc, hf * 512:(hf + 1) * 512],
                                          kc == 0, kc == 7, reads=[mg.r, Wo.r], writes=[p_.r])
                                kb.tt(DVE, x1_.t[0:tw, hf * 512:(hf + 1) * 512], p_.t[0:tw, :],
                                      x_.t[0:tw, hf * 512:(hf + 1) * 512], ALU.add, reads=[p_.r, x_.r], writes=[x1_.r])
                            if not is_halo:
                                kb.dma(POOL, X1_s[c0 + i * 128:c0 + i * 128 + tw, :], x1_.t[0:tw, :], reads=[x1_.r])
                            norm_transpose(nt2[i % 4], x1_, tw, h2b, i * 128, part="norm")
                        prep_m(bi + 1, "tr")
                        for i, (x_, tw) in enumerate(xts):
                            norm_transpose(nt2[i % 4], x1[i % 4], tw, h2b, i * 128, part="tr")
                        if is_halo:
                            kb.tt(DVE, h2b.t[:, :, 0:2], h2b.t[:, :, 0:2],
                                  mask.t[:, :].unsqueeze(1).to_broadcast([128, 8, 2]), ALU.mult,
                                  reads=[h2b.r, mask.r], writes=[h2b.r])
                            kb.dma(POOL, H2T_s[:, :, 0:1].rearrange("c p s -> p c s"), h2b.t[:, :, 0:1], reads=[h2b.r], slow=True)
                            kb.dma(POOL, H2T_s[:, :, n_own + 1:n_own + 2].rearrange("c p s -> p c s"), h2b.t[:, :, 1:2],
                                   reads=[h2b.r], slow=True)
                        else:
                            kb.dma(POOL, H2T_s[:, :, 1 + c0:1 + c0 + W].rearrange("c p s -> p c s"), h2b.t[:, :, 0:W],
                                   reads=[h2b.r])
                    kb.barrier()

            if "N" in phases:
                kb.phase = 1 + 6 * seq_idx + 5
                with ExitStack() as es:
                    Wup = sb(es, "Wup", [128, 8, 2 * DFF], BF16)
                    Wdn = sb(es, "Wdn", [128, NHC, D], BF16)
                    h2 = [sb(es, f"h2n{i}", [128, 8, 512], BF16) for i in range(2)]
                    nb = (n_own + FB - 1) // FB
                    h2_loaded = set()

                    def load_h2(b):
                        if b < nb and b not in h2_loaded:
                            h2_loaded.add(b)
                            t0_ = b * FB
                            Wb_ = min(FB, n_own - t0_)
                            kb.dma(SP, h2[b % 2].t[:, :, 0:Wb_ + 2],
                                   H2T_s[:, :, t0_:t0_ + Wb_ + 2].rearrange("c p s -> p c s"), writes=[h2[b % 2].r])
                    load_h2(0)
                    NWB = 11
                    wup_r = [Res() for _ in range(NWB)]
                    order = []
                    for cp_ in range(NHC):
                        for blk_ in (cp_ // 4, (cp_ + NHC) // 4):
                            if blk_ not in order:
                                order.append(blk_)
                    for blk_ in order:
                        for c in range(8):
                            kb.dma(SP, Wup.t[:, c, blk_ * 512:(blk_ + 1) * 512], Wup_s[c, :, blk_ * 512:(blk_ + 1) * 512],
                                   writes=[wup_r[blk_]], accum=True)
                    for c in range(NHC):
                        kb.dma(SP, Wdn.t[:, c, :], Wdn_s[c], writes=[Wdn.r], accum=True)
                    cw = sb(es, "cw", [128, NFC, 3], F32)
                    cbias = sb(es, "cbias", [128, NFC], F32)
                    gfin = sb(es, "gfin", [128, D], F32)
                    kb.dma(SP, cw.t[:], convw_d[:, :, :], writes=[cw.r])
                    kb.dma(SP, cbias.t[:], convb_d[:, :], writes=[cbias.r])
                    kb.dma(SP, gfin.t[:], gfin_d[:, :], writes=[gfin.r])
                    acc = [sb(es, f"acc{i}", [128, 512], F32) for i in range(4)]
                    sg = [sb(es, f"sg{i}", [128, 512], F32) for i in range(2)]
                    actT = sb(es, "actT", [128, NHC, 512], BF16)
                    x1t = [sb(es, f"x1n{i}", [128, D], F32) for i in range(2)]
                    yo = [sb(es, f"yo{i}", [128, D], F32) for i in range(2)]
                    ss = [sb(es, f"ssn{i}", [128, 1], F32) for i in range(2)]
                    rstd = [sb(es, f"rstdn{i}", [128, 1], F32) for i in range(2)]
                    ti = 0
                    ai = 0
                    for b in range(nb):
                        t0 = b * FB
                        Wb = min(FB, n_own - t0)
                        h2_ = h2[b % 2]
                        load_h2(b)
                        load_h2(b + 1)
                        for cp_ in range(NHC):
                            accs = []
                            for half, ch in enumerate((cp_, cp_ + NHC)):
                                p_ = pb[(2 * cp_ + half) % 4]
                                for dc in range(8):
                                    kb.mm(p_.t[:, 0:Wb + 2], Wup.t[:, dc, ch * 128:(ch + 1) * 128], h2_.t[:, dc, 0:Wb + 2],
                                          dc == 0, dc == 7, reads=[wup_r[ch // 4], h2_.r], writes=[p_.r])
                                a_ = acc[ai % 4]
                                ai += 1
                                kb.act(a_.t[:, 0:Wb], p_.t[:, 1:Wb + 1], AF.Identity, reads=[p_.r, cw.r, cbias.r],
                                       writes=[a_.r], scale=cw.t[:, ch, 1:2], bias=cbias.t[:, ch:ch + 1])
                                kb.stt(DVE, a_.t[:, 0:Wb], p_.t[:, 0:Wb], cw.t[:, ch, 0:1], a_.t[:, 0:Wb], ALU.mult, ALU.add,
                                       reads=[p_.r, a_.r, cw.r], writes=[a_.r])
                                kb.stt(DVE, a_.t[:, 0:Wb], p_.t[:, 2:Wb + 2], cw.t[:, ch, 2:3], a_.t[:, 0:Wb], ALU.mult, ALU.add,
                                       reads=[p_.r, a_.r, cw.r], writes=[a_.r])
                                accs.append(a_)
                            s_ = sg[cp_ % 2]
                            kb.act(s_.t[:, 0:Wb], accs[0].t[:, 0:Wb], AF.Silu, reads=[accs[0].r], writes=[s_.r])
                            kb.tt(POOL, actT.t[:, cp_, 0:Wb], s_.t[:, 0:Wb], accs[1].t[:, 0:Wb], ALU.mult,
                                  reads=[s_.r, accs[1].r], writes=[actT.r])
                        ntile = (Wb + 127) // 128
                        for i in range(ntile):
                            tw = min(128, Wb - i * 128)
                            r0 = t0 + i * 128
                            x1_ = x1t[ti % 2]
                            x2_ = x1_
                            y_ = yo[ti % 2]
                            ss_ = ss[ti % 2]
                            rs_ = rstd[ti % 2]
                            kb.dma(SP, x1_.t[0:tw, :], X1_s[r0:r0 + tw, :], writes=[x1_.r])
                            for hf in range(2):
                                p_ = pb[4 + hf]
                                for c in range(NHC):
                                    kb.mm(p_.t[0:tw, :], actT.t[:, c, i * 128:i * 128 + tw], Wdn.t[:, c, hf * 512:(hf + 1) * 512],
                                          c == 0, c == NHC - 1, reads=[actT.r, Wdn.r], writes=[p_.r])
                                kb.tt(DVE, x2_.t[0:tw, hf * 512:(hf + 1) * 512], p_.t[0:tw, :],
                                      x1_.t[0:tw, hf * 512:(hf + 1) * 512], ALU.add, reads=[p_.r, x1_.r], writes=[x2_.r])
                            kb.act(y_.t[0:tw, :], x2_.t[0:tw, :], AF.Square, reads=[x2_.r], writes=[y_.r, ss_.r],
                                   accum_out=ss_.t[0:tw, :])
                            kb.act(ss_.t[0:tw, :], ss_.t[0:tw, :], AF.Sqrt, reads=[ss_.r, epsq.r], writes=[ss_.r],
                                   bias=epsq.t[0:tw, 0:1], scale=1.0 / D)
                            kb.recip(rs_.t[0:tw, :], ss_.t[0:tw, :], reads=[ss_.r], writes=[rs_.r])
                            kb.stt(DVE, y_.t[0:tw, :], x2_.t[0:tw, :], rs_.t[0:tw, 0:1], gfin.t[0:tw, :], ALU.mult, ALU.mult,
                                   reads=[x2_.r, rs_.r, gfin.r], writes=[y_.r])
                            kb.dma(POOL, y_out[r0:r0 + tw, :], y_.t[0:tw, :], reads=[y_.r])
                            ti += 1
                    kb.barrier()

        if cfg.get("do_prompt", True):
            run_sequence(0, "p", SP_, NP_, xp_full, xp_own, OWNP, True, yp)
        for si in range(cfg.get("n_run_seq", NSEQ)):
            run_sequence(1 + si, "s", SS_, NS_, xs[si], xs[si], SS_, False, ys[si])
        kb.barrier()
    return nc


def _bf(a):
    return np.ascontiguousarray(a.astype(ml_dtypes.bfloat16))


def host_tables(S, N, own_pos, scale):
    inv = (10000.0 ** (-np.arange(0, ROPE, 2, dtype=np.float32) / ROPE)).astype(np.float32)
    tq = np.arange(S)
    posA = (tq % N) * N + tq // N
    angK = posA[None, :].astype(np.float32) * np.concatenate([inv, inv])[:, None]
    kcs = np.concatenate([np.cos(angK), np.sin(angK)], 0).astype(np.float32)
    angQ = np.asarray(own_pos, np.float32)[None, :] * np.concatenate([inv, inv])[:, None]
    nq = len(own_pos)
    qcos = np.concatenate([np.full((NOPE, nq), scale, np.float32), scale * np.cos(angQ)], 0).astype(np.float32)
    qsin = np.concatenate([np.zeros((NOPE, nq), np.float32), scale * np.sin(angQ)], 0).astype(np.float32)
    n2 = np.arange(N, dtype=np.float64)[:, None]
    ph = 2 * np.pi * n2 * np.asarray(own_pos, np.float64)[None, :] / S
    ecs = np.stack([np.cos(ph), np.sin(ph)], 1)
    th = 2 * np.pi * np.outer(np.arange(N), np.arange(N)) / N
    ca = np.concatenate([np.cos(th), -np.sin(th)], 1)
    cb = np.concatenate([np.sin(th), np.cos(th)], 1)
    return dict(kcs=kcs, qcos=qcos, qsin=qsin, ecs=_bf(ecs), ca=_bf(ca), cb=_bf(cb))


def host_consts():
    th = 2 * np.pi * np.outer(np.arange(128), np.arange(128)) / 128
    cs128 = np.concatenate([np.cos(th), -np.sin(th)], 1)
    eplace = np.zeros((64, QK), np.float32)
    for r in range(32):
        eplace[r, 64 + r] = 1.0
        eplace[32 + r, 64 + r] = 1.0
    return dict(cs128=_bf(cs128), ident=_bf(np.eye(128, dtype=np.float32)), eplace=_bf(eplace))


def weight_maps(g_mix, w_in, g_q, w_uq, g_kv, w_ukv, w_attn_out, w_fourier_out, w_out, g_ffn, w_up,
                conv_w, conv_b, w_down, g_final):
    c = np.ascontiguousarray
    f = np.float32
    return dict(
        w_in=c(w_in[0], f), w_uq=c(w_uq[0], f), w_ukv=c(w_ukv[0], f), w_ao=c(w_attn_out[0], f),
        w_fo=c(w_fourier_out[0], f), w_out=c(w_out[0], f), w_up=c(w_up[0], f), w_dn=c(w_down[0], f),
        g_mix8=c(g_mix[0].reshape(8, 128).T, f), g_q2=c(g_q[0].reshape(2, 128).T, f),
        g_kv1=c(g_kv[0].reshape(1, 128).T, f), g_ffn8=c(g_ffn[0].reshape(8, 128).T, f),
        convw=c(conv_w[0].reshape(3, NFC, 128).transpose(2, 1, 0), f),
        convb=c(conv_b[0].reshape(NFC, 128).T, f),
        gfin=c(np.broadcast_to(g_final.reshape(1, D), (128, D)), f))


_NC_CACHE = {}


def run(cfg, x_prompt, x_sample, wm, n_cores=8):
    SP_, SS_, NSEQ = cfg["SP"], cfg["SS"], cfg["NSEQ"]
    OWNP = SP_ // 4
    key = tuple(sorted((k, str(v)) for k, v in cfg.items()))
    if key not in _NC_CACHE:
        _NC_CACHE[key] = build(cfg)
    nc = _NC_CACHE[key]
    scale = float(QK) ** -0.5
    consts = host_consts()
    ts = host_tables(SS_, cfg["NS"], np.arange(SS_), scale)
    in_maps = []
    for c in range(n_cores):
        b, j = c // 4, c % 4
        own_pos = list(range(OWNP * j, OWNP * (j + 1))) + [OWNP * j - 1, OWNP * (j + 1)]
        tp = host_tables(SP_, cfg["NP"], own_pos, scale)
        xo = np.zeros((OWNP + 2, D), np.float32)
        xo[:OWNP] = x_prompt[b, OWNP * j:OWNP * (j + 1)]
        mask = np.zeros((128, 2), np.float32)
        if j > 0:
            xo[OWNP] = x_prompt[b, OWNP * j - 1]
            mask[:, 0] = 1.0
        if j < 3:
            xo[OWNP + 1] = x_prompt[b, OWNP * (j + 1)]
            mask[:, 1] = 1.0
        m = dict(xp_full=np.ascontiguousarray(x_prompt[b]), xp_own=xo,
                 xs=np.ascontiguousarray(x_sample[NSEQ * c:NSEQ * (c + 1)]), mask=mask)
        for k, v in tp.items():
            m[f"{k}_p"] = v
        for k, v in ts.items():
            m[f"{k}_s"] = v
        m.update(consts)
        m.update(wm)
        in_maps.append(m)
    res = run_bass_kernel_spmd(nc, in_maps, core_ids=list(range(n_cores)))
    return res.results


FULL_CFG = dict(SP=16384, NP=128, SS=4096, NS=64, NSEQ=2)


def kernel(x_prompt, x_sample, g_mix, w_in, g_q, w_uq, g_kv, w_ukv, w_attn_out, w_fourier_out, w_out,
           g_ffn, w_up, conv_w, conv_b, w_down, g_final):
    cfg = FULL_CFG
    x_prompt = np.asarray(x_prompt, np.float32)
    x_sample = np.asarray(x_sample, np.float32)
    wm = weight_maps(*[np.asarray(a, np.float32) for a in (g_mix, w_in, g_q, w_uq, g_kv, w_ukv, w_attn_out,
                                                            w_fourier_out, w_out, g_ffn, w_up, conv_w, conv_b,
                                                            w_down, g_final)])
    results = run(cfg, x_prompt, x_sample, wm)
    OWNP = cfg["SP"] // 4
    yp = np.zeros_like(x_prompt)
    ysm = np.zeros_like(x_sample)
    for c in range(8):
        b, j = c // 4, c % 4
        yp[b, OWNP * j:OWNP * (j + 1)] = results[c]["yp"]
        ysm[cfg["NSEQ"] * c:cfg["NSEQ"] * (c + 1)] = results[c]["ys"]
    return (yp, ysm)
```
